# Optimizing a Trainium2 kernel written in Bass

```python
import math
import jax
import jax.numpy as jnp
from jax import lax
import numpy as np

D_MODEL = 2048
BATCH = 4
SEQ = 2048
DEPTH = 2
DEC_BATCH = 128
DEC_SEQ = 8
PAST_LEN = 2048
PAGE_SIZE = 128

N_BRANCH = 4
W_BRANCH = D_MODEL // 4
W_POOL = W_BRANCH
POOL_WINDOWS = (2, 4, 8, 16)
POOL_GROUP = W_POOL // len(POOL_WINDOWS)
POOL_BUF = max(POOL_WINDOWS) - 1
W_CONV = W_BRANCH
CONV_WIDTH = 3
N_HEADS_C = 8
HEAD_DIM_C = W_BRANCH // N_HEADS_C
W_ATTN = N_HEADS_C * HEAD_DIM_C
ATTN_SCALE = HEAD_DIM_C ** -0.5
IDX_HEADS = 8
IDX_DIM = 64
IDX_SCALE = IDX_DIM ** -0.5
IDX_W_SCALE = IDX_HEADS ** -0.5
TOPK_MAX = 256
Q_BLOCK = 128
W_SSM = W_BRANCH
SSM_GROUP_CH = 16
SSM_GROUPS = W_SSM // SSM_GROUP_CH
SSM_STATE = 64
STEP_MIN = 1e-3
STEP_MAX = 1e-1
X_HEADS = 4
X_HEAD_DIM = 128
X_SCALE = X_HEAD_DIM ** -0.5
N_MEM = 256
D_FF = 2 * D_MODEL
IN_SIZES = (W_POOL, W_CONV, W_CONV, W_CONV, W_ATTN, HEAD_DIM_C, HEAD_DIM_C,
            IDX_HEADS * IDX_DIM, IDX_DIM, IDX_HEADS, W_SSM)
N_IN = sum(IN_SIZES)
EPS = 1e-6

kernel_name = 'hybrid_pool_conv_dsa_s5_decoder_step'


def rmsnorm(x, g):
    xf = x.astype(jnp.float32)
    y = xf * lax.rsqrt(jnp.mean(xf * xf, axis=-1, keepdims=True) + EPS)
    return (y * g.astype(jnp.float32)).astype(x.dtype)


def swiglu(x, w_in, w_out):
    a, b = jnp.split(x @ w_in, 2, axis=-1)
    return (jax.nn.silu(a) * b) @ w_out


def split_columns(z):
    parts, start = [], 0
    for n in IN_SIZES:
        parts.append(z[..., start:start + n])
        start += n
    return parts


def pool_mixer(xp, buf, pos0, w_mix, scale):
    Bt, T, W = xp.shape
    full = jnp.concatenate([buf.astype(xp.dtype), xp], axis=1).astype(jnp.float32)
    cs = jnp.concatenate([jnp.zeros((Bt, 1, W), jnp.float32), jnp.cumsum(full, axis=1)], axis=1)
    pos = pos0 + jnp.arange(T)
    outs = []
    for gi, w in enumerate(POOL_WINDOWS):
        sl = slice(gi * POOL_GROUP, (gi + 1) * POOL_GROUP)
        hi = cs[:, POOL_BUF + 1:POOL_BUF + 1 + T, sl]
        lo = cs[:, POOL_BUF + 1 - w:POOL_BUF + 1 - w + T, sl]
        cnt = jnp.minimum(pos + 1, w).astype(jnp.float32)[None, :, None]
        outs.append((hi - lo) / cnt - full[:, POOL_BUF:, sl])
    d = jnp.stack(outs, axis=2)
    y = jnp.einsum('btgc,gcd->btgd', d, w_mix.astype(jnp.float32)).reshape(Bt, T, W)
    y = y * scale.astype(jnp.float32)
    return y.astype(xp.dtype), full[:, -POOL_BUF:].astype(xp.dtype)


def conv_mixer(xc, bg, cg, buf, conv_w):
    T = xc.shape[1]
    v = cg * xc
    full = jnp.concatenate([buf.astype(v.dtype), v], axis=1)
    y = conv_w[0] * full[:, 0:T]
    for j in range(1, CONV_WIDTH):
        y = y + conv_w[j] * full[:, j:j + T]
    return bg * y, full[:, -(CONV_WIDTH - 1):]


def sparse_attention(q, k_all, v_all, iq, ik_all, iw, q_pos):
    L = k_all.shape[1]
    n_sel = min(TOPK_MAX, L // 4)
    causal = jnp.arange(L)[None, :] <= q_pos[:, None]
    idx_logits = jnp.einsum('bthd,bsd->bths', iq, ik_all, preferred_element_type=jnp.float32) * IDX_SCALE
    score = jnp.einsum('bths,bth->bts', jax.nn.relu(idx_logits), iw.astype(jnp.float32))
    score = jnp.where(causal[None], score, -jnp.inf)
    _, sel = lax.top_k(score, n_sel)
    valid = sel <= q_pos[None, :, None]
    gather = jax.vmap(lambda rows, ids: rows[ids])
    k_sel = gather(k_all, sel)
    v_sel = gather(v_all, sel)
    logits = jnp.einsum('bthd,btkd->bthk', q, k_sel, preferred_element_type=jnp.float32) * ATTN_SCALE
    logits = jnp.where(valid[:, :, None, :], logits, -jnp.inf)
    p = jax.nn.softmax(logits, axis=-1).astype(v_sel.dtype)
    return jnp.einsum('bthk,btkd->bthd', p, v_sel)


def sparse_attention_blocked(q, k_all, v_all, iq, ik_all, iw, q_pos):
    T = q.shape[1]
    if T <= Q_BLOCK:
        return sparse_attention(q, k_all, v_all, iq, ik_all, iw, q_pos)
    nb = T // Q_BLOCK

    def blk(a):
        return jnp.moveaxis(a.reshape(a.shape[0], nb, Q_BLOCK, *a.shape[2:]), 1, 0)

    outs = lax.map(
        lambda xs: sparse_attention(xs[0], k_all, v_all, xs[1], ik_all, xs[2], xs[3]),
        (blk(q), blk(iq), blk(iw), q_pos.reshape(nb, Q_BLOCK)))
    return jnp.moveaxis(outs, 0, 1).reshape(q.shape)


def complex_affine_combine(e1, e2):
    ar1, ai1, br1, bi1 = e1
    ar2, ai2, br2, bi2 = e2
    return (ar1 * ar2 - ai1 * ai2,
            ar1 * ai2 + ai1 * ar2,
            ar2 * br1 - ai2 * bi1 + br2,
            ar2 * bi1 + ai2 * br1 + bi2)


def ssm_mixer(xs, s_re0, s_im0, a_re, a_im, log_step, b_re, b_im, c_re, c_im, d, glu_w, glu_b):
    f32 = jnp.float32
    Bt, T, W = xs.shape
    uf = xs.astype(f32).reshape(Bt, T, SSM_GROUPS, SSM_GROUP_CH)
    a_re = a_re.astype(f32)
    a_im = a_im.astype(f32)
    step = jnp.exp(log_step.astype(f32))[:, None]
    decay = jnp.exp(step * a_re)
    ab_re = decay * jnp.cos(step * a_im)
    ab_im = decay * jnp.sin(step * a_im)
    den = a_re * a_re + a_im * a_im
    nr = ab_re - 1.0
    co_re = (nr * a_re + ab_im * a_im) / den
    co_im = (ab_im * a_re - nr * a_im) / den
    b_re = b_re.astype(f32)
    b_im = b_im.astype(f32)
    bb_re = co_re[..., None] * b_re - co_im[..., None] * b_im
    bb_im = co_re[..., None] * b_im + co_im[..., None] * b_re
    bu_re = jnp.einsum('btgh,gnh->btgn', uf, bb_re)
    bu_im = jnp.einsum('btgh,gnh->btgn', uf, bb_im)
    s_re0 = s_re0.astype(f32)
    s_im0 = s_im0.astype(f32)
    bu_re = bu_re.at[:, 0].add(ab_re * s_re0 - ab_im * s_im0)
    bu_im = bu_im.at[:, 0].add(ab_re * s_im0 + ab_im * s_re0)
    ar = jnp.broadcast_to(ab_re, bu_re.shape)
    ai = jnp.broadcast_to(ab_im, bu_im.shape)
    _, _, h_re, h_im = lax.associative_scan(complex_affine_combine, (ar, ai, bu_re, bu_im), axis=1)
    y = (jnp.einsum('btgn,ghn->btgh', h_re, c_re.astype(f32))
         - jnp.einsum('btgn,ghn->btgh', h_im, c_im.astype(f32))).reshape(Bt, T, W)
    y = y + d.astype(f32) * xs.astype(f32)
    z = jax.nn.gelu(y)
    out = z * jax.nn.sigmoid(z @ glu_w.astype(f32) + glu_b.astype(f32))
    return out.astype(xs.dtype), h_re[:, -1], h_im[:, -1]


def memory_kv(mem, g_mem, w_xk, w_xv, g_xk):
    Bt, M, _ = mem.shape
    m = rmsnorm(mem, g_mem)
    k = rmsnorm((m @ w_xk).reshape(Bt, M, X_HEADS, X_HEAD_DIM), g_xk)
    v = (m @ w_xv).reshape(Bt, M, X_HEADS, X_HEAD_DIM)
    return k, v


def cross_attention(h, mem_k, mem_v, w_xq, g_xq, w_xo):
    Bt, T, _ = h.shape
    q = rmsnorm((h @ w_xq).reshape(Bt, T, X_HEADS, X_HEAD_DIM), g_xq)
    s = jnp.einsum('bthd,bmhd->bhtm', q, mem_k.astype(q.dtype), preferred_element_type=jnp.float32) * X_SCALE
    p = jax.nn.softmax(s, axis=-1).astype(mem_v.dtype)
    o = jnp.einsum('bhtm,bmhd->bthd', p, mem_v)
    return o.reshape(Bt, T, X_HEADS * X_HEAD_DIM).astype(h.dtype) @ w_xo


def decoder_layer(x, pos0, past_k, past_v, past_ik, pool_buf, conv_buf, ssm_re, ssm_im, mem_k, mem_v, p):
    Bt, T, _ = x.shape
    h = x + 0.5 * swiglu(rmsnorm(x, p['norm_g'][0]), p['ffn_in'][0], p['ffn_out'][0])
    u = rmsnorm(h, p['norm_g'][1])
    xp, xc, bg, cg, q, k, v, iq, ik, iw, xs = split_columns(u @ p['w_in'])
    ya, new_pool = pool_mixer(xp, pool_buf, pos0, p['pool_mix'], p['pool_scale'])
    yb, new_conv = conv_mixer(xc, bg, cg, conv_buf, p['conv_w'])
    q = rmsnorm(q.reshape(Bt, T, N_HEADS_C, HEAD_DIM_C), p['q_norm_g'])
    k = rmsnorm(k, p['k_norm_g'])
    k_all = jnp.concatenate([past_k.astype(k.dtype), k], axis=1)
    v_all = jnp.concatenate([past_v.astype(v.dtype), v], axis=1)
    ik_all = jnp.concatenate([past_ik.astype(ik.dtype), ik], axis=1)
    q_pos = pos0 + jnp.arange(T)
    yc = sparse_attention_blocked(q, k_all, v_all, iq.reshape(Bt, T, IDX_HEADS, IDX_DIM),
                                  ik_all, iw * IDX_W_SCALE, q_pos).reshape(Bt, T, W_ATTN)
    yd, s_re, s_im = ssm_mixer(xs, ssm_re, ssm_im, p['ssm_a_re'], p['ssm_a_im'], p['ssm_log_step'],
                               p['ssm_b_re'], p['ssm_b_im'], p['ssm_c_re'], p['ssm_c_im'],
                               p['ssm_d'], p['ssm_glu_w'], p['ssm_glu_b'])
    br = jnp.stack([ya, yb, yc.astype(ya.dtype), yd], axis=2)
    outs = jnp.einsum('btiw,iwd->btid', br, p['w_branch'])
    gates = jax.nn.sigmoid((u @ p['w_gate'] + p['b_gate']).reshape(Bt, T, N_BRANCH, D_MODEL))
    h = h + jnp.sum(gates * outs, axis=2) @ p['w_o']
    h = h + cross_attention(rmsnorm(h, p['norm_g'][2]), mem_k, mem_v, p['w_xq'], p['xq_norm_g'], p['w_xo'])
    h = h + 0.5 * swiglu(rmsnorm(h, p['norm_g'][3]), p['ffn_in'][1], p['ffn_out'][1])
    return h, (k, v, ik, new_pool, new_conv, s_re, s_im)


def setup_inputs(seed: int = 0) -> dict:
    key = jax.random.key(seed)
    ks = iter(jax.random.split(key, 64))
    f32 = jnp.float32

    def nrm(shape, scale):
        return jax.random.normal(next(ks), shape, f32) * scale

    def gain(shape):
        return 1.0 + nrm(shape, 0.01)

    n_pages = PAST_LEN // PAGE_SIZE
    n_pool = (DEC_BATCH * n_pages * 5) // 4
    x_prompt = nrm((BATCH, SEQ, D_MODEL), 1.0)
    x_sample = nrm((DEC_BATCH, DEC_SEQ, D_MODEL), 1.0)
    cache_attn_k = nrm((DEPTH, n_pool, PAGE_SIZE, HEAD_DIM_C), 1.0)
    cache_attn_v = nrm((DEPTH, n_pool, PAGE_SIZE, HEAD_DIM_C), 1.0)
    cache_idx_k = nrm((DEPTH, n_pool, PAGE_SIZE, IDX_DIM), 1.0)
    cache_mem_k = nrm((DEPTH, DEC_BATCH, N_MEM, X_HEADS, X_HEAD_DIM), 1.0)
    cache_mem_v = nrm((DEPTH, DEC_BATCH, N_MEM, X_HEADS, X_HEAD_DIM), 1.0)
    state_pool = nrm((DEPTH, DEC_BATCH, POOL_BUF, W_POOL), 1.0)
    state_conv = nrm((DEPTH, DEC_BATCH, CONV_WIDTH - 1, W_CONV), 1.0)
    state_ssm_re = nrm((DEPTH, DEC_BATCH, SSM_GROUPS, SSM_STATE), 0.1)
    state_ssm_im = nrm((DEPTH, DEC_BATCH, SSM_GROUPS, SSM_STATE), 0.1)
    perm = jax.random.permutation(next(ks), n_pool)
    page_table = perm[:DEC_BATCH * n_pages].reshape(DEC_BATCH, n_pages).astype(jnp.int32)
    mem_prompt = nrm((BATCH, N_MEM, D_MODEL), 1.0)
    return {
        'x_prompt': x_prompt,
        'x_sample': x_sample,
        'cache_attn_k': cache_attn_k,
        'cache_attn_v': cache_attn_v,
        'cache_idx_k': cache_idx_k,
        'cache_mem_k': cache_mem_k,
        'cache_mem_v': cache_mem_v,
        'state_pool': state_pool,
        'state_conv': state_conv,
        'state_ssm_re': state_ssm_re,
        'state_ssm_im': state_ssm_im,
        'page_table': page_table,
        'mem_prompt': mem_prompt,
        'norm_g': gain((DEPTH, 4, D_MODEL)),
        'ffn_in': nrm((DEPTH, 2, D_MODEL, 2 * D_FF), D_MODEL ** -0.5),
        'ffn_out': nrm((DEPTH, 2, D_FF, D_MODEL), D_FF ** -0.5),
        'w_in': nrm((DEPTH, D_MODEL, N_IN), D_MODEL ** -0.5),
        'q_norm_g': gain((DEPTH, HEAD_DIM_C)),
        'k_norm_g': gain((DEPTH, HEAD_DIM_C)),
        'pool_mix': nrm((DEPTH, len(POOL_WINDOWS), POOL_GROUP, POOL_GROUP), POOL_GROUP ** -0.5),
        'pool_scale': gain((DEPTH, W_POOL)),
        'conv_w': nrm((DEPTH, CONV_WIDTH, W_CONV), CONV_WIDTH ** -0.5),
        'ssm_a_re': -0.5 + nrm((DEPTH, SSM_GROUPS, SSM_STATE), 0.01),
        'ssm_a_im': math.pi * jnp.arange(SSM_STATE, dtype=f32) + nrm((DEPTH, SSM_GROUPS, SSM_STATE), 0.01),
        'ssm_log_step': jax.random.uniform(next(ks), (DEPTH, SSM_GROUPS), f32,
                                           math.log(STEP_MIN), math.log(STEP_MAX)),
        'ssm_b_re': nrm((DEPTH, SSM_GROUPS, SSM_STATE, SSM_GROUP_CH), (2 * SSM_GROUP_CH) ** -0.5),
        'ssm_b_im': nrm((DEPTH, SSM_GROUPS, SSM_STATE, SSM_GROUP_CH), (2 * SSM_GROUP_CH) ** -0.5),
        'ssm_c_re': nrm((DEPTH, SSM_GROUPS, SSM_GROUP_CH, SSM_STATE), (2 * SSM_STATE) ** -0.5),
        'ssm_c_im': nrm((DEPTH, SSM_GROUPS, SSM_GROUP_CH, SSM_STATE), (2 * SSM_STATE) ** -0.5),
        'ssm_d': nrm((DEPTH, W_SSM), 1.0),
        'ssm_glu_w': nrm((DEPTH, W_SSM, W_SSM), W_SSM ** -0.5),
        'ssm_glu_b': nrm((DEPTH, W_SSM), 0.01),
        'w_branch': nrm((DEPTH, N_BRANCH, W_BRANCH, D_MODEL), W_BRANCH ** -0.5),
        'w_gate': nrm((DEPTH, D_MODEL, N_BRANCH * D_MODEL), D_MODEL ** -0.5),
        'b_gate': nrm((DEPTH, N_BRANCH * D_MODEL), 0.01),
        'w_o': nrm((DEPTH, D_MODEL, D_MODEL), 0.5 * D_MODEL ** -0.5),
        'mem_norm_g': gain((DEPTH, D_MODEL)),
        'w_xq': nrm((DEPTH, D_MODEL, X_HEADS * X_HEAD_DIM), D_MODEL ** -0.5),
        'w_xk': nrm((DEPTH, D_MODEL, X_HEADS * X_HEAD_DIM), D_MODEL ** -0.5),
        'w_xv': nrm((DEPTH, D_MODEL, X_HEADS * X_HEAD_DIM), D_MODEL ** -0.5),
        'xq_norm_g': gain((DEPTH, X_HEAD_DIM)),
        'xk_norm_g': gain((DEPTH, X_HEAD_DIM)),
        'w_xo': nrm((DEPTH, X_HEADS * X_HEAD_DIM, D_MODEL), (X_HEADS * X_HEAD_DIM) ** -0.5),
    }


def reference(x_prompt, x_sample, cache_attn_k, cache_attn_v, cache_idx_k, cache_mem_k, cache_mem_v,
              state_pool, state_conv, state_ssm_re, state_ssm_im, page_table, mem_prompt,
              norm_g, ffn_in, ffn_out, w_in, q_norm_g, k_norm_g, pool_mix, pool_scale, conv_w,
              ssm_a_re, ssm_a_im, ssm_log_step, ssm_b_re, ssm_b_im, ssm_c_re, ssm_c_im, ssm_d,
              ssm_glu_w, ssm_glu_b, w_branch, w_gate, b_gate, w_o, mem_norm_g, w_xq, w_xk, w_xv,
              xq_norm_g, xk_norm_g, w_xo):
    Bp = x_prompt.shape[0]
    Bs = x_sample.shape[0]
    dt = x_prompt.dtype
    hp, hs = x_prompt, x_sample
    pk, pv, pik, pmk, pmv, ppool, pconv, pre, pim = [], [], [], [], [], [], [], [], []
    sk, sv, sik, spool, sconv, sre, sim = [], [], [], [], [], [], []
    for l in range(DEPTH):
        p = {
            'norm_g': norm_g[l], 'ffn_in': ffn_in[l], 'ffn_out': ffn_out[l], 'w_in': w_in[l],
            'q_norm_g': q_norm_g[l], 'k_norm_g': k_norm_g[l], 'pool_mix': pool_mix[l],
            'pool_scale': pool_scale[l], 'conv_w': conv_w[l], 'ssm_a_re': ssm_a_re[l],
            'ssm_a_im': ssm_a_im[l], 'ssm_log_step': ssm_log_step[l], 'ssm_b_re': ssm_b_re[l],
            'ssm_b_im': ssm_b_im[l], 'ssm_c_re': ssm_c_re[l], 'ssm_c_im': ssm_c_im[l],
            'ssm_d': ssm_d[l], 'ssm_glu_w': ssm_glu_w[l], 'ssm_glu_b': ssm_glu_b[l],
            'w_branch': w_branch[l], 'w_gate': w_gate[l], 'b_gate': b_gate[l], 'w_o': w_o[l],
            'w_xq': w_xq[l], 'xq_norm_g': xq_norm_g[l], 'w_xo': w_xo[l],
        }
        mk, mv = memory_kv(mem_prompt, mem_norm_g[l], w_xk[l], w_xv[l], xk_norm_g[l])
        hp, st = decoder_layer(
            hp, 0,
            jnp.zeros((Bp, 0, HEAD_DIM_C), dt), jnp.zeros((Bp, 0, HEAD_DIM_C), dt),
            jnp.zeros((Bp, 0, IDX_DIM), dt),
            jnp.zeros((Bp, POOL_BUF, W_POOL), dt), jnp.zeros((Bp, CONV_WIDTH - 1, W_CONV), dt),
            jnp.zeros((Bp, SSM_GROUPS, SSM_STATE), jnp.float32),
            jnp.zeros((Bp, SSM_GROUPS, SSM_STATE), jnp.float32),
            mk, mv, p)
        pk.append(st[0]); pv.append(st[1]); pik.append(st[2]); ppool.append(st[3])
        pconv.append(st[4]); pre.append(st[5]); pim.append(st[6]); pmk.append(mk); pmv.append(mv)
        past_k = cache_attn_k[l][page_table].reshape(Bs, -1, HEAD_DIM_C)
        past_v = cache_attn_v[l][page_table].reshape(Bs, -1, HEAD_DIM_C)
        past_ik = cache_idx_k[l][page_table].reshape(Bs, -1, IDX_DIM)
        hs, st = decoder_layer(
            hs, past_k.shape[1], past_k, past_v, past_ik,
            state_pool[l], state_conv[l], state_ssm_re[l], state_ssm_im[l],
            cache_mem_k[l], cache_mem_v[l], p)
        sk.append(st[0]); sv.append(st[1]); sik.append(st[2]); spool.append(st[3])
        sconv.append(st[4]); sre.append(st[5]); sim.append(st[6])
    return (hp, hs,
            jnp.stack(pk), jnp.stack(pv), jnp.stack(pik), jnp.stack(pmk), jnp.stack(pmv),
            jnp.stack(ppool), jnp.stack(pconv), jnp.stack(pre), jnp.stack(pim),
            jnp.stack(sk), jnp.stack(sv), jnp.stack(sik), jnp.stack(spool), jnp.stack(sconv),
            jnp.stack(sre), jnp.stack(sim))
```

```python
import contextlib
import numpy as np
import concourse.bass as bass
import concourse.mybir as mybir
from concourse.bass_utils import run_bass_kernel_spmd

F32 = mybir.dt.float32
BF16 = mybir.dt.bfloat16
I32 = mybir.dt.int32
ALU = mybir.AluOpType
AF = mybir.ActivationFunctionType
AX = mybir.AxisListType

L = 2
D = 2048
NT = 1152
PT = 1024
NSEQ = 16
CH = [(0, 384), (384, 384), (768, 384)]
NEG = -1.0e30
EPS = 1e-6


class Eng:
    def __init__(self, fw, name, handle, is_pe=False):
        self.fw = fw
        self.name = name
        self.h = handle
        self.is_pe = is_pe
        self.sem = fw.new_sem("e_" + name)
        self.count = 0
        self.seen = {}

    def wait(self, sem, val):
        key = id(sem)
        if self.seen.get(key, 0) >= val:
            return
        self.seen[key] = val
        if not self.fw.dry:
            self.h.wait_ge(sem, val)


class T:
    __slots__ = ("name", "w", "rd", "sem", "ndma", "pend")

    def __init__(self, name):
        self.name = name
        self.w = None
        self.rd = {}
        self.sem = None
        self.ndma = 0
        self.pend = {}


class FW:
    def __init__(self, nc, dry=False):
        self.nc = nc
        self.dry = dry
        self.stack = contextlib.ExitStack()
        self.nsem = 0
        self.pe = Eng(self, "pe", nc.tensor, is_pe=True)
        self.act = Eng(self, "act", nc.scalar)
        self.dve = Eng(self, "dve", nc.vector)
        self.pool = Eng(self, "pool", nc.gpsimd)
        self.sp = Eng(self, "sp", nc.sync)
        self.engs = [self.pe, self.act, self.dve, self.pool, self.sp]
        self.dsems = {}
        self.nalloc = 0

    def new_sem(self, name):
        self.nsem += 1
        if self.dry:
            return object()
        return self.stack.enter_context(self.nc.semaphore(name + "_%d" % self.nsem))

    def sbuf(self, name, shape, dtype, stack=None):
        self.nalloc += 1
        return (stack or self.stack).enter_context(
            self.nc.sbuf_tensor("%s_%d" % (name, self.nalloc), list(shape), dtype))

    def psum(self, name, shape, dtype):
        return self.stack.enter_context(self.nc.psum_tensor(name, list(shape), dtype))

    def tok(self, name="t"):
        return T(name)

    def _deps(self, eng, reads, writes):
        for t in reads:
            if t.w is not None:
                e, c = t.w
                if not (e is eng and eng.is_pe):
                    eng.wait(e.sem, c)
            for sem, val in t.pend.values():
                eng.wait(sem, val)
        for t in writes:
            if t.w is not None:
                e, c = t.w
                if not (e is eng and eng.is_pe):
                    eng.wait(e.sem, c)
            for e, c in t.rd.items():
                if not (e is eng and eng.is_pe):
                    eng.wait(e.sem, c)
            for sem, val in t.pend.values():
                eng.wait(sem, val)

    def op(self, eng, reads, writes, fn):
        self._deps(eng, reads, writes)
        eng.count += 1
        if not self.dry:
            inst = fn()
            inst.then_inc(eng.sem, 1)
        for t in reads:
            t.rd[eng] = eng.count
        for t in writes:
            t.w = (eng, eng.count)
            t.rd = {}

    def dma(self, eng, reads, writes, fn, main=None, inc=16):
        self._deps(eng, reads, writes)
        toks = list(reads) + list(writes)
        if main is None:
            main = writes[0] if writes else reads[0]
        if main.sem is None:
            main.sem = self.new_sem("d_" + main.name)
        if not self.dry:
            inst = fn()
            inst.then_inc(main.sem, inc)
        main.ndma += 1
        ent = (main.sem, inc * main.ndma)
        self.dsems[id(main.sem)] = (main.sem, inc * main.ndma, eng)
        for t in toks + [main]:
            t.pend[id(main.sem)] = ent

    def barrier(self):
        for e in self.engs:
            for x in self.engs:
                if x is not e and x.count:
                    e.wait(x.sem, x.count)
            for sem, val, iss in self.dsems.values():
                if iss is e and e is self.pool:
                    continue
                e.wait(sem, val)

    @contextlib.contextmanager
    def scope(self):
        st = contextlib.ExitStack()
        try:
            yield st
        finally:
            self.barrier()
            st.close()

    def finish(self, out_tokens):
        for t in out_tokens:
            for sem, val in t.pend.values():
                self.sp.wait(sem, val)
        for e in self.engs:
            if e is not self.sp and e.count:
                self.sp.wait(e.sem, e.count)
        for sem, val, iss in self.dsems.values():
            self.sp.wait(sem, val)

    def close(self):
        self.stack.close()


W_IN_TILES = 30
TI_XP, TI_XC, TI_BG, TI_CG, TI_Q, TI_K, TI_IQ, TI_IK, TI_XS = 0, 4, 8, 12, 16, 20, 21, 25, 26


class Kern:
    def __init__(self, nc, f, plan, cfg):
        self.nc = nc
        self.f = f
        self.plan = plan
        self.cfg = cfg
        self.wkeys = []
        self.wi = 0
        self.wissued = 0
        self.outs = []
        self.dram = {}
        self.dbg = {}

    def din(self, name, shape, dtype=F32):
        if name not in self.dram:
            self.dram[name] = self.nc.dram_tensor(name, list(shape), dtype, kind="ExternalInput").ap()
        return self.dram[name]

    def dout(self, name, shape, dtype=F32):
        if name not in self.dram:
            self.dram[name] = self.nc.dram_tensor(name, list(shape), dtype, kind="ExternalOutput").ap()
            t = self.f.tok(name)
            self.outs.append(t)
            self.dram[name + "__tok"] = t
        return self.dram[name], self.dram[name + "__tok"]

    def init_banks(self):
        self.banks = [self.f.psum("bank%d" % i, [128, 512], F32) for i in range(8)]
        self.tbank = [self.f.tok("bank%d" % i) for i in range(8)]
        self.bi = 0

    def bank(self, lo=0, hi=8):
        key = (lo, hi)
        if not hasattr(self, "_bk"):
            self._bk = {}
        i = self._bk.get(key, lo)
        self._bk[key] = lo + (i + 1 - lo) % (hi - lo)
        return self.banks[i], self.tbank[i]

    NSLOT = 4

    def init_wring(self):
        self.wslots = [self.f.sbuf("wslot", [128, 2048], BF16) for _ in range(self.NSLOT)]
        self.twslot = [self.f.tok("wslot%d" % i) for i in range(self.NSLOT)]

    def _issue(self, idx):
        key = self.plan[idx]
        name, index, n = key
        self.din(name, self.WSHAPES[name])
        s = idx % self.NSLOT
        src = self.dram[name]
        for i in index:
            src = src[i]
        slot = self.wslots[s]
        self.f.dma(self.f.pool, [], [self.twslot[s]],
                   lambda: self.nc.gpsimd.dma_start(out=slot[:, 0:n], in_=src))

    WSHAPES = {"w_ffn_in": [L, 2, 64, 128, 2048], "w_ffn_out": [L, 2, 2, 16, 128, 2048],
               "w_in_t": [L, W_IN_TILES, 128, 2048], "w_v_rhs": [L, 128, 1024], "w_iw_rhs": [L, 128, 128],
               "w_pool": [L, 4, 128, 128], "w_glu": [L, 4, 128, 512], "w_gate": [L, 4, 16, 128, 2048],
               "w_branch": [L, 4, 16, 128, 512], "w_o": [L, 2, 16, 128, 1024], "w_xq": [L, 4, 128, 2048],
               "w_xk": [L, 4, 128, 2048], "w_xv_rhs": [L, 4, 128, 2048], "w_xo": [L, 16, 128, 512]}

    def wget(self, name, index, n):
        self.din(name, self.WSHAPES[name])
        key = (name, tuple(index), n)
        if self.f.dry:
            self.wkeys.append(key)
            s = len(self.wkeys) - 1
            return self.wslots[s % self.NSLOT], self.twslot[s % self.NSLOT]
        assert self.plan[self.wi] == key, (self.plan[self.wi], key)
        i = self.wi
        self.wi += 1
        while self.wissued < min(len(self.plan), i + self.NSLOT):
            self._issue(self.wissued)
            self.wissued += 1
        return self.wslots[i % self.NSLOT], self.twslot[i % self.NSLOT]

    def mm(self, out, lhsT, rhs, start, stop, reads, writes, **kw):
        self.f.op(self.f.pe, reads, writes,
                  lambda: self.nc.tensor.matmul(out, lhsT=lhsT, rhs=rhs, start=start, stop=stop, **kw))

    def dbg_dump(self, name, ap, tok, shape, dtype=F32):
        if not self.cfg.get("debug"):
            return
        d, t = self.dout("dbg_" + name, shape, dtype)
        self.f.dma(self.f.sp, [tok], [t], lambda: self.nc.sync.dma_start(out=d, in_=ap), main=tok)

    def setup(self):
        nc, f = self.nc, self.f
        self.init_banks()
        self.init_wring()
        self.hT = f.sbuf("hT", [128, 16, NT], F32)
        self.th = [f.tok("h%d" % i) for i in range(16)]
        xT = self.din("xT", [16, 128, NT])
        for kt in range(16):
            f.dma(f.sp, [], [self.th[kt]], lambda: nc.sync.dma_start(out=self.hT[:, kt, :], in_=xT[kt]))
        cst = self.din("consts_bf", [128, 4 * 128], F32)
        self.c_f32 = f.sbuf("c_f32", [128, 512], F32)
        self.tc = f.tok("consts")
        f.dma(f.sp, [], [self.tc], lambda: nc.sync.dma_start(out=self.c_f32[:], in_=cst))
        self.c_bf = f.sbuf("c_bf", [128, 512], BF16)
        f.op(f.dve, [self.tc], [self.tc], lambda: nc.vector.tensor_copy(out=self.c_bf[:], in_=self.c_f32[:]))
        self.ones_bf = self.c_bf[:, 0:128]
        self.bd64_bf = self.c_bf[:, 128:256]
        self.ident_bf = self.c_bf[:, 256:384]
        self.ident_f = self.c_f32[:, 256:384]
        self.swap_bf = self.c_bf[:, 384:512]
        for nm in ("cache_ik", "cache_k", "cache_v"):
            for l_ in range(L):
                self.din("%s_%d" % (nm, l_), [self.cfg.get("npool8", 20480), 1024])
        sp_ = self.din("gains", [128, L * 4 * 16])
        self.gains = f.sbuf("gains", [128, L * 4 * 16], F32)
        self.tg = f.tok("gains")
        f.dma(f.sp, [], [self.tg], lambda: nc.sync.dma_start(out=self.gains[:], in_=sp_))
        hp = self.din("headp", [128, 8 * L])
        self.headp = f.sbuf("headp", [128, 8 * L], F32)
        self.thp = f.tok("headp")
        f.dma(f.sp, [], [self.thp], lambda: nc.sync.dma_start(out=self.headp[:], in_=hp))
        cb = self.din("cbias", [128, 259])
        self.cbias = f.sbuf("cbias", [128, 259], F32)
        self.tcb = f.tok("cbias")
        f.dma(f.sp, [], [self.tcb], lambda: nc.sync.dma_start(out=self.cbias[:], in_=cb))
        ptr = self.din("pt_rep", [128, 16], I32)
        ptsb = f.sbuf("ptsb", [128, 16], I32)
        self.idx = f.sbuf("idx", [128, 16], I32)
        self.tidx = f.tok("idx")
        f.dma(f.sp, [], [self.tidx], lambda: nc.sync.dma_start(out=ptsb[:], in_=ptr))
        f.op(f.dve, [self.tidx, self.tcb], [self.tidx], lambda: nc.vector.tensor_scalar(
            out=self.idx[:], in0=ptsb[:], scalar1=8.0, scalar2=self.cbias[:, 258:259], op0=ALU.mult, op1=ALU.add))
        c2 = self.din("cst2", [128, 404])
        self.cst2 = f.sbuf("cst2", [128, 404], F32)
        self.tcb2 = f.tok("cst2")
        f.dma(f.sp, [], [self.tcb2], lambda: nc.sync.dma_start(out=self.cst2[:], in_=c2))
        C2 = self.cst2
        self.rcnt = C2[:, 0:64].rearrange("p (a b) -> p a b", b=16)
        self.pscale = C2[:, 64:72]
        self.convw = C2[:, 72:96]
        self.ssd = C2[:, 96:104]
        self.glub = C2[:, 104:112]
        self.xqg = C2[:, 112:114]
        self.xkg = C2[:, 114:116]
        self.memg = C2[:, 116:148]
        self.bgate = C2[:, 148:276]
        self.bmask = C2[:, 276:404]
        self.eps_t = f.sbuf("eps", [128, 1], F32)
        self.teps = f.tok("eps")
        f.op(f.dve, [], [self.teps], lambda: nc.vector.memset(self.eps_t[:], EPS))
        self.rstd = f.sbuf("rstd", [128, NT], F32)
        self.trstd = f.tok("rstd")

    def rmsnorm(self, uT, tu, gcol0, st):
        nc, f = self.nc, self.f
        sq = [f.sbuf("sq", [128, 384], BF16, st) for _ in range(4)]
        tsq = [f.tok("sq%d" % i) for i in range(4)]
        qi = 0
        for (c0, cn) in CH:
            bk, tb = self.bank()
            for kt in range(16):
                s_, ts_ = sq[qi % 4], tsq[qi % 4]
                qi += 1
                f.op(f.act, [self.th[kt]], [ts_],
                     lambda: nc.scalar.activation(out=s_[:], in_=self.hT[:, kt, c0:c0 + cn], func=AF.Square))
                self.mm(bk[:, 0:cn], self.ones_bf, s_[:], kt == 0, kt == 15, [ts_, self.tc], [tb])
            f.op(f.act, [tb, self.teps], [self.trstd],
                 lambda: nc.scalar.activation(out=self.rstd[:, c0:c0 + cn], in_=bk[:, 0:cn], func=AF.Sqrt,
                                              bias=self.eps_t[:, 0:1], scale=1.0 / D))
        f.op(f.dve, [self.trstd], [self.trstd], lambda: nc.vector.reciprocal(out=self.rstd[:], in_=self.rstd[:]))
        for kt in range(16):
            f.op(f.dve, [self.th[kt], self.trstd, self.tg], [tu[kt]],
                 lambda: nc.vector.scalar_tensor_tensor(
                     out=uT[:, kt, :], in0=self.hT[:, kt, :], scalar=self.gains[:, gcol0 + kt:gcol0 + kt + 1],
                     in1=self.rstd[:], op0=ALU.mult, op1=ALU.mult))

    def ffn(self, l, w, uT, tu, st):
        nc, f = self.nc, self.f
        g = f.sbuf("g", [128, 16, NT], BF16, st)
        tgt = [f.tok("g%d" % i) for i in range(16)]
        sa = [f.sbuf("sa", [128, NT], F32, st) for _ in range(2)]
        tsa = [f.tok("sa%d" % i) for i in range(2)]
        for hh in range(2):
            for jt in range(self.cfg.get("ffn_nj", 16)):
                j = hh * 16 + jt
                wa, twa = self.wget("w_ffn_in", (l, w, j), 2048)
                ba = [self.bank() for _ in CH]
                for kt in range(16):
                    for ci, (c0, cn) in enumerate(CH):
                        self.mm(ba[ci][0][:, 0:cn], wa[:, kt * 128:(kt + 1) * 128], uT[:, kt, c0:c0 + cn],
                                kt == 0, kt == 15, [twa, tu[kt]], [ba[ci][1]])
                s_, ts_ = sa[j % 2], tsa[j % 2]
                for ci, (c0, cn) in enumerate(CH):
                    f.op(f.act, [ba[ci][1]], [ts_],
                         lambda: nc.scalar.activation(out=s_[:, c0:c0 + cn], in_=ba[ci][0][:, 0:cn], func=AF.Silu))
                wb, twb = self.wget("w_ffn_in", (l, w, 32 + j), 2048)
                bb = [self.bank() for _ in CH]
                for kt in range(16):
                    for ci, (c0, cn) in enumerate(CH):
                        self.mm(bb[ci][0][:, 0:cn], wb[:, kt * 128:(kt + 1) * 128], uT[:, kt, c0:c0 + cn],
                                kt == 0, kt == 15, [twb, tu[kt]], [bb[ci][1]])
                for ci, (c0, cn) in enumerate(CH):
                    f.op(f.dve, [bb[ci][1], ts_], [tgt[jt]],
                         lambda: nc.vector.tensor_tensor(out=g[:, jt, c0:c0 + cn], in0=bb[ci][0][:, 0:cn],
                                                         in1=s_[:, c0:c0 + cn], op=ALU.mult))
            for e in range(self.cfg.get("ffn_ne", 16)):
                wo, two = self.wget("w_ffn_out", (l, w, hh, e), 2048)
                for ci, (c0, cn) in enumerate(CH):
                    bk, tb = self.bank()
                    nj = self.cfg.get("ffn_nj", 16)
                    for jt in range(nj):
                        self.mm(bk[:, 0:cn], wo[:, jt * 128:(jt + 1) * 128], g[:, jt, c0:c0 + cn],
                                jt == 0, jt == nj - 1, [two, tgt[jt]], [tb])
                    f.op(f.dve, [tb, self.th[e]], [self.th[e]],
                         lambda: nc.vector.scalar_tensor_tensor(
                             out=self.hT[:, e, c0:c0 + cn], in0=bk[:, 0:cn], scalar=0.5,
                             in1=self.hT[:, e, c0:c0 + cn], op0=ALU.mult, op1=ALU.add))

    def proj(self, wname, index, uT, tu, evac):
        w_, tw = self.wget(wname, index, 2048)
        bs = [self.bank() for _ in CH]
        for kt in range(16):
            for ci, (c0, cn) in enumerate(CH):
                self.mm(bs[ci][0][:, 0:cn], w_[:, kt * 128:(kt + 1) * 128], uT[:, kt, c0:c0 + cn],
                        kt == 0, kt == 15, [tw, tu[kt]], [bs[ci][1]])
        for ci, (c0, cn) in enumerate(CH):
            evac(ci, c0, cn, bs[ci][0], bs[ci][1])

    def layer(self, l):
        nc, f = self.nc, self.f
        cfg = self.cfg
        with f.scope() as st:
            uT = f.sbuf("uT", [128, 16, NT], BF16, st)
            tu = [f.tok("u%d" % i) for i in range(16)]
            self.rmsnorm(uT, tu, (l * 4 + 0) * 16, st)
            self.ffn(l, 0, uT, tu, st)
        if cfg.get("stop") == "ffn1":
            return False
        with f.scope() as st_m:
            self._st_m = st_m
            ycT = f.sbuf("ycT", [128, 4, NT], BF16, st_m)
            tyc = f.tok("ycT")
            if cfg.get("skip") or cfg.get("nq", 8) < 8:
                f.op(f.dve, [], [tyc], lambda: nc.vector.memset(ycT[:], 0.0))
            halo = f.sbuf("halo", [128, 68], F32, st_m)
            hown = f.sbuf("hown", [128, 68], F32, st_m)
            thalo, thown = f.tok("halo"), f.tok("hown")
            self._halo = (halo, thalo, hown, thown)
            self.mixer_attention(l, ycT, tyc, hown, thown)
            if l == 0:
                self.dbg_dump("yc%d" % l, ycT[:], tyc, [128, 4, NT], BF16)
            if cfg.get("stop") == "att":
                return False
            br = f.sbuf("br", [128, 12, NT], BF16, st_m)
            tbr = [f.tok("br%d" % i) for i in range(12)]
            self.mixer_ssm(l, br, tbr)
            self.mixer_poolconv(l, br, tbr, halo, thalo)
            if l == 0:
                self.dbg_dump("br%d" % l, br[:], tbr[11], [128, 12, NT], BF16)
            if cfg.get("stop") == "mix":
                return False
            self.gate_phase(l, br, tbr, ycT, tyc)
        if cfg.get("stop") == "gate":
            return False
        self.cross_attention(l)
        if cfg.get("stop") == "xattn":
            return False
        if cfg.get("stop") == "full" and l == 0:
            pass
        with f.scope() as st:
            uT = f.sbuf("uT", [128, 16, NT], BF16, st)
            tu = [f.tok("u%d" % i) for i in range(16)]
            self.rmsnorm(uT, tu, (l * 4 + 3) * 16, st)
            self.ffn(l, 1, uT, tu, st)
        return True

    def headnorm(self, src, tsrc, dh, gcol, out, tout, st, bd):
        nc, f = self.nc, self.f
        if not hasattr(st, "_hn"):
            st._hn = (f.sbuf("hsq", [128, NT], BF16, st), f.sbuf("hrs", [128, NT], F32, st), f.tok("hsq"), f.tok("hrs"))
        sq, rs, tsq, trs = st._hn
        f.op(f.act, [tsrc], [tsq], lambda: nc.scalar.activation(out=sq[:], in_=src, func=AF.Square))
        for (c0, cn) in CH:
            bk, tb = self.bank()
            self.mm(bk[:, 0:cn], bd, sq[:, c0:c0 + cn], True, True, [tsq, self.tc], [tb])
            f.op(f.act, [tb, self.teps], [trs],
                 lambda: nc.scalar.activation(out=rs[:, c0:c0 + cn], in_=bk[:, 0:cn], func=AF.Sqrt,
                                              bias=self.eps_t[:, 0:1], scale=1.0 / dh))
        f.op(f.dve, [trs], [trs], lambda: nc.vector.reciprocal(out=rs[:], in_=rs[:]))
        f.op(f.dve, [trs, tsrc, self.thp], [tout],
             lambda: nc.vector.scalar_tensor_tensor(out=out, in0=src, scalar=gcol, in1=rs[:],
                                                    op0=ALU.mult, op1=ALU.mult))

    def cust(self, ap, dims):
        return bass.AP(ap.tensor, ap.offset, [list(ap.ap[0])] + [list(d) for d in dims])

    def mixer_attention(self, l, ycT, tyc, halo, thalo):
        nc, f, cfg = self.nc, self.f, self.cfg
        PAYW = 2704
        with f.scope() as st_s:
            self._mixer_attention(l, ycT, tyc, halo, thalo, st_s, PAYW)

    def _mixer_attention(self, l, ycT, tyc, halo, thalo, st_s, PAYW):
        nc, f, cfg = self.nc, self.f, self.cfg
        iqa = f.sbuf("iqa", [128, 8, 128], BF16, st_s)
        qa = f.sbuf("qa", [128, 8, 128], BF16, st_s)
        siwbc = f.sbuf("siwbc", [128, 1024], F32, st_s)
        KTn = f.sbuf("KTn", [128, 128], BF16, st_s)
        IKTn = f.sbuf("IKTn", [128, 128], BF16, st_s)
        Vn = f.sbuf("Vn", [128, 65], BF16, st_s)
        tnew = f.tok("newkeys")
        tiqa, tqa, tsiwbc = f.tok("iqa"), f.tok("qa"), f.tok("siwbc")
        with f.scope() as st_a:
            qT = f.sbuf("qT", [128, 4, NT], BF16, st_a)
            iqT = f.sbuf("iqT", [128, 4, NT], BF16, st_a)
            tq, tiq = f.tok("qT"), f.tok("iqT")
            iw_tm = f.sbuf("iw_tm", [128, 9, 8], F32, st_a)
            tiw = f.tok("iw_tm")
            pay = f.sbuf("pay", [128, PAYW], BF16, st_a)
            tpay, tprev = f.tok("pay"), f.tok("prev")
            with f.scope() as st:
                uT = f.sbuf("uT", [128, 16, NT], BF16, st)
                tu = [f.tok("u%d" % i) for i in range(16)]
                self.rmsnorm(uT, tu, (l * 4 + 1) * 16, st)
                vtm = f.sbuf("vtm", [128, 9, 64], F32, st)
                tv = f.tok("vtm")
                pf = f.sbuf("pf", [128, NT], F32, st)
                tpf = f.tok("pf")
                kT, ikT, tkT, tikT = pf, pf, tpf, tpf
                ok, tok_ = self.dout("o_k", [L, 64, NT])
                oik, toik = self.dout("o_ik", [L, 64, NT])
                ov, tov = self.dout("o_v", [L, 9, 128, 64])

                def evac_to(dst, tdst):
                    def ev(ci, c0, cn, bk, tb):
                        f.op(f.act, [tb], [tdst], lambda: nc.scalar.copy(out=dst[:, c0:c0 + cn], in_=bk[:, 0:cn]))
                    return ev
                self.proj("w_in_t", (l, TI_K), uT, tu, evac_to(pf, tpf))
                self.headnorm(pf[:], tpf, 64, self.headp[:, 8 * l:8 * l + 1], pf[:], tpf, st, self.bd64_bf)
                f.dma(f.sp, [tpf], [tok_], lambda: nc.sync.dma_start(out=ok[l], in_=pf[0:64, :]), main=tpf)
                f.op(f.dve, [tpf], [tpay], lambda: nc.vector.tensor_copy(out=pay[:, 0:1024], in_=pf[:, 0:PT]))
                f.op(f.dve, [tpf], [tnew], lambda: nc.vector.tensor_copy(out=KTn[:], in_=pf[:, PT:NT]))
                self.proj("w_in_t", (l, TI_IK), uT, tu, evac_to(pf, tpf))
                f.dma(f.sp, [tpf], [toik], lambda: nc.sync.dma_start(out=oik[l], in_=pf[0:64, :]), main=tpf)
                f.op(f.dve, [tpf], [tpay], lambda: nc.vector.tensor_copy(out=pay[:, 1024:2048], in_=pf[:, 0:PT]))
                f.op(f.dve, [tpf], [tnew], lambda: nc.vector.tensor_copy(out=IKTn[:], in_=pf[:, PT:NT]))
                wv, twv = self.wget("w_v_rhs", (l,), 1024)
                for tt in range(9):
                    bk, tb = self.bank()
                    for kt in range(16):
                        self.mm(bk[:, 0:64], uT[:, kt, tt * 128:(tt + 1) * 128], wv[:, kt * 64:(kt + 1) * 64],
                                kt == 0, kt == 15, [twv, tu[kt]], [tb])
                    f.op(f.act, [tb], [tv], lambda: nc.scalar.copy(out=vtm[:, tt, :], in_=bk[:, 0:64]))
                wiw, twiw = self.wget("w_iw_rhs", (l,), 128)
                for tt in range(9):
                    bk, tb = self.bank()
                    for kt in range(16):
                        self.mm(bk[:, 0:8], uT[:, kt, tt * 128:(tt + 1) * 128], wiw[:, kt * 8:(kt + 1) * 8],
                                kt == 0, kt == 15, [twiw, tu[kt]], [tb])
                    f.op(f.act, [tb], [tiw], lambda: nc.scalar.copy(out=iw_tm[:, tt, :], in_=bk[:, 0:8]))
                f.dma(f.sp, [tv], [tov],
                      lambda: nc.sync.dma_start(out=ov[l].rearrange("t p d -> p t d"), in_=vtm[:]), main=tv)
                vview = pay[:, 2048:2568].rearrange("p (t d) -> p t d", d=65)
                f.op(f.dve, [tv], [tpay], lambda: nc.vector.tensor_copy(out=vview[:, :, 0:64], in_=vtm[:, 0:8, :]))
                f.op(f.dve, [], [tpay], lambda: nc.vector.memset(vview[:, :, 64:65], 1.0))
                f.op(f.dve, [tv], [tnew], lambda: nc.vector.tensor_copy(out=Vn[:, 0:64], in_=vtm[:, 8, :]))
                f.op(f.dve, [], [tnew], lambda: nc.vector.memset(Vn[:, 64:65], 1.0))
                for t in range(4):
                    self.proj("w_in_t", (l, TI_Q + t), uT, tu, evac_to(pf, tpf))
                    self.headnorm(pf[:], tpf, 64, self.headp[:, 8 * l + 1:8 * l + 2], qT[:, t, :], tq, st, self.bd64_bf)

                    def ev_iq(ci, c0, cn, bk, tb, t=t):
                        f.op(f.act, [tb], [tiq], lambda: nc.scalar.copy(out=iqT[:, t, c0:c0 + cn], in_=bk[:, 0:cn]))
                    self.proj("w_in_t", (l, TI_IQ + t), uT, tu, ev_iq)
                payf = halo[:]
                hx = f.sbuf("hx", [128, 3, 16], F32, st)
                thx = f.tok("hx")

                def small_proj(ti, dst):
                    w_, tw = self.wget("w_in_t", (l, ti), 2048)
                    bk, tb = self.bank()
                    for kt in range(16):
                        self.mm(bk[:, 0:16], w_[:, kt * 128:(kt + 1) * 128], uT[:, kt, PT - 16:PT], kt == 0, kt == 15, [tw, tu[kt]], [tb])
                    f.op(f.act, [tb], [thx], lambda: nc.scalar.copy(out=dst, in_=bk[:, 0:16]))
                for t in range(4):
                    small_proj(TI_XP + t, hx[:, 0, :])
                    f.op(f.dve, [thx], [thalo], lambda: nc.vector.tensor_copy(out=payf[:, t * 15:(t + 1) * 15], in_=hx[:, 0, 1:16]))
                    small_proj(TI_XC + t, hx[:, 1, :])
                    small_proj(TI_CG + t, hx[:, 2, :])
                    f.op(f.dve, [thx], [thalo], lambda: nc.vector.tensor_tensor(out=payf[:, 60 + 2 * t:62 + 2 * t], in0=hx[:, 1, 14:16], in1=hx[:, 2, 14:16], op=ALU.mult))
                f.op(f.dve, [], [tpay], lambda: nc.vector.memset(pay[:, 2568:PAYW], 0.0))
            prev = f.sbuf("prev", [128, PAYW], BF16, st_a)
            pay_d = self.dint("x1_pay_%d" % l, [128, PAYW], BF16)
            gat_d = self.dint("x1_gat_%d" % l, [256, PAYW], BF16)
            tpd, tgd = f.tok("x1pay"), f.tok("x1gat")
            f.dma(f.sp, [tpay], [tpd], lambda: nc.sync.dma_start(out=pay_d, in_=pay[:]), main=tpd)
            f.dma(f.pool, [tpd], [tgd], lambda: nc.gpsimd.collective_compute(
                "AllGather", ALU.bypass, replica_groups=self.cfg["groups"], ins=[pay_d], outs=[gat_d]),
                main=tgd, inc=1)
            f.dma(f.sp, [tgd], [tprev], lambda: nc.sync.dma_start(out=prev[:], in_=gat_d[0:128, :]), main=tprev)
            if "P" not in cfg.get("skip", ""):
                with f.scope() as st:
                    self.attention_prompt(l, st, qT, tq, iqT, tiq, iw_tm, tiw, pay, tpay, prev, tprev, ycT, tyc)
            with f.scope() as st:
                Dg = f.sbuf("sDg", [128, 2, 4, 128], F32, st)
                tDg = f.tok("sDg")
                for (src, tsrc, dst, tdst) in ((iqT, tiq, iqa, tiqa), (qT, tq, qa, tqa)):
                    for j in range(4):
                        f.op(f.dve, [tsrc], [tdst], lambda: nc.vector.tensor_copy(out=dst[0:64, 2 * j, :], in_=src[0:64, j, PT:NT]))
                        f.op(f.dve, [tsrc], [tdst], lambda: nc.vector.tensor_copy(out=dst[64:128, 2 * j + 1, :], in_=src[64:128, j, PT:NT]))
                        bk, tb = self.bank(4, 8)
                        self.mm(bk[:, 0:128], self.swap_bf, src[:, j, PT:NT], True, True, [tsrc, self.tc], [tb])
                        f.op(f.act, [tb], [tdst], lambda: nc.scalar.copy(out=dst[0:64, 2 * j + 1, :], in_=bk[0:64, 0:128]))
                        f.op(f.act, [tb], [tdst], lambda: nc.scalar.copy(out=dst[64:128, 2 * j, :], in_=bk[64:128, 0:128]))
                self.iw_broadcast(iw_tm, tiw, 8, Dg, tDg, siwbc, tsiwbc, natural=True)
        if "S" not in cfg.get("skip", ""):
            with f.scope() as st:
                self.attention_sample(l, st, iqa, tiqa, qa, tqa, siwbc, tsiwbc, KTn, IKTn, Vn, tnew, ycT, tyc)

    def iw_broadcast(self, iw_tm, tiw, tt, Dg, tDg, iwbc, tiwbc, natural=False):
        nc, f = self.nc, self.f
        a = iw_tm[:, tt, :]
        in1 = self.cust(a, [(1, 2), (2, 4), (0, 128)])
        if natural:
            in1 = self.cust(a, [(4, 2), (1, 4), (0, 128)])
        idf = self.ident_f
        in0 = self.cust(idf, [(0, 2), (0, 4), (1, 128)])
        f.op(f.dve, [tiw, self.tc], [tDg],
             lambda: nc.vector.tensor_tensor(out=Dg[:], in0=in0, in1=in1, op=ALU.mult))
        dflat = Dg[:].rearrange("p a b q -> p (a b q)")
        for hb in range(2):
            bk, tb = self.bank(4, 8)
            self.mm(bk[:, 0:512], self.c_f32[:, 0:128], dflat[:, hb * 512:(hb + 1) * 512], True, True,
                    [tDg, self.tc], [tb])
            f.op(f.act, [tb], [tiwbc], lambda: nc.scalar.copy(out=iwbc[:, hb * 512:(hb + 1) * 512], in_=bk[:, 0:512]))

    def topk_mask(self, S, tS, WK, tWK, m8, tm8, thr, MK, tMK, nk):
        nc, f = self.nc, self.f
        cur, tcur = S, tS
        for r in range(32):
            f.op(f.dve, [tcur], [tm8], lambda: nc.vector.max(out=m8[:, r * 8:(r + 1) * 8], in_=cur[:, 0:nk]))
            if r < 31:
                f.op(f.dve, [tcur, tm8], [tWK],
                     lambda: nc.vector.match_replace(out=WK[:, 0:nk], in_to_replace=m8[:, r * 8:(r + 1) * 8],
                                                     in_values=cur[:, 0:nk], imm_value=-3.0e38))
                cur, tcur = WK, tWK
        f.op(f.dve, [tm8], [tm8], lambda: nc.vector.tensor_scalar(out=thr[:], in0=m8[:, 255:256], scalar1=-1.0e29,
                                                                   scalar2=None, op0=ALU.max))
        f.op(f.dve, [tS, tm8], [tMK], lambda: nc.vector.tensor_scalar(out=MK[:, 0:nk], in0=S[:, 0:nk], scalar1=thr[:, 0:1],
                                                                       scalar2=None, op0=ALU.is_ge))

    def attention_prompt(self, l, st, qT, tq, iqT, tiq, iw_tm, tiw, pay, tpay, prev, tprev, ycT, tyc):
        nc, f = self.nc, self.f
        ST = f.sbuf("ST", [128, 16, 128], F32, st)
        S = f.sbuf("S", [128, 2048], F32, st)
        m8 = f.sbuf("m8", [128, 256], F32, st)
        thr = f.sbuf("thr", [128, 1], F32, st)
        MK = f.sbuf("MK", [128, 2048], BF16, st)
        MT = f.sbuf("MT", [128, 16, 128], BF16, st)
        RL = f.sbuf("RL", [128, 1024], F32, st)
        E = [f.sbuf("E", [128, 512], BF16, st) for _ in range(2)]
        iwbc = f.sbuf("iwbc", [128, 1024], F32, st)
        Dg = f.sbuf("Dg", [128, 2, 4, 128], F32, st)
        otm = f.sbuf("otm", [128, 512], BF16, st)
        rec = f.sbuf("rec", [128, 8], F32, st)
        tST, tS, tm8, tMK, tMT, tRL, tiwbc, tDg, totm, trec = [f.tok(n) for n in
            ("ST", "S", "m8", "MK", "MT", "RL", "iwbc", "Dg", "otm", "rec")]
        tE = [f.tok("E0"), f.tok("E1")]
        WK = ST[:].rearrange("p a b -> p (a b)")
        pb = self.cbias
        for i in range(self.cfg.get("nq", 8)):
            q0 = i * 128
            kts = list(range(8)) + [8 + j for j in range(i + 1)]
            nk = len(kts) * 128
            self.iw_broadcast(iw_tm, tiw, i, Dg, tDg, iwbc, tiwbc)
            for n, kt in enumerate(kts):
                Ks, tK = (prev, tprev) if kt < 8 else (pay, tpay)
                kc = 1024 + (kt % 8) * 128
                for hb in range(2):
                    bk, tb = self.bank(4, 8)
                    rhs = self.cust(iqT[hb * 64:(hb + 1) * 64, 0, q0:q0 + 128], [(NT, 4), (1, 128)])
                    self.mm(bk[:, 0:512], Ks[hb * 64:(hb + 1) * 64, kc:kc + 128], rhs, True, True, [tK, tiq], [tb])
                    f.op(f.act, [tb], [tRL], lambda: nc.scalar.activation(out=RL[:, hb * 512:(hb + 1) * 512],
                                                                           in_=bk[:, 0:512], func=AF.Relu))
                f.op(f.dve, [tRL, tiwbc], [tRL], lambda: nc.vector.tensor_tensor(out=RL[:], in0=RL[:], in1=iwbc[:], op=ALU.mult))
                f.op(f.dve, [tRL], [tST], lambda: nc.vector.tensor_reduce(
                    out=ST[:, n, :], in_=self.cust(RL[:], [(1, 128), (128, 8)]), axis=AX.X, op=ALU.add))
            for n, kt in enumerate(kts):
                bk, tb = self.bank(4, 8)
                f.op(f.pe, [tST, self.tc], [tb], lambda: nc.tensor.transpose(out=bk[:, 0:128], in_=ST[:, n, :], identity=self.ident_f))
                dst = S[:, n * 128:(n + 1) * 128]
                if kt < 8:
                    f.op(f.act, [tb, self.tcb], [tS], lambda: nc.scalar.activation(out=dst, in_=bk[:, 0:128], func=AF.Identity,
                                                                                   bias=pb[:, 0:1], scale=1.0))
                elif kt == 8 + i:
                    f.op(f.dve, [tb, self.tcb], [tS], lambda: nc.vector.tensor_tensor(out=dst, in0=bk[:, 0:128], in1=pb[:, 1:129], op=ALU.add))
                else:
                    f.op(f.act, [tb], [tS], lambda: nc.scalar.copy(out=dst, in_=bk[:, 0:128]))
            self.topk_mask(S, tS, WK, tST, m8, tm8, thr, MK, tMK, nk)
            for n in range(len(kts)):
                bk, tb = self.bank(4, 8)
                bkb = bk[:].bitcast(BF16)
                f.op(f.pe, [tMK, self.tc], [tb], lambda: nc.tensor.transpose(out=bkb[:, 0:128], in_=MK[:, n * 128:(n + 1) * 128], identity=self.ident_bf))
                f.op(f.act, [tb], [tMT], lambda: nc.scalar.copy(out=MT[:, n, :], in_=bkb[:, 0:128]))
            for hb in range(2):
                accs = [(self.banks[j], self.tbank[j]) for j in range(4)]
                for n, kt in enumerate(kts):
                    Ks, tK = (prev, tprev) if kt < 8 else (pay, tpay)
                    kc = (kt % 8) * 128
                    vb = 2048 + (kt % 8) * 65
                    bk, tb = self.bank(4, 8)
                    rhs = self.cust(qT[hb * 64:(hb + 1) * 64, 0, q0:q0 + 128], [(NT, 4), (1, 128)])
                    self.mm(bk[:, 0:512], Ks[hb * 64:(hb + 1) * 64, kc:kc + 128], rhs, True, True, [tK, tq], [tb])
                    e, te = E[n % 2], tE[n % 2]
                    f.op(f.act, [tb], [te], lambda: nc.scalar.activation(out=e[:], in_=bk[:, 0:512], func=AF.Exp, scale=0.125))
                    ev = e[:].rearrange("p (j q) -> p j q", j=4)
                    f.op(f.dve, [te, tMT], [te], lambda: nc.vector.tensor_tensor(
                        out=ev, in0=ev, in1=self.cust(MT[:, n, :], [(0, 4), (1, 128)]), op=ALU.mult))
                    for j in range(4):
                        self.mm(accs[j][0][:, 0:65], e[:, j * 128:(j + 1) * 128], Ks[:, vb:vb + 65],
                                n == 0, n == len(kts) - 1, [te, tK], [accs[j][1]])
                for j in range(4):
                    c = hb * 4 + j
                    f.op(f.dve, [accs[j][1]], [trec], lambda: nc.vector.reciprocal(out=rec[:, c:c + 1], in_=accs[j][0][:, 64:65]))
                    f.op(f.dve, [accs[j][1], trec], [totm], lambda: nc.vector.tensor_scalar(
                        out=otm[:, (j * 2 + hb) * 64:(j * 2 + hb + 1) * 64], in0=accs[j][0][:, 0:64],
                        scalar1=rec[:, c:c + 1], scalar2=None, op0=ALU.mult))
            for t in range(4):
                bk, tb = self.bank(4, 8)
                bkb = bk[:].bitcast(BF16)
                f.op(f.pe, [totm, self.tc], [tb], lambda: nc.tensor.transpose(out=bkb[:, 0:128], in_=otm[:, t * 128:(t + 1) * 128], identity=self.ident_bf))
                f.op(f.act, [tb], [tyc], lambda: nc.scalar.copy(out=ycT[:, t, q0:q0 + 128], in_=bkb[:, 0:128]))

    def attention_sample(self, l, st, iqa, tiqa, qa, tqa, iwbc, tiwbc, KTn, IKTn, Vn, tnew, ycT, tyc):
        nc, f = self.nc, self.f
        NKT = 17
        ST = f.sbuf("sST", [128, NKT, 128], F32, st)
        S = f.sbuf("sS", [128, NKT * 128], F32, st)
        m8 = f.sbuf("sm8", [128, 256], F32, st)
        thr = f.sbuf("sthr", [128, 1], F32, st)
        MK = f.sbuf("sMK", [128, NKT * 128], BF16, st)
        MT = f.sbuf("sMT", [128, NKT, 128], BF16, st)
        G = [f.sbuf("G", [128, 1024], F32, st) for _ in range(4)]
        tG = [f.tok("G%d" % i) for i in range(4)]
        XT = [f.sbuf("XT", [128, 1024], BF16, st) for _ in range(2)]
        tXT = [f.tok("XT%d" % i) for i in range(2)]
        Va = [f.sbuf("Va", [128, 16, 65], BF16, st) for _ in range(2)]
        tVa = [f.tok("Va%d" % i) for i in range(2)]
        RL = f.sbuf("sRL", [128, NKT, 64], F32, st)
        E = [f.sbuf("sE", [128, NKT, 64], BF16, st) for _ in range(2)]
        tE = [f.tok("sE%d" % i) for i in range(2)]
        oall = f.sbuf("oall", [64, 16, 128], BF16, st)
        rec = f.sbuf("srec", [64, 1], F32, st)
        tST, tS, tm8, tMK, tMT, tRL, toall, trec = [f.tok(n) for n in
            ("sST", "sS", "sm8", "sMK", "sMT", "sRL", "oall", "srec")]
        WK = ST[:].rearrange("p a b -> p (a b)")
        cik = self.dram["cache_ik_%d" % l]
        ck = self.dram["cache_k_%d" % l]
        cv = self.dram["cache_v_%d" % l]
        def gather(cache, s, gi):
            g, tg = G[gi], tG[gi]
            f.dma(f.pool, [self.tidx], [tg], lambda: nc.gpsimd.indirect_dma_start(
                out=g[:], out_offset=None, in_=cache,
                in_offset=bass.IndirectOffsetOnAxis(ap=self.idx[:, s:s + 1], axis=0)))
            return g, tg

        def transpose_keys(g, tg, xt, txt):
            var = self.cfg.get("tkv", "C")
            if var == "B":
                for pr in range(8):
                    bk, tb = self.bank(4, 8)
                    f.op(f.pe, [tg, self.tc], [tb], lambda: nc.tensor.transpose(
                        out=bk[:, 0:128], in_=g[:, pr * 128:(pr + 1) * 128], identity=self.ident_f))
                    f.op(f.act, [tb], [txt], lambda: nc.scalar.copy(out=xt[:, pr * 128:(pr + 1) * 128], in_=bk[:, 0:128]))
                return
            for q4 in range(2):
                bk, tb = self.bank(4, 8)
                for i4 in range(4):
                    pr = q4 * 4 + i4
                    f.op(f.pe, [tg, self.tc], [tb], lambda: nc.tensor.transpose(
                        out=bk[:, i4 * 128:(i4 + 1) * 128], in_=g[:, pr * 128:(pr + 1) * 128],
                        identity=self.ident_f))
                if var == "A":
                    continue
                f.op(f.act, [tb], [txt], lambda: nc.scalar.copy(out=xt[:, q4 * 512:(q4 + 1) * 512], in_=bk[:, 0:512]))

        def key_ops(xt, kt, newT, qsrc, s):
            hp = (kt % 2) * 64 if kt < 16 else 0
            lhsT = xt[hp:hp + 64, (kt // 2) * 128:(kt // 2 + 1) * 128] if kt < 16 else newT[0:64, :]
            rhs = self.cust(qsrc[hp:hp + 64, 0, s * 8:(s + 1) * 8], [(128, 8), (1, 8)])
            return lhsT, rhs

        sstop = self.cfg.get("sstop", 99)
        for s in range(NSEQ if sstop >= 1 else 0):
            g, tg = gather(cik, s, s % 2)
            xt, txt = XT[s % 2], tXT[s % 2]
            if sstop < 2:
                continue
            transpose_keys(g, tg, xt, txt)
            if sstop < 3:
                continue
            for kts in (list(range(0, 16, 2)), list(range(1, 16, 2)), [16]):
                bk, tb = self.bank(4, 8)
                for n, kt in enumerate(kts):
                    lhsT, rhs = key_ops(xt, kt, IKTn, iqa, s)
                    self.mm(bk[:, n * 64:(n + 1) * 64], lhsT, rhs, True, True, [txt, tnew, tiqa], [tb])
                w = len(kts)
                f.op(f.act, [tb], [tRL], lambda: nc.scalar.activation(
                    out=self.cust(RL[:, kts[0], :], [(128 if w > 1 else 64, w), (1, 64)]),
                    in_=bk[:, 0:w * 64].rearrange("p (a b) -> p a b", b=64), func=AF.Relu))
            if sstop < 4:
                continue
            rl4 = RL[:].rearrange("p k (h q) -> p k h q", h=8)
            f.op(f.dve, [tRL, tiwbc], [tRL], lambda: nc.vector.tensor_tensor(
                out=rl4, in0=rl4, in1=self.cust(iwbc[:, s * 8:s * 8 + 8], [(0, NKT), (128, 8), (1, 8)]), op=ALU.mult))
            f.op(f.dve, [tRL], [tST], lambda: nc.vector.tensor_reduce(
                out=ST[:, :, s * 8:(s + 1) * 8], in_=self.cust(RL[:], [(64, NKT), (1, 8), (8, 8)]), axis=AX.X, op=ALU.add))
        if sstop < 5:
            return
        for n in range(NKT):
            bk, tb = self.bank(4, 8)
            f.op(f.pe, [tST, self.tc], [tb], lambda: nc.tensor.transpose(out=bk[:, 0:128], in_=ST[:, n, :], identity=self.ident_f))
            dst = S[:, n * 128:(n + 1) * 128]
            if n == 16:
                f.op(f.dve, [tb, self.tcb], [tS], lambda: nc.vector.tensor_tensor(out=dst, in0=bk[:, 0:128], in1=self.cbias[:, 129:257], op=ALU.add))
            else:
                f.op(f.act, [tb], [tS], lambda: nc.scalar.copy(out=dst, in_=bk[:, 0:128]))
        self.topk_mask(S, tS, WK, tST, m8, tm8, thr, MK, tMK, NKT * 128)
        for n in range(NKT):
            bk, tb = self.bank(4, 8)
            bkb = bk[:].bitcast(BF16)
            f.op(f.pe, [tMK, self.tc], [tb], lambda: nc.tensor.transpose(out=bkb[:, 0:128], in_=MK[:, n * 128:(n + 1) * 128], identity=self.ident_bf))
            f.op(f.act, [tb], [tMT], lambda: nc.scalar.copy(out=MT[:, n, :], in_=bkb[:, 0:128]))
        if sstop < 6:
            return
        for s in range(NSEQ):
            g, tg = gather(ck, s, s % 2)
            gv, tgv = gather(cv, s, 2 + s % 2)
            xt, txt = XT[s % 2], tXT[s % 2]
            transpose_keys(g, tg, xt, txt)
            va, tva = Va[s % 2], tVa[s % 2]
            f.op(f.dve, [tgv], [tva], lambda: nc.vector.tensor_copy(out=va[:, :, 0:64], in_=gv[:].rearrange("p (a b) -> p a b", b=64)))
            f.op(f.dve, [], [tva], lambda: nc.vector.memset(va[:, :, 64:65], 1.0))
            e, te = E[s % 2], tE[s % 2]
            for kts in (list(range(0, 16, 2)), list(range(1, 16, 2)), [16]):
                bk, tb = self.bank(4, 8)
                for n, kt in enumerate(kts):
                    lhsT, rhs = key_ops(xt, kt, KTn, qa, s)
                    self.mm(bk[:, n * 64:(n + 1) * 64], lhsT, rhs, True, True, [txt, tnew, tqa], [tb])
                w = len(kts)
                f.op(f.act, [tb], [te], lambda: nc.scalar.activation(
                    out=self.cust(e[:, kts[0], :], [(128 if w > 1 else 64, w), (1, 64)]),
                    in_=bk[:, 0:w * 64].rearrange("p (a b) -> p a b", b=64), func=AF.Exp, scale=0.125))
            e4 = e[:].rearrange("p k (h q) -> p k h q", h=8)
            f.op(f.dve, [te, tMT], [te], lambda: nc.vector.tensor_tensor(
                out=e4, in0=e4, in1=self.cust(MT[:, 0, s * 8:(s + 1) * 8], [(128, NKT), (0, 8), (1, 8)]), op=ALU.mult))
            acc, tacc = self.banks[0], self.tbank[0]
            for kt in range(NKT):
                rv = va[:, kt, :] if kt < 16 else Vn[:]
                self.mm(acc[0:64, 0:65], e[:, kt, :], rv, kt == 0, kt == NKT - 1, [te, tva, tnew], [tacc])
            f.op(f.dve, [tacc], [trec], lambda: nc.vector.reciprocal(out=rec[:], in_=acc[0:64, 64:65]))
            for d2 in range(2):
                f.op(f.dve, [tacc, trec], [toall], lambda: nc.vector.tensor_scalar(
                    out=oall[:, s, d2 * 64:(d2 + 1) * 64], in0=acc[0:64, 0:64], scalar1=rec[:, 0:1], scalar2=None, op0=ALU.mult))
        if sstop < 7:
            return
        bk, tb = self.bank(4, 8)
        bkb = bk[:].bitcast(BF16)
        for s in range(NSEQ):
            f.op(f.pe, [toall, self.tc], [tb], lambda: nc.tensor.transpose(
                out=bkb[:, s * 64:(s + 1) * 64], in_=oall[:, s, :], identity=self.ident_bf[0:64, 0:64]))
        for h in range(8):
            p0 = (h % 2) * 64
            src = self.cust(bkb[p0:p0 + 64, h * 8:h * 8 + 8], [(64, 16), (1, 8)])
            dst = ycT[p0:p0 + 64, h // 2, PT:NT].rearrange("p (s q) -> p s q", q=8)
            f.op(f.act, [tb], [tyc], lambda: nc.scalar.copy(out=dst, in_=src))

    def apply_norm(self, uT, tu, gcol0):
        nc, f = self.nc, self.f
        for kt in range(16):
            f.op(f.dve, [self.th[kt], self.trstd, self.tg], [tu[kt]],
                 lambda: nc.vector.scalar_tensor_tensor(
                     out=uT[:, kt, :], in0=self.hT[:, kt, :], scalar=self.gains[:, gcol0 + kt:gcol0 + kt + 1],
                     in1=self.rstd[:], op0=ALU.mult, op1=ALU.mult))

    def proj_split(self, wname, index, uT, tu, evac):
        def ev(ci, c0, cn, bk, tb):
            if c0 + cn <= PT:
                evac("p", c0, cn, bk[:, 0:cn], tb)
            else:
                evac("p", c0, PT - c0, bk[:, 0:PT - c0], tb)
                evac("s", 0, 128, bk[:, PT - c0:cn], tb)
        self.proj(wname, index, uT, tu, ev)

    def mixer_poolconv(self, l, br, tbr, halo, thalo):
        nc, f, cfg = self.nc, self.f, self.cfg
        PW = 1040
        with f.scope() as st:
            uT = f.sbuf("uT", [128, 16, NT], BF16, st)
            tu = [f.tok("u%d" % i) for i in range(16)]
            self.apply_norm(uT, tu, (l * 4 + 1) * 16)
            X = f.sbuf("pX", [128, PW + 384], F32, st)
            A = f.sbuf("pA", [128, PW + 384], F32, st)
            B = f.sbuf("pB", [128, PW + 384], F32, st)
            dT = f.sbuf("pdT", [128, NT], BF16, st)
            tmp16 = f.sbuf("ptmp", [128, 16], F32, st)
            tX, tA, tB, tdT, ttmp = [f.tok(n) for n in ("pX", "pA", "pB", "pdT", "ptmp")]
            sp_in = self.din("st_pool", [L, 4, 128, 16, 15])
            sc_in = self.din("st_conv", [L, 4, 128, 16, 2])
            opp, topp = self.dout("o_pool_p", [L, 4, 128, 15])
            ops_, tops = self.dout("o_pool_s", [L, 4, 128, 16, 15])
            ocp, tocp = self.dout("o_conv_p", [L, 4, 128, 2])
            ocs, tocs = self.dout("o_conv_s", [L, 4, 128, 16, 2])

            def SV(buf, a, b):
                return buf[:, PW:PW + 384].rearrange("p (s c) -> p s c", c=24)[:, :, a:b]
            for t in range(4):
                w = 2 ** (t + 1)
                f.op(f.dve, [thalo], [tX], lambda: nc.vector.tensor_copy(out=X[:, 1:16], in_=halo[:, t * 15:(t + 1) * 15]))
                f.dma(f.sp, [], [tX], lambda: nc.sync.dma_start(out=SV(X, 1, 16), in_=sp_in[l, t]))

                def ev_x(kind, t0, n, src, tb):
                    if kind == "p":
                        f.op(f.act, [tb], [tX], lambda: nc.scalar.copy(out=X[:, 16 + t0:16 + t0 + n], in_=src))
                    else:
                        f.op(f.act, [tb], [tX], lambda: nc.scalar.copy(out=SV(X, 16, 24), in_=src.rearrange("p (s q) -> p s q", q=8)))
                self.proj_split("w_in_t", (l, TI_XP + t), uT, tu, ev_x)
                f.dma(f.sp, [tX], [topp], lambda: nc.sync.dma_start(out=opp[l, t], in_=X[:, PW - 15:PW]), main=tX)
                f.dma(f.sp, [tX], [tops], lambda: nc.sync.dma_start(out=ops_[l, t], in_=SV(X, 9, 24)), main=tX)
                src_, tsrc = X, tX
                bufs = [(A, tA), (B, tB)]
                sh = 1
                for lev in range(t + 1):
                    dst, tdst = bufs[lev % 2]
                    lo = 2 * sh
                    f.op(f.dve, [tsrc], [tdst], lambda: nc.vector.tensor_tensor(
                        out=dst[:, lo:PW], in0=src_[:, lo:PW], in1=src_[:, lo - sh:PW - sh], op=ALU.add))
                    f.op(f.dve, [tsrc], [tdst], lambda: nc.vector.tensor_tensor(
                        out=SV(dst, lo, 24), in0=SV(src_, lo, 24), in1=SV(src_, lo - sh, 24 - sh), op=ALU.add))
                    src_, tsrc = dst, tdst
                    sh *= 2
                f.op(f.dve, [tsrc, tX], [tdT], lambda: nc.vector.scalar_tensor_tensor(
                    out=dT[:, 0:PT], in0=src_[:, 16:PW], scalar=1.0 / w, in1=X[:, 16:PW], op0=ALU.mult, op1=ALU.subtract))
                f.op(f.dve, [tsrc, tX], [tdT], lambda: nc.vector.scalar_tensor_tensor(
                    out=dT[:, PT:NT].rearrange("p (s q) -> p s q", q=8), in0=SV(src_, 16, 24), scalar=1.0 / w,
                    in1=SV(X, 16, 24), op0=ALU.mult, op1=ALU.subtract))
                f.op(f.dve, [tsrc, self.tcb2], [ttmp], lambda: nc.vector.tensor_tensor(
                    out=tmp16[:], in0=src_[:, 16:32], in1=self.rcnt[:, t, :], op=ALU.mult))
                f.op(f.dve, [ttmp, tX], [tdT], lambda: nc.vector.tensor_tensor(
                    out=dT[:, 0:16], in0=tmp16[:], in1=X[:, 16:32], op=ALU.subtract))
                wpool, twpool = self.wget("w_pool", (l, t), 128)
                for (c0, cn) in CH:
                    bk, tb = self.bank()
                    self.mm(bk[:, 0:cn], wpool[:, 0:128], dT[:, c0:c0 + cn], True, True, [twpool, tdT], [tb])
                    f.op(f.act, [tb, self.tcb2], [tbr[t]], lambda: nc.scalar.activation(
                        out=br[:, t, c0:c0 + cn], in_=bk[:, 0:cn], func=AF.Identity, scale=self.pscale[:, l * 4 + t:l * 4 + t + 1]))
            VW = 1026
            V = A
            tV = tA
            Y = B
            tY = tB
            CG = X
            tCG = tX

            def VS(a, b):
                return V[:, VW:VW + 160].rearrange("p (s c) -> p s c", c=10)[:, :, a:b]

            def YS():
                return Y[:, PT:NT].rearrange("p (s q) -> p s q", q=8)
            for c in range(4):
                f.op(f.dve, [thalo], [tV], lambda: nc.vector.tensor_copy(out=V[:, 0:2], in_=halo[:, 60 + 2 * c:62 + 2 * c]))
                f.dma(f.sp, [], [tV], lambda: nc.sync.dma_start(out=VS(0, 2), in_=sc_in[l, c]))

                def ev_cg(kind, t0, n, src, tb):
                    o = CG[:, t0:t0 + n] if kind == "p" else CG[:, PT:NT]
                    f.op(f.act, [tb], [tCG], lambda: nc.scalar.copy(out=o, in_=src))
                self.proj_split("w_in_t", (l, TI_CG + c), uT, tu, ev_cg)

                def ev_xc(kind, t0, n, src, tb):
                    if kind == "p":
                        f.op(f.dve, [tb, tCG], [tV], lambda: nc.vector.tensor_tensor(out=V[:, 2 + t0:2 + t0 + n], in0=src, in1=CG[:, t0:t0 + n], op=ALU.mult))
                    else:
                        f.op(f.dve, [tb, tCG], [tV], lambda: nc.vector.tensor_tensor(
                            out=VS(2, 10), in0=src.rearrange("p (s q) -> p s q", q=8),
                            in1=CG[:, PT:NT].rearrange("p (s q) -> p s q", q=8), op=ALU.mult))
                self.proj_split("w_in_t", (l, TI_XC + c), uT, tu, ev_xc)
                f.dma(f.sp, [tV], [tocp], lambda: nc.sync.dma_start(out=ocp[l, c], in_=V[:, VW - 2:VW]), main=tV)
                f.dma(f.sp, [tV], [tocs], lambda: nc.sync.dma_start(out=ocs[l, c], in_=VS(8, 10)), main=tV)
                cw = lambda j: self.convw[:, (l * 4 + c) * 3 + j:(l * 4 + c) * 3 + j + 1]
                f.op(f.dve, [tV, self.tcb2], [tY], lambda: nc.vector.tensor_scalar(out=Y[:, 0:PT], in0=V[:, 2:VW], scalar1=cw(2), scalar2=None, op0=ALU.mult))
                f.op(f.dve, [tV, self.tcb2], [tY], lambda: nc.vector.tensor_scalar(out=YS(), in0=VS(2, 10), scalar1=cw(2), scalar2=None, op0=ALU.mult))
                for j, sh in ((1, 1), (0, 2)):
                    f.op(f.dve, [tV, tY, self.tcb2], [tY], lambda: nc.vector.scalar_tensor_tensor(
                        out=Y[:, 0:PT], in0=V[:, 2 - sh:VW - sh], scalar=cw(j), in1=Y[:, 0:PT], op0=ALU.mult, op1=ALU.add))
                    f.op(f.dve, [tV, tY, self.tcb2], [tY], lambda: nc.vector.scalar_tensor_tensor(
                        out=YS(), in0=VS(2 - sh, 10 - sh), scalar=cw(j), in1=YS(), op0=ALU.mult, op1=ALU.add))

                def ev_bg(kind, t0, n, src, tb):
                    if kind == "p":
                        f.op(f.dve, [tb, tY], [tbr[4 + c]], lambda: nc.vector.tensor_tensor(out=br[:, 4 + c, t0:t0 + n], in0=src, in1=Y[:, t0:t0 + n], op=ALU.mult))
                    else:
                        f.op(f.dve, [tb, tY], [tbr[4 + c]], lambda: nc.vector.tensor_tensor(out=br[:, 4 + c, PT:NT], in0=src, in1=Y[:, PT:NT], op=ALU.mult))
                self.proj_split("w_in_t", (l, TI_BG + c), uT, tu, ev_bg)

    def ssm_coefs(self, src, F, st, tsrc):
        nc, f = self.nc, self.f
        import math
        names = ["step", "decay", "th", "x", "m", "sin", "cos", "abre", "abim", "rden", "nr", "core", "coim", "t1"]
        T = {n: f.sbuf("sc_" + n, [128, F], F32, st) for n in names}
        tk = f.tok("ssmcoef")
        are, aim, lst = src[:, :, 0], src[:, :, 1], src[:, :, 2]
        V, A = nc.vector, nc.scalar

        def dv(fn):
            f.op(f.dve, [tk, tsrc], [tk], fn)

        def ac(fn):
            f.op(f.act, [tk, tsrc], [tk], fn)
        ac(lambda: A.activation(out=T["step"][:], in_=lst, func=AF.Exp))
        dv(lambda: V.tensor_tensor(out=T["t1"][:], in0=T["step"][:], in1=are, op=ALU.mult))
        ac(lambda: A.activation(out=T["decay"][:], in_=T["t1"][:], func=AF.Exp))
        dv(lambda: V.tensor_tensor(out=T["th"][:], in0=T["step"][:], in1=aim, op=ALU.mult))
        for name, shift in (("sin", 0.0), ("cos", math.pi / 2)):
            dv(lambda: V.tensor_scalar(out=T["x"][:], in0=T["th"][:], scalar1=shift, scalar2=None, op0=ALU.add))
            for k in (1, 3, 5, 7, 9):
                dv(lambda: V.tensor_scalar(out=T["m"][:], in0=T["th"][:], scalar1=k * math.pi - shift, scalar2=-2 * math.pi,
                                           op0=ALU.is_gt, op1=ALU.mult))
                dv(lambda: V.tensor_tensor(out=T["x"][:], in0=T["x"][:], in1=T["m"][:], op=ALU.add))
            ac(lambda: A.activation(out=T[name][:], in_=T["x"][:], func=AF.Sin))
        dv(lambda: V.tensor_tensor(out=T["abre"][:], in0=T["decay"][:], in1=T["cos"][:], op=ALU.mult))
        dv(lambda: V.tensor_tensor(out=T["abim"][:], in0=T["decay"][:], in1=T["sin"][:], op=ALU.mult))
        dv(lambda: V.tensor_tensor(out=T["t1"][:], in0=are, in1=are, op=ALU.mult))
        dv(lambda: V.tensor_tensor(out=T["m"][:], in0=aim, in1=aim, op=ALU.mult))
        dv(lambda: V.tensor_tensor(out=T["t1"][:], in0=T["t1"][:], in1=T["m"][:], op=ALU.add))
        dv(lambda: V.reciprocal(out=T["rden"][:], in_=T["t1"][:]))
        dv(lambda: V.tensor_scalar(out=T["nr"][:], in0=T["abre"][:], scalar1=-1.0, scalar2=None, op0=ALU.add))
        dv(lambda: V.tensor_tensor(out=T["t1"][:], in0=T["nr"][:], in1=are, op=ALU.mult))
        dv(lambda: V.tensor_tensor(out=T["m"][:], in0=T["abim"][:], in1=aim, op=ALU.mult))
        dv(lambda: V.tensor_tensor(out=T["t1"][:], in0=T["t1"][:], in1=T["m"][:], op=ALU.add))
        dv(lambda: V.tensor_tensor(out=T["core"][:], in0=T["t1"][:], in1=T["rden"][:], op=ALU.mult))
        dv(lambda: V.tensor_tensor(out=T["t1"][:], in0=T["abim"][:], in1=are, op=ALU.mult))
        dv(lambda: V.tensor_tensor(out=T["m"][:], in0=T["nr"][:], in1=aim, op=ALU.mult))
        dv(lambda: V.tensor_tensor(out=T["t1"][:], in0=T["t1"][:], in1=T["m"][:], op=ALU.subtract))
        dv(lambda: V.tensor_tensor(out=T["coim"][:], in0=T["t1"][:], in1=T["rden"][:], op=ALU.mult))
        return T, tk

    def mixer_ssm(self, l, br, tbr):
        nc, f, cfg = self.nc, self.f, self.cfg
        V, A = nc.vector, nc.scalar
        with f.scope() as st:
            xsf = f.sbuf("xsf", [128, 4, NT], BF16, st)
            txs = f.tok("xsf")
            with f.scope() as st2:
                uT = f.sbuf("uT", [128, 16, NT], BF16, st2)
                tu = [f.tok("u%d" % i) for i in range(16)]
                self.apply_norm(uT, tu, (l * 4 + 1) * 16)
                for c in range(4):
                    def ev(ci, c0, cn, bk, tb, c=c):
                        f.op(f.act, [tb], [txs], lambda: A.copy(out=xsf[:, c, c0:c0 + cn], in_=bk[:, 0:cn]))
                    self.proj("w_in_t", (l, TI_XS + c), uT, tu, ev)
            Bl = [f.sbuf("Bl", [128, 16, 128], BF16, st) for _ in range(2)]
            Cd = [f.sbuf("Cd", [128, 16, 128], BF16, st) for _ in range(2)]
            tBl, tCd = f.tok("Bl"), f.tok("Cd")
            ab = f.sbuf("ssab", [128, 16, 2], F32, st)
            pw = f.sbuf("sspw", [128, 16, 10, 3], F32, st)
            s0 = f.sbuf("sss0", [128, 16, 16, 2], F32, st)
            mid = f.sbuf("ssmid", [128, 16, 2], F32, st)
            endp = f.sbuf("ssendp", [128, 16, 2], F32, st)
            ends = f.sbuf("ssends", [128, 16, 16, 2], F32, st)
            tab, tpw, ts0, tmid, tendp, tends = [f.tok(n) for n in ("ssab", "sspw", "sss0", "ssmid", "ssendp", "ssends")]
            with f.scope() as st2:
                sm_d = self.din("ssm_sm", [L, 128, 16, 3])
                cm_d = self.din("ssm_cm", [L, 128, 256, 3])
                bcm_d = self.din("ssm_bcm", [L, 128, 256, 2])
                csm_d = self.din("ssm_csm", [L, 128, 16, 16, 2])
                s0_d = self.din("st_ssm", [L, 128, 16, 16, 2])
                sm = f.sbuf("sm", [128, 16, 3], F32, st2)
                cm = f.sbuf("cm", [128, 256, 3], F32, st2)
                bcm = f.sbuf("bcm", [128, 256, 2], F32, st2)
                csm = f.sbuf("csm", [128, 16, 16, 2], F32, st2)
                bb = f.sbuf("bb", [128, 256, 2], F32, st2)
                tsm, tcm, tbcm, tcsm, tbb = [f.tok(n) for n in ("sm", "cm", "bcm", "csm", "bb")]
                f.dma(f.sp, [], [tsm], lambda: nc.sync.dma_start(out=sm[:], in_=sm_d[l]))
                f.dma(f.sp, [], [tcm], lambda: nc.sync.dma_start(out=cm[:], in_=cm_d[l]))
                f.dma(f.sp, [], [tbcm], lambda: nc.sync.dma_start(out=bcm[:], in_=bcm_d[l]))
                f.dma(f.sp, [], [tcsm], lambda: nc.sync.dma_start(out=csm[:], in_=csm_d[l]))
                f.dma(f.sp, [], [ts0], lambda: nc.sync.dma_start(out=s0[:], in_=s0_d[l]))
                Ts, tks = self.ssm_coefs(sm[:], 16, st2, tsm)
                f.op(f.dve, [tks], [tab], lambda: V.tensor_copy(out=ab[:, :, 0], in_=Ts["abre"][:]))
                f.op(f.dve, [tks], [tab], lambda: V.tensor_copy(out=ab[:, :, 1], in_=Ts["abim"][:]))
                f.op(f.dve, [tab], [tpw], lambda: V.tensor_copy(out=pw[:, :, 0, 0:2], in_=ab[:]))
                t1 = f.sbuf("pwt", [128, 16, 2], F32, st2)
                tt1 = f.tok("pwt")
                for k in range(1, 10):
                    pr, pi = pw[:, :, k - 1, 0], pw[:, :, k - 1, 1]
                    f.op(f.dve, [tpw], [tt1], lambda: V.tensor_tensor(out=t1[:, :, 0], in0=pr, in1=pr, op=ALU.mult))
                    f.op(f.dve, [tpw], [tt1], lambda: V.tensor_tensor(out=t1[:, :, 1], in0=pi, in1=pi, op=ALU.mult))
                    f.op(f.dve, [tt1], [tpw], lambda: V.tensor_tensor(out=pw[:, :, k, 0], in0=t1[:, :, 0], in1=t1[:, :, 1], op=ALU.subtract))
                    f.op(f.dve, [tpw], [tt1], lambda: V.tensor_tensor(out=t1[:, :, 0], in0=pr, in1=pi, op=ALU.mult))
                    f.op(f.dve, [tt1], [tpw], lambda: V.tensor_scalar(out=pw[:, :, k, 1], in0=t1[:, :, 0], scalar1=2.0, scalar2=None, op0=ALU.mult))
                f.op(f.dve, [tpw], [tpw], lambda: V.tensor_scalar(out=pw[:, :, :, 2], in0=pw[:, :, :, 1], scalar1=-1.0, scalar2=None, op0=ALU.mult))
                Tc, tkc = self.ssm_coefs(cm[:], 256, st2, tcm)
                tA_ = f.sbuf("bbt", [128, 256], F32, st2)
                ttA = f.tok("bbt")
                bre, bim = bcm[:, :, 0], bcm[:, :, 1]
                f.op(f.dve, [tkc, tbcm], [tbb], lambda: V.tensor_tensor(out=bb[:, :, 0], in0=Tc["core"][:], in1=bre, op=ALU.mult))
                f.op(f.dve, [tkc, tbcm], [ttA], lambda: V.tensor_tensor(out=tA_[:], in0=Tc["coim"][:], in1=bim, op=ALU.mult))
                f.op(f.dve, [ttA, tbb], [tbb], lambda: V.tensor_tensor(out=bb[:, :, 0], in0=bb[:, :, 0], in1=tA_[:], op=ALU.subtract))
                f.op(f.dve, [tkc, tbcm], [tbb], lambda: V.tensor_tensor(out=bb[:, :, 1], in0=Tc["core"][:], in1=bim, op=ALU.mult))
                f.op(f.dve, [tkc, tbcm], [ttA], lambda: V.tensor_tensor(out=tA_[:], in0=Tc["coim"][:], in1=bre, op=ALU.mult))
                f.op(f.dve, [ttA, tbb], [tbb], lambda: V.tensor_tensor(out=bb[:, :, 1], in0=bb[:, :, 1], in1=tA_[:], op=ALU.add))
                for ri in range(2):
                    f.op(f.dve, [], [tBl], lambda: V.memset(Bl[ri][:], 0.0))
                    f.op(f.dve, [], [tCd], lambda: V.memset(Cd[ri][:], 0.0))
                for j in range(16):
                    base, c = (j % 4) * 32, j // 4
                    for ri in range(2):
                        in0 = self.cust(bb[base:base + 32, c * 64, ri], [(0, 2), (2, 64)])
                        in1 = self.bmask[base:base + 32, :].rearrange("p (a b) -> p a b", b=64)
                        f.op(f.dve, [tbb, self.tcb2], [tBl], lambda: V.tensor_tensor(
                            out=Bl[ri][base:base + 32, j, :].rearrange("p (a b) -> p a b", b=64), in0=in0, in1=in1, op=ALU.mult))
                        a0 = ((2 * j) % 8) * 16
                        for hf in range(2):
                            f.op(f.dve, [tcsm], [tCd], lambda: V.tensor_scalar(
                                out=Cd[ri][hf * 64:(hf + 1) * 64, j, a0 + 16 * hf:a0 + 16 * hf + 16],
                                in0=csm[hf * 64:(hf + 1) * 64, j, :, ri], scalar1=(1.0 if ri == 0 else -1.0), scalar2=None, op0=ALU.mult))
            with f.scope() as st2:
                H = [f.sbuf("ssH", [128, NT], F32, st2) for _ in range(4)]
                tH = [f.tok("ssH%d" % i) for i in range(4)]
                hb = [f.sbuf("sshb", [128, NT], BF16, st2) for _ in range(2)]
                thb = [f.tok("sshb%d" % i) for i in range(2)]
                z = f.sbuf("ssz", [128, 4, NT], BF16, st2)
                tz = f.tok("ssz")
                xsb, txsb = None, txs
                yv, y2, tyv, ty2 = H[2], H[3], tH[2], tH[3]

                def S3(buf, a, b):
                    return buf[:, PT:NT].rearrange("p (s q) -> p s q", q=8)[:, :, a:b]

                self._ssm_ctx = dict(H=H, tH=tH, hb=hb, thb=thb, xsb=xsb, txsb=txsb, yv=yv, y2=y2, tyv=tyv, ty2=ty2, S3=S3,
                                     ab=ab, tab=tab, pw=pw, tpw=tpw, s0=s0, ts0=ts0, mid=mid, tmid=tmid, endp=endp, tendp=tendp,
                                     ends=ends, tends=tends, Bl=Bl, tBl=tBl, Cd=Cd, tCd=tCd, xsf=xsf, txs=txs, z=z, tz=tz)
                self.ssm_scans(l, br, tbr)

    def ssm_scans(self, l, br, tbr):
        nc, f = self.nc, self.f
        V, A = nc.vector, nc.scalar
        c_ = self._ssm_ctx
        H, tH, hb, thb, xsb, txsb, yv, y2, tyv, ty2, S3 = [c_[k] for k in ("H", "tH", "hb", "thb", "xsb", "txsb", "yv", "y2", "tyv", "ty2", "S3")]
        ab, tab, pw, tpw, s0, ts0, mid, tmid, endp, tendp, ends, tends = [c_[k] for k in
            ("ab", "tab", "pw", "tpw", "s0", "ts0", "mid", "tmid", "endp", "tendp", "ends", "tends")]
        Bl, tBl, Cd, tCd, xsf, txs, z, tz = [c_[k] for k in ("Bl", "tBl", "Cd", "tCd", "xsf", "txs", "z", "tz")]
        ybk = [(self.banks[i], self.tbank[i]) for i in range(3)]

        def cadd(dre, dim, tdre, tdim, sre, sim, tsrc, pr, pi, npi, are_in1, aim_in1):
            f.op(f.dve, tsrc + [tpw, tab], [tdre], lambda: V.scalar_tensor_tensor(out=dre, in0=sre, scalar=pr, in1=are_in1, op0=ALU.mult, op1=ALU.add))
            f.op(f.dve, tsrc + [tpw, tab], [tdre], lambda: V.scalar_tensor_tensor(out=dre, in0=sim, scalar=npi, in1=dre, op0=ALU.mult, op1=ALU.add))
            f.op(f.dve, tsrc + [tpw, tab], [tdim], lambda: V.scalar_tensor_tensor(out=dim, in0=sim, scalar=pr, in1=aim_in1, op0=ALU.mult, op1=ALU.add))
            f.op(f.dve, tsrc + [tpw, tab], [tdim], lambda: V.scalar_tensor_tensor(out=dim, in0=sre, scalar=pi, in1=dim, op0=ALU.mult, op1=ALU.add))

        def scan(j, pass2):
            for ri in range(2):
                for (c0, cn) in CH:
                    bk, tb = self.bank(3, 8)
                    self.mm(bk[:, 0:cn], Bl[ri][:, j, :], xsf[:, j // 4, c0:c0 + cn], True, True, [tBl, txs], [tb])
                    f.op(f.act, [tb], [tH[ri]], lambda: A.copy(out=H[ri][:, c0:c0 + cn], in_=bk[:, 0:cn]))
            pr0, pi0, npi0 = pw[:, j, 0, 0:1], pw[:, j, 0, 1:2], pw[:, j, 0, 2:3]
            if pass2:
                cadd(H[0][:, 0:1], H[1][:, 0:1], tH[0], tH[1], mid[:, j, 0:1], mid[:, j, 1:2], [tmid], pr0, pi0, npi0,
                     H[0][:, 0:1], H[1][:, 0:1])
                cadd(S3(H[0], 0, 1), S3(H[1], 0, 1), tH[0], tH[1], s0[:, j, :, 0:1], s0[:, j, :, 1:2], [ts0], pr0, pi0, npi0,
                     S3(H[0], 0, 1), S3(H[1], 0, 1))
            cur, oth = (0, 1), (2, 3)
            for k in range(10):
                sh = 1 << k
                pr, pi, npi = pw[:, j, k, 0:1], pw[:, j, k, 1:2], pw[:, j, k, 2:3]
                sre, sim, dre, dim = H[cur[0]], H[cur[1]], H[oth[0]], H[oth[1]]
                tsrc = [tH[cur[0]], tH[cur[1]]]
                cadd(dre[:, sh:PT], dim[:, sh:PT], tH[oth[0]], tH[oth[1]], sre[:, 0:PT - sh], sim[:, 0:PT - sh], tsrc, pr, pi, npi,
                     sre[:, sh:PT], sim[:, sh:PT])
                f.op(f.act, [tH[cur[0]]], [tH[oth[0]]], lambda: A.copy(out=dre[:, 0:sh], in_=sre[:, 0:sh]))
                f.op(f.act, [tH[cur[1]]], [tH[oth[1]]], lambda: A.copy(out=dim[:, 0:sh], in_=sim[:, 0:sh]))
                if pass2 and k < 3:
                    cadd(S3(dre, sh, 8), S3(dim, sh, 8), tH[oth[0]], tH[oth[1]], S3(sre, 0, 8 - sh), S3(sim, 0, 8 - sh), tsrc, pr, pi, npi,
                         S3(sre, sh, 8), S3(sim, sh, 8))
                    f.op(f.act, [tH[cur[0]]], [tH[oth[0]]], lambda: A.copy(out=S3(dre, 0, sh), in_=S3(sre, 0, sh)))
                    f.op(f.act, [tH[cur[1]]], [tH[oth[1]]], lambda: A.copy(out=S3(dim, 0, sh), in_=S3(sim, 0, sh)))
                cur, oth = oth, cur
            for ri in range(2):
                f.op(f.act, [tH[ri]], [tendp], lambda: A.copy(out=endp[:, j, ri:ri + 1], in_=H[ri][:, PT - 1:PT]))
            if not pass2:
                return
            for ri in range(2):
                f.op(f.act, [tH[2 + ri]], [tends], lambda: A.copy(out=ends[:, j, :, ri:ri + 1], in_=S3(H[2 + ri], 7, 8)))
                f.op(f.act, [tH[ri]], [thb[ri]], lambda: A.copy(out=hb[ri][:, 0:PT], in_=H[ri][:, 0:PT]))
                f.op(f.dve, [tH[2 + ri]], [thb[ri]], lambda: V.tensor_copy(out=hb[ri][:, PT:NT], in_=H[2 + ri][:, PT:NT]))
            for ri in range(2):
                for ci, (c0, cn) in enumerate(CH):
                    self.mm(ybk[ci][0][:, 0:cn], Cd[ri][:, j, :], hb[ri][:, c0:c0 + cn],
                            (j % 4 == 0 and ri == 0), (j % 4 == 3 and ri == 1), [tCd, thb[ri]], [ybk[ci][1]])

        def load_xsb(c):
            pass
        for j in range(16):
            if j % 4 == 0:
                load_xsb(j // 4)
            scan(j, False)
        halo, thalo, hown, thown = self._halo
        p_d = self.dint("x2_pay_%d" % l, [128, 100], F32)
        g_d = self.dint("x2_gat_%d" % l, [256, 100], F32)
        tpd, tgd = f.tok("x2pay"), f.tok("x2gat")
        f.dma(f.sp, [tendp], [tpd], lambda: nc.sync.dma_start(out=p_d[:, 0:32], in_=endp[:].rearrange("p a b -> p (a b)")), main=tpd)
        f.dma(f.sp, [thown], [tpd], lambda: nc.sync.dma_start(out=p_d[:, 32:100], in_=hown[:]), main=tpd)
        f.dma(f.pool, [tpd], [tgd], lambda: nc.gpsimd.collective_compute(
            "AllGather", ALU.bypass, replica_groups=self.cfg["groups"], ins=[p_d], outs=[g_d]), main=tgd, inc=1)
        f.dma(f.sp, [tgd], [tmid], lambda: nc.sync.dma_start(out=mid[:].rearrange("p a b -> p (a b)"), in_=g_d[0:128, 0:32]), main=tmid)
        f.dma(f.sp, [tgd], [thalo], lambda: nc.sync.dma_start(out=halo[:], in_=g_d[0:128, 32:100]), main=thalo)
        f.op(f.dve, [tmid, self.tcb], [tmid], lambda: V.tensor_scalar(out=mid[:], in0=mid[:], scalar1=self.cbias[:, 257:258], scalar2=None, op0=ALU.mult))
        f.op(f.dve, [thalo, self.tcb], [thalo], lambda: V.tensor_scalar(out=halo[:], in0=halo[:], scalar1=self.cbias[:, 257:258], scalar2=None, op0=ALU.mult))
        for j in range(16):
            c = j // 4
            if j % 4 == 0:
                load_xsb(c)
            scan(j, True)
            if j % 4 == 3:
                for ci, (c0, cn) in enumerate(CH):
                    f.op(f.dve, [ybk[ci][1], txs, self.tcb2], [tyv], lambda: V.scalar_tensor_tensor(
                        out=yv[:, c0:c0 + cn], in0=xsf[:, c, c0:c0 + cn], scalar=self.ssd[:, l * 4 + c:l * 4 + c + 1],
                        in1=ybk[ci][0][:, 0:cn], op0=ALU.mult, op1=ALU.add))
                f.op(f.dve, [tyv], [ty2], lambda: V.tensor_tensor(out=y2[:], in0=yv[:], in1=yv[:], op=ALU.mult))
                f.op(f.dve, [ty2], [ty2], lambda: V.tensor_scalar(out=y2[:], in0=y2[:], scalar1=0.044715, scalar2=1.0, op0=ALU.mult, op1=ALU.add))
                f.op(f.dve, [ty2, tyv], [ty2], lambda: V.tensor_tensor(out=y2[:], in0=y2[:], in1=yv[:], op=ALU.mult))
                f.op(f.act, [ty2], [ty2], lambda: A.activation(out=y2[:], in_=y2[:], func=AF.Sigmoid, scale=1.5957691216057308))
                f.op(f.dve, [ty2, tyv], [tz], lambda: V.tensor_tensor(out=z[:, c, :], in0=y2[:], in1=yv[:], op=ALU.mult))
        osp, tosp = self.dout("o_ssm_p", [L, 128, 32])
        oss, toss = self.dout("o_ssm_s", [L, 128, 512])
        f.dma(f.sp, [tendp], [tosp], lambda: nc.sync.dma_start(out=osp[l], in_=endp[:].rearrange("p a b -> p (a b)")), main=tendp)
        f.dma(f.sp, [tends], [toss], lambda: nc.sync.dma_start(out=oss[l], in_=ends[:].rearrange("p a b c -> p (a b c)")), main=tends)
        for e in range(4):
            wgl, twgl = self.wget("w_glu", (l, e), 512)
            for (c0, cn) in CH:
                bk, tb = self.bank(3, 8)
                for kt in range(4):
                    self.mm(bk[:, 0:cn], wgl[:, kt * 128:(kt + 1) * 128], z[:, kt, c0:c0 + cn], kt == 0, kt == 3, [twgl, tz], [tb])
                f.op(f.act, [tb, self.tcb2], [tyv], lambda: A.activation(out=yv[:, c0:c0 + cn], in_=bk[:, 0:cn], func=AF.Sigmoid,
                                                                         bias=self.glub[:, l * 4 + e:l * 4 + e + 1], scale=1.0))
                f.op(f.dve, [tyv, tz], [tbr[8 + e]], lambda: V.tensor_tensor(out=br[:, 8 + e, c0:c0 + cn], in0=yv[:, c0:c0 + cn], in1=z[:, e, c0:c0 + cn], op=ALU.mult))

    def gate_phase(self, l, br, tbr, ycT, tyc):
        nc, f = self.nc, self.f
        V, A = nc.vector, nc.scalar
        with f.scope() as st:
            uT = f.sbuf("uT", [128, 16, NT], BF16, st)
            tu = [f.tok("u%d" % i) for i in range(16)]
            self.apply_norm(uT, tu, (l * 4 + 1) * 16)
            mg = f.sbuf("mg", [128, 8, NT], BF16, st)
            tmg = [f.tok("mg%d" % i) for i in range(8)]
            m32 = f.sbuf("m32", [128, NT], F32, st)
            sg = f.sbuf("sg", [128, NT], F32, st)
            tm32, tsg = f.tok("m32"), f.tok("sg")

            def branch_rhs(i, kt, c0, cn):
                if i == 0:
                    return br[:, kt, c0:c0 + cn], tbr[kt]
                if i == 1:
                    return br[:, 4 + kt, c0:c0 + cn], tbr[4 + kt]
                if i == 2:
                    return ycT[:, kt, c0:c0 + cn], tyc
                return br[:, 8 + kt, c0:c0 + cn], tbr[8 + kt]
            for half in range(2):
                for dd in range(8):
                    d = half * 8 + dd
                    for i in range(4):
                        wg, twg = self.wget("w_gate", (l, i, d), 2048)
                        gb = [self.bank() for _ in CH]
                        for kt in range(16):
                            for ci, (c0, cn) in enumerate(CH):
                                self.mm(gb[ci][0][:, 0:cn], wg[:, kt * 128:(kt + 1) * 128], uT[:, kt, c0:c0 + cn],
                                        kt == 0, kt == 15, [twg, tu[kt]], [gb[ci][1]])
                        wb, twb = self.wget("w_branch", (l, i, d), 512)
                        ob = [self.bank() for _ in CH]
                        for kt in range(4):
                            for ci, (c0, cn) in enumerate(CH):
                                rhs, trhs = branch_rhs(i, kt, c0, cn)
                                self.mm(ob[ci][0][:, 0:cn], wb[:, kt * 128:(kt + 1) * 128], rhs, kt == 0, kt == 3, [twb, trhs], [ob[ci][1]])
                        for ci, (c0, cn) in enumerate(CH):
                            f.op(f.act, [gb[ci][1], self.tcb2], [tsg], lambda: A.activation(
                                out=sg[:, c0:c0 + cn], in_=gb[ci][0][:, 0:cn], func=AF.Sigmoid,
                                bias=self.bgate[:, (l * 4 + i) * 16 + d:(l * 4 + i) * 16 + d + 1], scale=1.0))
                            if i == 0:
                                f.op(f.dve, [tsg, ob[ci][1]], [tm32], lambda: V.tensor_tensor(out=m32[:, c0:c0 + cn], in0=ob[ci][0][:, 0:cn], in1=sg[:, c0:c0 + cn], op=ALU.mult))
                            else:
                                f.op(f.dve, [tsg, ob[ci][1]], [tsg], lambda: V.tensor_tensor(out=sg[:, c0:c0 + cn], in0=ob[ci][0][:, 0:cn], in1=sg[:, c0:c0 + cn], op=ALU.mult))
                                if i < 3:
                                    f.op(f.dve, [tsg, tm32], [tm32], lambda: V.tensor_tensor(out=m32[:, c0:c0 + cn], in0=m32[:, c0:c0 + cn], in1=sg[:, c0:c0 + cn], op=ALU.add))
                                else:
                                    f.op(f.dve, [tsg, tm32], [tmg[dd]], lambda: V.tensor_tensor(out=mg[:, dd, c0:c0 + cn], in0=m32[:, c0:c0 + cn], in1=sg[:, c0:c0 + cn], op=ALU.add))
                if l == 0 and half == 0:
                    self.dbg_dump("mg0", mg[:], tmg[7], [128, 8, NT], BF16)
                for e in range(16):
                    wo, two = self.wget("w_o", (l, half, e), 1024)
                    for (c0, cn) in CH:
                        bk, tb = self.bank()
                        for dd in range(8):
                            self.mm(bk[:, 0:cn], wo[:, dd * 128:(dd + 1) * 128], mg[:, dd, c0:c0 + cn], dd == 0, dd == 7, [two, tmg[dd]], [tb])
                        f.op(f.dve, [tb, self.th[e]], [self.th[e]], lambda: V.tensor_tensor(
                            out=self.hT[:, e, c0:c0 + cn], in0=bk[:, 0:cn], in1=self.hT[:, e, c0:c0 + cn], op=ALU.add))

    def cross_attention(self, l):
        nc, f = self.nc, self.f
        V, A = nc.vector, nc.scalar
        XS = 128 ** -0.5
        with f.scope() as st_x:
            qx = f.sbuf("qx", [128, 4, NT], BF16, st_x)
            ox = f.sbuf("ox", [128, 4, NT], BF16, st_x)
            tqx, tox = f.tok("qx"), f.tok("ox")
            kxb = f.sbuf("kxb", [128, 4, 256], BF16, st_x)
            vxb = f.sbuf("vxb", [128, 2, 512], BF16, st_x)
            tkx, tvx = f.tok("kxb"), f.tok("vxb")
            with f.scope() as st:
                uT = f.sbuf("uT", [128, 16, NT], BF16, st)
                tu = [f.tok("u%d" % i) for i in range(16)]
                self.rmsnorm(uT, tu, (l * 4 + 2) * 16, st)
                pf = f.sbuf("pfx", [128, NT], F32, st)
                tpf = f.tok("pfx")
                for hh in range(4):
                    def ev(ci, c0, cn, bk, tb):
                        f.op(f.act, [tb], [tpf], lambda: A.copy(out=pf[:, c0:c0 + cn], in_=bk[:, 0:cn]))
                    self.proj("w_xq", (l, hh), uT, tu, ev)
                    self.headnorm(pf[:], tpf, 128, self.xqg[:, l:l + 1], qx[:, hh, :], tqx, st, self.ones_bf)
                memT = self.din("memT", [16, 128, 256])
                mm_ = f.sbuf("memf", [128, 16, 256], F32, st)
                mu = f.sbuf("memu", [128, 16, 256], BF16, st)
                tmm, tmu = f.tok("memf"), f.tok("memu")
                for kt in range(16):
                    f.dma(f.sp, [], [tmm], lambda: nc.sync.dma_start(out=mm_[:, kt, :], in_=memT[kt]))
                msq = f.sbuf("msq", [128, 256], BF16, st)
                mrs = f.sbuf("mrs", [128, 256], F32, st)
                tmsq, tmrs = f.tok("msq"), f.tok("mrs")
                bk, tb = self.bank()
                for kt in range(16):
                    f.op(f.act, [tmm], [tmsq], lambda: A.activation(out=msq[:], in_=mm_[:, kt, :], func=AF.Square))
                    self.mm(bk[:, 0:256], self.ones_bf, msq[:], kt == 0, kt == 15, [tmsq, self.tc], [tb])
                f.op(f.act, [tb, self.teps], [tmrs], lambda: A.activation(out=mrs[:], in_=bk[:, 0:256], func=AF.Sqrt, bias=self.eps_t[:, 0:1], scale=1.0 / D))
                f.op(f.dve, [tmrs], [tmrs], lambda: V.reciprocal(out=mrs[:], in_=mrs[:]))
                for kt in range(16):
                    f.op(f.dve, [tmm, tmrs, self.tcb2], [tmu], lambda: V.scalar_tensor_tensor(
                        out=mu[:, kt, :], in0=mm_[:, kt, :], scalar=self.memg[:, l * 16 + kt:l * 16 + kt + 1], in1=mrs[:], op0=ALU.mult, op1=ALU.mult))
                kxf = f.sbuf("kxf", [128, 4, 256], F32, st)
                vxf = f.sbuf("vxf", [128, 2, 512], F32, st)
                tkf, tvf = f.tok("kxf"), f.tok("vxf")
                ksq = f.sbuf("kxsq", [128, 256], BF16, st)
                tksq = f.tok("kxsq")
                for hh in range(4):
                    wk, twk = self.wget("w_xk", (l, hh), 2048)
                    bk, tb = self.bank()
                    for kt in range(16):
                        self.mm(bk[:, 0:256], wk[:, kt * 128:(kt + 1) * 128], mu[:, kt, :], kt == 0, kt == 15, [twk, tmu], [tb])
                    f.op(f.act, [tb], [tkf], lambda: A.copy(out=kxf[:, hh, :], in_=bk[:, 0:256]))
                    f.op(f.act, [tkf], [tksq], lambda: A.activation(out=ksq[:], in_=kxf[:, hh, :], func=AF.Square))
                    bk2, tb2 = self.bank()
                    self.mm(bk2[:, 0:256], self.ones_bf, ksq[:], True, True, [tksq, self.tc], [tb2])
                    f.op(f.act, [tb2, self.teps], [tmrs], lambda: A.activation(out=mrs[:], in_=bk2[:, 0:256], func=AF.Sqrt, bias=self.eps_t[:, 0:1], scale=1.0 / 128))
                    f.op(f.dve, [tmrs], [tmrs], lambda: V.reciprocal(out=mrs[:], in_=mrs[:]))
                    f.op(f.dve, [tmrs, tkf, self.tcb2], [tkf], lambda: V.scalar_tensor_tensor(
                        out=kxf[:, hh, :], in0=kxf[:, hh, :], scalar=self.xkg[:, l:l + 1], in1=mrs[:], op0=ALU.mult, op1=ALU.mult))
                f.op(f.dve, [tkf], [tkx], lambda: V.tensor_copy(out=kxb[:], in_=kxf[:]))
                for mt in range(2):
                    bk, tb = self.bank()
                    for q4 in range(4):
                        wv, twv = self.wget("w_xv_rhs", (l, q4), 2048)
                        for k4 in range(4):
                            kt = q4 * 4 + k4
                            self.mm(bk[:, 0:512], mu[:, kt, mt * 128:(mt + 1) * 128], wv[:, k4 * 512:(k4 + 1) * 512], kt == 0, kt == 15, [twv, tmu], [tb])
                    f.op(f.act, [tb], [tvf], lambda: A.copy(out=vxf[:, mt, :], in_=bk[:, 0:512]))
                f.op(f.dve, [tvf], [tvx], lambda: V.tensor_copy(out=vxb[:], in_=vxf[:]))
                omk, tomk = self.dout("o_mk", [L, 128, 4, 256])
                omv, tomv = self.dout("o_mv", [L, 128, 2, 512])
                f.dma(f.sp, [tkf], [tomk], lambda: nc.sync.dma_start(out=omk[l], in_=kxf[:]), main=tkf)
                f.dma(f.sp, [tvf], [tomv], lambda: nc.sync.dma_start(out=omv[l], in_=vxf[:]), main=tvf)
            with f.scope() as st:
                E = [f.sbuf("xE", [128, 2, 512], BF16, st) for _ in range(2)]
                tE = [f.tok("xE0"), f.tok("xE1")]
                rd = f.sbuf("xrd", [128, 512], F32, st)
                trd = f.tok("xrd")
                ei = 0
                for hh in range(4):
                    for qc in range(2):
                        q0 = qc * 512
                        e, te = E[ei % 2], tE[ei % 2]
                        ei += 1
                        for mt in range(2):
                            bk, tb = self.bank()
                            self.mm(bk[:, 0:512], kxb[:, hh, mt * 128:(mt + 1) * 128], qx[:, hh, q0:q0 + 512], True, True, [tkx, tqx], [tb])
                            f.op(f.act, [tb], [te], lambda: A.activation(out=e[:, mt, :], in_=bk[:, 0:512], func=AF.Exp, scale=XS))
                        bo, tbo = self.bank()
                        bd_, tbd = self.bank()
                        for mt in range(2):
                            self.mm(bo[:, 0:512], vxb[:, mt, hh * 128:(hh + 1) * 128], e[:, mt, :], mt == 0, mt == 1, [tvx, te], [tbo])
                            self.mm(bd_[:, 0:512], self.ones_bf, e[:, mt, :], mt == 0, mt == 1, [self.tc, te], [tbd])
                        f.op(f.dve, [tbd], [trd], lambda: V.reciprocal(out=rd[:], in_=bd_[:, 0:512]))
                        f.op(f.dve, [tbo, trd], [tox], lambda: V.tensor_tensor(out=ox[:, hh, q0:q0 + 512], in0=bo[:, 0:512], in1=rd[:], op=ALU.mult))
                ck_d = self.din("cmem_k", [L, NSEQ, 2, 128, 512])
                cv_d = self.din("cmem_v", [L, NSEQ, 2, 128, 512])
                MKs = [f.sbuf("xMK", [128, 2, 512], F32, st) for _ in range(2)]
                MVs = [f.sbuf("xMV", [128, 2, 512], F32, st) for _ in range(2)]
                tMKs = [f.tok("xMK0"), f.tok("xMK1")]
                tMVs = [f.tok("xMV0"), f.tok("xMV1")]
                kts_ = [f.sbuf("xkT", [128, 4, 256], BF16, st) for _ in range(2)]
                vbs = [f.sbuf("xvb", [128, 2, 512], BF16, st) for _ in range(2)]
                tkts = [f.tok("xkT0"), f.tok("xkT1")]
                tvbs = [f.tok("xvb0"), f.tok("xvb1")]
                es = [f.sbuf("xes", [128, 64], BF16, st) for _ in range(2)]
                tes = [f.tok("xes0"), f.tok("xes1")]
                rds = f.sbuf("xrds", [128, 32], F32, st)
                trds = f.tok("xrds")
                for s_ in range(NSEQ):
                    b2 = s_ % 2
                    mk, mv, tmk, tmv = MKs[b2], MVs[b2], tMKs[b2], tMVs[b2]
                    f.dma(f.sp, [], [tmk], lambda: nc.sync.dma_start(out=mk[:], in_=ck_d[l, s_].rearrange("t p d -> p t d")))
                    f.dma(f.sp, [], [tmv], lambda: nc.sync.dma_start(out=mv[:], in_=cv_d[l, s_].rearrange("t p d -> p t d")))
                    kt_, vb_, tkt_, tvb_ = kts_[b2], vbs[b2], tkts[b2], tvbs[b2]
                    for mt in range(2):
                        bk, tb = self.bank()
                        for hh in range(4):
                            f.op(f.pe, [tmk, self.tc], [tb], lambda: nc.tensor.transpose(out=bk[:, hh * 128:(hh + 1) * 128], in_=mk[:, mt, hh * 128:(hh + 1) * 128], identity=self.ident_f))
                        f.op(f.act, [tb], [tkt_], lambda: A.copy(out=kt_[:, :, mt * 128:(mt + 1) * 128], in_=bk[:, 0:512].rearrange("p (h m) -> p h m", h=4)))
                    f.op(f.dve, [tmv], [tvb_], lambda: V.tensor_copy(out=vb_[:], in_=mv[:]))
                    c0 = PT + s_ * 8
                    e, te = es[b2], tes[b2]
                    bk, tb = self.bank()
                    for hh in range(4):
                        for mt in range(2):
                            self.mm(bk[:, (hh * 2 + mt) * 8:(hh * 2 + mt + 1) * 8], kt_[:, hh, mt * 128:(mt + 1) * 128], qx[:, hh, c0:c0 + 8], True, True, [tkt_, tqx], [tb])
                    f.op(f.act, [tb], [te], lambda: A.activation(out=e[:], in_=bk[:, 0:64], func=AF.Exp, scale=XS))
                    bo, tbo = self.bank()
                    for hh in range(4):
                        for mt in range(2):
                            self.mm(bo[:, hh * 8:(hh + 1) * 8], vb_[:, mt, hh * 128:(hh + 1) * 128], e[:, (hh * 2 + mt) * 8:(hh * 2 + mt + 1) * 8], mt == 0, mt == 1, [tvb_, te], [tbo], skip_group_check=True)
                        for mt in range(2):
                            self.mm(bo[:, 32 + hh * 8:32 + (hh + 1) * 8], self.ones_bf, e[:, (hh * 2 + mt) * 8:(hh * 2 + mt + 1) * 8], mt == 0, mt == 1, [self.tc, te], [tbo], skip_group_check=True)
                    f.op(f.dve, [tbo], [trds], lambda: V.reciprocal(out=rds[:], in_=bo[:, 32:64]))
                    f.op(f.dve, [tbo, trds], [tox], lambda: V.tensor_tensor(
                        out=self.cust(ox[:, 0, c0:c0 + 8], [(NT, 4), (1, 8)]), in0=bo[:, 0:32].rearrange("p (h q) -> p h q", q=8),
                        in1=rds[:].rearrange("p (h q) -> p h q", q=8), op=ALU.mult))
            for e_ in range(16):
                wxo, twxo = self.wget("w_xo", (l, e_), 512)
                for (c0, cn) in CH:
                    bk, tb = self.bank()
                    for hh in range(4):
                        self.mm(bk[:, 0:cn], wxo[:, hh * 128:(hh + 1) * 128], ox[:, hh, c0:c0 + cn], hh == 0, hh == 3, [twxo, tox], [tb])
                    f.op(f.dve, [tb, self.th[e_]], [self.th[e_]], lambda: V.tensor_tensor(
                        out=self.hT[:, e_, c0:c0 + cn], in0=bk[:, 0:cn], in1=self.hT[:, e_, c0:c0 + cn], op=ALU.add))

    def dint(self, name, shape, dtype=F32):
        if name not in self.dram:
            self.dram[name] = self.nc.dram_tensor(name, list(shape), dtype, kind="Internal").ap()
        return self.dram[name]

    def run(self):
        if "custom_run" in self.cfg:
            return self.cfg["custom_run"](self)
        self.setup()
        for l in range(L):
            if not self.layer(l):
                break
            if self.cfg.get("stop") == "full1":
                break
        if self.cfg.get("debug"):
            for kt in range(16):
                pass
        oy, toy = self.dout("o_y", [16, 128, NT])
        for kt in range(16):
            self.f.dma(self.f.sp, [self.th[kt]], [toy],
                       lambda: self.nc.sync.dma_start(out=oy[kt], in_=self.hT[:, kt, :]), main=self.th[kt])


def build_program(cfg):
    nc0 = bass.Bass("TRN2", target_bir_lowering=False)
    f0 = FW(nc0, dry=True)
    k0 = Kern(nc0, f0, None, cfg)
    k0.run()
    plan = k0.wkeys
    f0.close()
    nc = bass.Bass("TRN2", target_bir_lowering=False)
    f = FW(nc, dry=False)
    k = Kern(nc, f, plan, cfg)
    k.run()
    f.finish(k.outs)
    f.close()
    return nc, k


def tile_k(W):
    K, N = W.shape
    KT, NC = K // 128, N // 128
    return np.ascontiguousarray(W.reshape(KT, 128, NC, 128).transpose(2, 1, 0, 3).reshape(NC, 128, KT * 128))


def rhs_k(W):
    K, N = W.shape
    KT = K // 128
    return np.ascontiguousarray(W.reshape(KT, 128, N).transpose(1, 0, 2).reshape(128, KT * N))


def pad_cols(W, dup=False):
    K, n = W.shape
    out = np.zeros((K, 128), W.dtype)
    out[:, :n] = W
    if dup:
        out[:, 64:64 + n] = W
    return out


def prep_shared(inp):
    sh = {}
    f_in = inp["ffn_in"]
    sh["w_ffn_in"] = np.stack([np.stack([tile_k(f_in[l, w]) for w in range(2)]) for l in range(L)])
    f_out = inp["ffn_out"]
    sh["w_ffn_out"] = np.stack([np.stack([np.stack([tile_k(f_out[l, w, hh * 2048:(hh + 1) * 2048]) for hh in range(2)])
                                          for w in range(2)]) for l in range(L)])
    w_in = inp["w_in"]
    tiles = []
    for l in range(L):
        W = w_in[l]
        cols = []
        for base in (0, 512, 1024, 1536, 2048):
            cols += [W[:, base + i * 128: base + (i + 1) * 128] for i in range(4)]
        cols.append(pad_cols(W[:, 2560:2624], dup=True))
        cols += [W[:, 2688 + i * 128: 2688 + (i + 1) * 128] for i in range(4)]
        cols.append(pad_cols(W[:, 3200:3264], dup=True))
        cols += [W[:, 3272 + i * 128: 3272 + (i + 1) * 128] for i in range(4)]
        Wt = np.concatenate(cols, axis=1)
        tiles.append(tile_k(Wt))
    sh["w_in_t"] = np.stack(tiles)
    sh["w_v_rhs"] = np.stack([rhs_k(w_in[l][:, 2624:2688]) for l in range(L)])
    sh["w_iw_rhs"] = np.stack([rhs_k(w_in[l][:, 3264:3272]) for l in range(L)])
    for l in range(L):
        sh["cache_ik_%d" % l] = inp["cache_idx_k"][l].reshape(20480, 1024)
        sh["cache_k_%d" % l] = inp["cache_attn_k"][l].reshape(20480, 1024)
        sh["cache_v_%d" % l] = inp["cache_attn_v"][l].reshape(20480, 1024)
    sh["w_pool"] = np.ascontiguousarray(inp["pool_mix"])
    sh["w_glu"] = np.stack([tile_k(inp["ssm_glu_w"][l]) for l in range(L)])
    sh["w_gate"] = np.stack([np.stack([tile_k(inp["w_gate"][l][:, i * 2048:(i + 1) * 2048]) for i in range(4)]) for l in range(L)])
    sh["w_branch"] = np.stack([np.stack([tile_k(inp["w_branch"][l, i]) for i in range(4)]) for l in range(L)])
    sh["w_o"] = np.stack([np.stack([tile_k(inp["w_o"][l][hf * 1024:(hf + 1) * 1024]) for hf in range(2)]) for l in range(L)])
    sh["w_xq"] = np.stack([tile_k(inp["w_xq"][l]) for l in range(L)])
    sh["w_xk"] = np.stack([tile_k(inp["w_xk"][l]) for l in range(L)])
    sh["w_xv_rhs"] = np.stack([np.ascontiguousarray(inp["w_xv"][l].reshape(4, 4, 128, 512).transpose(0, 2, 1, 3).reshape(4, 128, 2048)) for l in range(L)])
    sh["w_xo"] = np.stack([tile_k(inp["w_xo"][l]) for l in range(L)])
    are, aim = inp["ssm_a_re"], inp["ssm_a_im"]
    lst = np.broadcast_to(inp["ssm_log_step"][:, :, None], are.shape)
    def sm(a):
        return a.reshape(L, 16, 2, 64).transpose(0, 2, 3, 1).reshape(L, 128, 16)
    def cmaj(a):
        x = a.reshape(L, 4, 8, 64).transpose(0, 2, 1, 3)
        x = np.broadcast_to(x[:, :, None], (L, 8, 16, 4, 64))
        return x.reshape(L, 128, 256)
    sh["ssm_sm"] = np.ascontiguousarray(np.stack([sm(are), sm(aim), sm(lst)], axis=-1))
    sh["ssm_cm"] = np.ascontiguousarray(np.stack([cmaj(are), cmaj(aim), cmaj(lst)], axis=-1))
    def bcm(b):
        return b.reshape(L, 4, 8, 64, 16).transpose(0, 2, 4, 1, 3).reshape(L, 128, 256)
    sh["ssm_bcm"] = np.ascontiguousarray(np.stack([bcm(inp["ssm_b_re"]), bcm(inp["ssm_b_im"])], axis=-1))
    def csm(c_):
        return c_.reshape(L, 16, 2, 16, 64).transpose(0, 2, 4, 1, 3).reshape(L, 128, 16, 16)
    sh["ssm_csm"] = np.ascontiguousarray(np.stack([csm(inp["ssm_c_re"]), csm(inp["ssm_c_im"])], axis=-1))
    c2 = np.zeros((128, 404), np.float32)
    def fm(v, nt):
        return v.reshape(nt, 128).T
    for l in range(L):
        c2[:, 64 + 4 * l:68 + 4 * l] = fm(inp["pool_scale"][l], 4)
        cw = inp["conv_w"][l]
        for c_ in range(4):
            for j in range(3):
                c2[:, 72 + (l * 4 + c_) * 3 + j] = cw[j, c_ * 128:(c_ + 1) * 128]
        c2[:, 96 + 4 * l:100 + 4 * l] = fm(inp["ssm_d"][l], 4)
        c2[:, 104 + 4 * l:108 + 4 * l] = fm(inp["ssm_glu_b"][l], 4)
        c2[:, 112 + l] = inp["xq_norm_g"][l]
        c2[:, 114 + l] = inp["xk_norm_g"][l]
        c2[:, 116 + 16 * l:132 + 16 * l] = fm(inp["mem_norm_g"][l], 16)
        for i in range(4):
            c2[:, 148 + (l * 4 + i) * 16:148 + (l * 4 + i + 1) * 16] = fm(inp["b_gate"][l][i * 2048:(i + 1) * 2048], 16)
    r_ = np.arange(128)
    c2[:, 276:404] = ((r_[:, None] % 32) // 16 == (r_[None, :] // 64)).astype(np.float32)
    sh["cst2_base"] = c2
    c = np.zeros((128, 512), np.float32)
    c[:, 0:128] = 1.0
    c[0:64, 128:192] = 1.0
    c[64:128, 192:256] = 1.0
    c[:, 256:384] = np.eye(128, dtype=np.float32)
    c[0:64, 384 + 64:384 + 128] = np.eye(64, dtype=np.float32)
    c[64:128, 384:384 + 64] = np.eye(64, dtype=np.float32)
    sh["consts_bf"] = c
    ng = inp["norm_g"]
    sh["gains"] = np.ascontiguousarray(ng.reshape(L, 4, 16, 128).transpose(3, 0, 1, 2).reshape(128, L * 4 * 16))
    hp = np.zeros((128, 8 * L), np.float32)
    for l in range(L):
        hp[:, 8 * l + 0] = np.tile(inp["k_norm_g"][l], 2)
        hp[:, 8 * l + 1] = np.tile(inp["q_norm_g"][l], 2)
    sh["headp"] = hp
    return sh


def prep_core(inp, c, sh=None):
    b, hf = c // 2, c % 2
    xp = inp["x_prompt"][b, hf * PT:(hf + 1) * PT]
    xs = inp["x_sample"][c * NSEQ:(c + 1) * NSEQ].reshape(NSEQ * 8, D)
    x = np.concatenate([xp, xs], axis=0)
    d = {"xT": np.ascontiguousarray(x.T.reshape(16, 128, NT))}
    cb = np.zeros((128, 259), np.float32)
    cb[:, 0] = 0.0 if hf == 1 else NEG
    qi = np.arange(128)
    cb[:, 1:129] = np.where(qi[None, :] <= qi[:, None], 0.0, NEG)
    sq, qq = qi // 8, qi % 8
    cb[:, 129:257] = np.where((sq[None, :] == sq[:, None]) & (qq[None, :] <= qq[:, None]), 0.0, NEG)
    cb[:, 257] = 1.0 if hf == 1 else 0.0
    cb[:, 258] = qi % 8
    d["cbias"] = cb
    c2 = sh["cst2_base"].copy()
    for gi in range(4):
        w = 2 ** (gi + 1)
        pos = np.arange(16)
        cnt = np.minimum(pos + 1, w) if hf == 0 else np.full(16, w)
        c2[:, gi * 16:(gi + 1) * 16] = (1.0 / cnt)[None, :]
    d["cst2"] = c2
    d["memT"] = np.ascontiguousarray(inp["mem_prompt"][b].T.reshape(16, 128, 256))
    sl = slice(c * NSEQ, (c + 1) * NSEQ)
    d["cmem_k"] = np.ascontiguousarray(inp["cache_mem_k"][:, sl].reshape(L, NSEQ, 2, 128, 512))
    d["cmem_v"] = np.ascontiguousarray(inp["cache_mem_v"][:, sl].reshape(L, NSEQ, 2, 128, 512))
    d["st_pool"] = np.ascontiguousarray(inp["state_pool"][:, sl].reshape(L, NSEQ, 15, 4, 128).transpose(0, 3, 4, 1, 2))
    d["st_conv"] = np.ascontiguousarray(inp["state_conv"][:, sl].reshape(L, NSEQ, 2, 4, 128).transpose(0, 3, 4, 1, 2))
    def ssm_st(a):
        return a.reshape(L, NSEQ, 16, 2, 64).transpose(0, 3, 4, 2, 1).reshape(L, 128, 16, NSEQ)
    d["st_ssm"] = np.ascontiguousarray(np.stack([ssm_st(inp["state_ssm_re"][:, sl]), ssm_st(inp["state_ssm_im"][:, sl])], axis=-1))
    pt = inp["page_table"][c * NSEQ:(c + 1) * NSEQ]
    d["pt_rep"] = np.ascontiguousarray(pt.T[qi // 8, :]).astype(np.int32)
    return d


_PROGRAM_CACHE = {}


def kernel(**inputs):
    inp = {k: np.asarray(v) for k, v in inputs.items()}
    cfg = {"groups": [[0, 1], [2, 3], [4, 5], [6, 7]]}
    if "prog" not in _PROGRAM_CACHE:
        _PROGRAM_CACHE["prog"] = build_program(cfg)
    nc, k = _PROGRAM_CACHE["prog"]
    sh = prep_shared(inp)
    need = [n for n in k.dram if not n.endswith("__tok") and not n.startswith("o_") and not n.startswith("dbg_")
            and not n.startswith("x1_") and not n.startswith("x2_")]
    in_maps = []
    for c in range(8):
        d = prep_core(inp, c, sh)
        in_maps.append({n: (d[n] if n in d else sh[n]) for n in need})
    res = run_bass_kernel_spmd(nc, in_maps, core_ids=list(range(8)))
    R = res.results
    f32 = np.float32
    B, S = 4, 2048
    y_p = np.zeros((B, S, D), f32); y_s = np.zeros((128, 8, D), f32)
    pk = np.zeros((L, B, S, 64), f32); pv = np.zeros((L, B, S, 64), f32); pik = np.zeros((L, B, S, 64), f32)
    sk = np.zeros((L, 128, 8, 64), f32); sv = np.zeros((L, 128, 8, 64), f32); sik = np.zeros((L, 128, 8, 64), f32)
    pmk = np.zeros((L, B, 256, 4, 128), f32); pmv = np.zeros((L, B, 256, 4, 128), f32)
    ppool = np.zeros((L, B, 15, 512), f32); pconv = np.zeros((L, B, 2, 512), f32)
    pre = np.zeros((L, B, 32, 64), f32); pim = np.zeros((L, B, 32, 64), f32)
    spool = np.zeros((L, 128, 15, 512), f32); sconv = np.zeros((L, 128, 2, 512), f32)
    sre = np.zeros((L, 128, 32, 64), f32); sim_ = np.zeros((L, 128, 32, 64), f32)
    for c in range(8):
        r = R[c]
        b, hf = c // 2, c % 2
        ps = slice(hf * PT, (hf + 1) * PT)
        ss = slice(c * NSEQ, (c + 1) * NSEQ)
        yt = np.asarray(r["o_y"], f32).reshape(D, NT).T
        y_p[b, ps] = yt[:PT]
        y_s[ss] = yt[PT:].reshape(NSEQ, 8, D)
        ok = np.asarray(r["o_k"], f32); oik = np.asarray(r["o_ik"], f32); ov = np.asarray(r["o_v"], f32).reshape(L, NT, 64)
        for l in range(L):
            kt = ok[l].T; ikt = oik[l].T
            pk[l, b, ps] = kt[:PT]; sk[l, ss] = kt[PT:].reshape(NSEQ, 8, 64)
            pik[l, b, ps] = ikt[:PT]; sik[l, ss] = ikt[PT:].reshape(NSEQ, 8, 64)
            pv[l, b, ps] = ov[l, :PT]; sv[l, ss] = ov[l, PT:].reshape(NSEQ, 8, 64)
            sp_ = np.asarray(r["o_pool_s"], f32)[l]
            spool[l, ss] = sp_.transpose(2, 3, 0, 1).reshape(NSEQ, 15, 512)
            sc_ = np.asarray(r["o_conv_s"], f32)[l]
            sconv[l, ss] = sc_.transpose(2, 3, 0, 1).reshape(NSEQ, 2, 512)
            st_ = np.asarray(r["o_ssm_s"], f32)[l].reshape(2, 64, 16, NSEQ, 2)
            st_ = st_.transpose(3, 2, 0, 1, 4).reshape(NSEQ, 32, 64, 2)
            sre[l, ss] = st_[..., 0]; sim_[l, ss] = st_[..., 1]
            if hf == 0:
                mk = np.asarray(r["o_mk"], f32)[l]
                pmk[l, b] = mk.transpose(2, 1, 0)
                mv = np.asarray(r["o_mv"], f32)[l]
                pmv[l, b] = mv.transpose(1, 0, 2).reshape(256, 4, 128)
            else:
                pp = np.asarray(r["o_pool_p"], f32)[l]
                ppool[l, b] = pp.transpose(2, 0, 1).reshape(15, 512)
                pc = np.asarray(r["o_conv_p"], f32)[l]
                pconv[l, b] = pc.transpose(2, 0, 1).reshape(2, 512)
                e_ = np.asarray(r["o_ssm_p"], f32)[l].reshape(2, 64, 16, 2).transpose(2, 0, 1, 3).reshape(32, 64, 2)
                pre[l, b] = e_[..., 0]; pim[l, b] = e_[..., 1]
    return (y_p, y_s, pk, pv, pik, pmk, pmv, ppool, pconv, pre, pim, sk, sv, sik, spool, sconv, sre, sim_)
```

```python
import contextlib
import numpy as np
import concourse.bass as bass
import concourse.mybir as mybir
from concourse.bass_utils import run_bass_kernel_spmd

F32 = mybir.dt.float32
BF16 = mybir.dt.bfloat16
I32 = mybir.dt.int32
ALU = mybir.AluOpType
AF = mybir.ActivationFunctionType
AX = mybir.AxisListType

L = 2
D = 2048
NT = 1152
PT = 1024
NSEQ = 16
CH = [(0, 384), (384, 384), (768, 384)]
NEG = -1.0e30
EPS = 1e-6


class Eng:
    def __init__(self, fw, name, handle, is_pe=False):
        self.fw = fw
        self.name = name
        self.h = handle
        self.is_pe = is_pe
        self.sem = fw.new_sem("e_" + name)
        self.count = 0
        self.seen = {}

    def wait(self, sem, val):
        key = id(sem)
        if self.seen.get(key, 0) >= val:
            return
        self.seen[key] = val
        if not self.fw.dry:
            self.h.wait_ge(sem, val)


class T:
    __slots__ = ("name", "w", "rd", "sem", "ndma", "pend")

    def __init__(self, name):
        self.name = name
        self.w = None
        self.rd = {}
        self.sem = None
        self.ndma = 0
        self.pend = {}


class FW:
    def __init__(self, nc, dry=False):
        self.nc = nc
        self.dry = dry
        self.stack = contextlib.ExitStack()
        self.nsem = 0
        self.pe = Eng(self, "pe", nc.tensor, is_pe=True)
        self.act = Eng(self, "act", nc.scalar)
        self.dve = Eng(self, "dve", nc.vector)
        self.pool = Eng(self, "pool", nc.gpsimd)
        self.sp = Eng(self, "sp", nc.sync)
        self.engs = [self.pe, self.act, self.dve, self.pool, self.sp]
        self.dsems = {}
        self.nalloc = 0

    def new_sem(self, name):
        self.nsem += 1
        if self.dry:
            return object()
        return self.stack.enter_context(self.nc.semaphore(name + "_%d" % self.nsem))

    def sbuf(self, name, shape, dtype, stack=None):
        self.nalloc += 1
        return (stack or self.stack).enter_context(
            self.nc.sbuf_tensor("%s_%d" % (name, self.nalloc), list(shape), dtype))

    def psum(self, name, shape, dtype):
        return self.stack.enter_context(self.nc.psum_tensor(name, list(shape), dtype))

    def tok(self, name="t"):
        return T(name)

    def _deps(self, eng, reads, writes):
        for t in reads:
            if t.w is not None:
                e, c = t.w
                if not (e is eng and eng.is_pe):
                    eng.wait(e.sem, c)
            for sem, val in t.pend.values():
                eng.wait(sem, val)
        for t in writes:
            if t.w is not None:
                e, c = t.w
                if not (e is eng and eng.is_pe):
                    eng.wait(e.sem, c)
            for e, c in t.rd.items():
                if not (e is eng and eng.is_pe):
                    eng.wait(e.sem, c)
            for sem, val in t.pend.values():
                eng.wait(sem, val)

    def op(self, eng, reads, writes, fn):
        self._deps(eng, reads, writes)
        eng.count += 1
        if not self.dry:
            inst = fn()
            inst.then_inc(eng.sem, 1)
        for t in reads:
            t.rd[eng] = eng.count
        for t in writes:
            t.w = (eng, eng.count)
            t.rd = {}

    def dma(self, eng, reads, writes, fn, main=None, inc=16):
        self._deps(eng, reads, writes)
        toks = list(reads) + list(writes)
        if main is None:
            main = writes[0] if writes else reads[0]
        if main.sem is None:
            main.sem = self.new_sem("d_" + main.name)
        if not self.dry:
            inst = fn()
            inst.then_inc(main.sem, inc)
        main.ndma += 1
        ent = (main.sem, inc * main.ndma)
        self.dsems[id(main.sem)] = (main.sem, inc * main.ndma, eng)
        for t in toks + [main]:
            t.pend[id(main.sem)] = ent

    def barrier(self):
        for e in self.engs:
            for x in self.engs:
                if x is not e and x.count:
                    e.wait(x.sem, x.count)
            for sem, val, iss in self.dsems.values():
                if iss is e and e is self.pool:
                    continue
                e.wait(sem, val)

    @contextlib.contextmanager
    def scope(self):
        st = contextlib.ExitStack()
        try:
            yield st
        finally:
            self.barrier()
            st.close()

    def finish(self, out_tokens):
        for t in out_tokens:
            for sem, val in t.pend.values():
                self.sp.wait(sem, val)
        for e in self.engs:
            if e is not self.sp and e.count:
                self.sp.wait(e.sem, e.count)
        for sem, val, iss in self.dsems.values():
            self.sp.wait(sem, val)

    def close(self):
        self.stack.close()


W_IN_TILES = 30
TI_XP, TI_XC, TI_BG, TI_CG, TI_Q, TI_K, TI_IQ, TI_IK, TI_XS = 0, 4, 8, 12, 16, 20, 21, 25, 26


class Kern:
    def __init__(self, nc, f, plan, cfg):
        self.nc = nc
        self.f = f
        self.plan = plan
        self.cfg = cfg
        self.wkeys = []
        self.wi = 0
        self.wissued = 0
        self.outs = []
        self.dram = {}
        self.dbg = {}

    def din(self, name, shape, dtype=F32):
        if name not in self.dram:
            self.dram[name] = self.nc.dram_tensor(name, list(shape), dtype, kind="ExternalInput").ap()
        return self.dram[name]

    def dout(self, name, shape, dtype=F32):
        if name not in self.dram:
            self.dram[name] = self.nc.dram_tensor(name, list(shape), dtype, kind="ExternalOutput").ap()
            t = self.f.tok(name)
            self.outs.append(t)
            self.dram[name + "__tok"] = t
        return self.dram[name], self.dram[name + "__tok"]

    def init_banks(self):
        self.banks = [self.f.psum("bank%d" % i, [128, 512], F32) for i in range(8)]
        self.tbank = [self.f.tok("bank%d" % i) for i in range(8)]
        self.bi = 0

    def bank(self, lo=0, hi=8):
        key = (lo, hi)
        if not hasattr(self, "_bk"):
            self._bk = {}
        i = self._bk.get(key, lo)
        self._bk[key] = lo + (i + 1 - lo) % (hi - lo)
        return self.banks[i], self.tbank[i]

    NSLOT = 4

    def init_wring(self):
        self.wslots = [self.f.sbuf("wslot", [128, 2048], BF16) for _ in range(self.NSLOT)]
        self.twslot = [self.f.tok("wslot%d" % i) for i in range(self.NSLOT)]

    def _issue(self, idx):
        key = self.plan[idx]
        name, index, n = key
        self.din(name, self.WSHAPES[name])
        s = idx % self.NSLOT
        src = self.dram[name]
        for i in index:
            src = src[i]
        slot = self.wslots[s]
        self.f.dma(self.f.pool, [], [self.twslot[s]],
                   lambda: self.nc.gpsimd.dma_start(out=slot[:, 0:n], in_=src))

    WSHAPES = {"w_ffn_in": [L, 2, 64, 128, 2048], "w_ffn_out": [L, 2, 2, 16, 128, 2048],
               "w_in_t": [L, W_IN_TILES, 128, 2048], "w_v_rhs": [L, 128, 1024], "w_iw_rhs": [L, 128, 128],
               "w_pool": [L, 4, 128, 128], "w_glu": [L, 4, 128, 512], "w_gate": [L, 4, 16, 128, 2048],
               "w_branch": [L, 4, 16, 128, 512], "w_o": [L, 2, 16, 128, 1024], "w_xq": [L, 4, 128, 2048],
               "w_xk": [L, 4, 128, 2048], "w_xv_rhs": [L, 4, 128, 2048], "w_xo": [L, 16, 128, 512]}

    def wget(self, name, index, n):
        self.din(name, self.WSHAPES[name])
        key = (name, tuple(index), n)
        if self.f.dry:
            self.wkeys.append(key)
            s = len(self.wkeys) - 1
            return self.wslots[s % self.NSLOT], self.twslot[s % self.NSLOT]
        assert self.plan[self.wi] == key, (self.plan[self.wi], key)
        i = self.wi
        self.wi += 1
        while self.wissued < min(len(self.plan), i + self.NSLOT):
            self._issue(self.wissued)
            self.wissued += 1
        return self.wslots[i % self.NSLOT], self.twslot[i % self.NSLOT]

    def mm(self, out, lhsT, rhs, start, stop, reads, writes, **kw):
        self.f.op(self.f.pe, reads, writes,
                  lambda: self.nc.tensor.matmul(out, lhsT=lhsT, rhs=rhs, start=start, stop=stop, **kw))

    def dbg_dump(self, name, ap, tok, shape, dtype=F32):
        if not self.cfg.get("debug"):
            return
        d, t = self.dout("dbg_" + name, shape, dtype)
        self.f.dma(self.f.sp, [tok], [t], lambda: self.nc.sync.dma_start(out=d, in_=ap), main=tok)

    def setup(self):
        nc, f = self.nc, self.f
        self.init_banks()
        self.init_wring()
        self.hT = f.sbuf("hT", [128, 16, NT], F32)
        self.th = [f.tok("h%d" % i) for i in range(16)]
        xT = self.din("xT", [16, 128, NT])
        for kt in range(16):
            f.dma(f.sp, [], [self.th[kt]], lambda: nc.sync.dma_start(out=self.hT[:, kt, :], in_=xT[kt]))
        cst = self.din("consts_bf", [128, 4 * 128], F32)
        self.c_f32 = f.sbuf("c_f32", [128, 512], F32)
        self.tc = f.tok("consts")
        f.dma(f.sp, [], [self.tc], lambda: nc.sync.dma_start(out=self.c_f32[:], in_=cst))
        self.c_bf = f.sbuf("c_bf", [128, 512], BF16)
        f.op(f.dve, [self.tc], [self.tc], lambda: nc.vector.tensor_copy(out=self.c_bf[:], in_=self.c_f32[:]))
        self.ones_bf = self.c_bf[:, 0:128]
        self.bd64_bf = self.c_bf[:, 128:256]
        self.ident_bf = self.c_bf[:, 256:384]
        self.ident_f = self.c_f32[:, 256:384]
        self.swap_bf = self.c_bf[:, 384:512]
        for nm in ("cache_ik", "cache_k", "cache_v"):
            for l_ in range(L):
                self.din("%s_%d" % (nm, l_), [self.cfg.get("npool8", 20480), 1024])
        sp_ = self.din("gains", [128, L * 4 * 16])
        self.gains = f.sbuf("gains", [128, L * 4 * 16], F32)
        self.tg = f.tok("gains")
        f.dma(f.sp, [], [self.tg], lambda: nc.sync.dma_start(out=self.gains[:], in_=sp_))
        hp = self.din("headp", [128, 8 * L])
        self.headp = f.sbuf("headp", [128, 8 * L], F32)
        self.thp = f.tok("headp")
        f.dma(f.sp, [], [self.thp], lambda: nc.sync.dma_start(out=self.headp[:], in_=hp))
        cb = self.din("cbias", [128, 259])
        self.cbias = f.sbuf("cbias", [128, 259], F32)
        self.tcb = f.tok("cbias")
        f.dma(f.sp, [], [self.tcb], lambda: nc.sync.dma_start(out=self.cbias[:], in_=cb))
        ptr = self.din("pt_rep", [128, 16], I32)
        ptsb = f.sbuf("ptsb", [128, 16], I32)
        self.idx = f.sbuf("idx", [128, 16], I32)
        self.tidx = f.tok("idx")
        f.dma(f.sp, [], [self.tidx], lambda: nc.sync.dma_start(out=ptsb[:], in_=ptr))
        f.op(f.dve, [self.tidx, self.tcb], [self.tidx], lambda: nc.vector.tensor_scalar(
            out=self.idx[:], in0=ptsb[:], scalar1=8.0, scalar2=self.cbias[:, 258:259], op0=ALU.mult, op1=ALU.add))
        c2 = self.din("cst2", [128, 404])
        self.cst2 = f.sbuf("cst2", [128, 404], F32)
        self.tcb2 = f.tok("cst2")
        f.dma(f.sp, [], [self.tcb2], lambda: nc.sync.dma_start(out=self.cst2[:], in_=c2))
        C2 = self.cst2
        self.rcnt = C2[:, 0:64].rearrange("p (a b) -> p a b", b=16)
        self.pscale = C2[:, 64:72]
        self.convw = C2[:, 72:96]
        self.ssd = C2[:, 96:104]
        self.glub = C2[:, 104:112]
        self.xqg = C2[:, 112:114]
        self.xkg = C2[:, 114:116]
        self.memg = C2[:, 116:148]
        self.bgate = C2[:, 148:276]
        self.bmask = C2[:, 276:404]
        self.eps_t = f.sbuf("eps", [128, 1], F32)
        self.teps = f.tok("eps")
        f.op(f.dve, [], [self.teps], lambda: nc.vector.memset(self.eps_t[:], EPS))
        self.rstd = f.sbuf("rstd", [128, NT], F32)
        self.trstd = f.tok("rstd")

    def rmsnorm(self, uT, tu, gcol0, st):
        nc, f = self.nc, self.f
        sq = [f.sbuf("sq", [128, 384], BF16, st) for _ in range(4)]
        tsq = [f.tok("sq%d" % i) for i in range(4)]
        qi = 0
        for (c0, cn) in CH:
            bk, tb = self.bank()
            for kt in range(16):
                s_, ts_ = sq[qi % 4], tsq[qi % 4]
                qi += 1
                f.op(f.act, [self.th[kt]], [ts_],
                     lambda: nc.scalar.activation(out=s_[:], in_=self.hT[:, kt, c0:c0 + cn], func=AF.Square))
                self.mm(bk[:, 0:cn], self.ones_bf, s_[:], kt == 0, kt == 15, [ts_, self.tc], [tb])
            f.op(f.act, [tb, self.teps], [self.trstd],
                 lambda: nc.scalar.activation(out=self.rstd[:, c0:c0 + cn], in_=bk[:, 0:cn], func=AF.Sqrt,
                                              bias=self.eps_t[:, 0:1], scale=1.0 / D))
        f.op(f.dve, [self.trstd], [self.trstd], lambda: nc.vector.reciprocal(out=self.rstd[:], in_=self.rstd[:]))
        for kt in range(16):
            f.op(f.dve, [self.th[kt], self.trstd, self.tg], [tu[kt]],
                 lambda: nc.vector.scalar_tensor_tensor(
                     out=uT[:, kt, :], in0=self.hT[:, kt, :], scalar=self.gains[:, gcol0 + kt:gcol0 + kt + 1],
                     in1=self.rstd[:], op0=ALU.mult, op1=ALU.mult))

    def ffn(self, l, w, uT, tu, st):
        nc, f = self.nc, self.f
        g = f.sbuf("g", [128, 16, NT], BF16, st)
        tgt = [f.tok("g%d" % i) for i in range(16)]
        sa = [f.sbuf("sa", [128, NT], F32, st) for _ in range(2)]
        tsa = [f.tok("sa%d" % i) for i in range(2)]
        for hh in range(2):
            for jt in range(self.cfg.get("ffn_nj", 16)):
                j = hh * 16 + jt
                wa, twa = self.wget("w_ffn_in", (l, w, j), 2048)
                ba = [self.bank() for _ in CH]
                for kt in range(16):
                    for ci, (c0, cn) in enumerate(CH):
                        self.mm(ba[ci][0][:, 0:cn], wa[:, kt * 128:(kt + 1) * 128], uT[:, kt, c0:c0 + cn],
                                kt == 0, kt == 15, [twa, tu[kt]], [ba[ci][1]])
                s_, ts_ = sa[j % 2], tsa[j % 2]
                for ci, (c0, cn) in enumerate(CH):
                    f.op(f.act, [ba[ci][1]], [ts_],
                         lambda: nc.scalar.activation(out=s_[:, c0:c0 + cn], in_=ba[ci][0][:, 0:cn], func=AF.Silu))
                wb, twb = self.wget("w_ffn_in", (l, w, 32 + j), 2048)
                bb = [self.bank() for _ in CH]
                for kt in range(16):
                    for ci, (c0, cn) in enumerate(CH):
                        self.mm(bb[ci][0][:, 0:cn], wb[:, kt * 128:(kt + 1) * 128], uT[:, kt, c0:c0 + cn],
                                kt == 0, kt == 15, [twb, tu[kt]], [bb[ci][1]])
                for ci, (c0, cn) in enumerate(CH):
                    f.op(f.dve, [bb[ci][1], ts_], [tgt[jt]],
                         lambda: nc.vector.tensor_tensor(out=g[:, jt, c0:c0 + cn], in0=bb[ci][0][:, 0:cn],
                                                         in1=s_[:, c0:c0 + cn], op=ALU.mult))
            for e in range(self.cfg.get("ffn_ne", 16)):
                wo, two = self.wget("w_ffn_out", (l, w, hh, e), 2048)
                for ci, (c0, cn) in enumerate(CH):
                    bk, tb = self.bank()
                    nj = self.cfg.get("ffn_nj", 16)
                    for jt in range(nj):
                        self.mm(bk[:, 0:cn], wo[:, jt * 128:(jt + 1) * 128], g[:, jt, c0:c0 + cn],
                                jt == 0, jt == nj - 1, [two, tgt[jt]], [tb])
                    f.op(f.dve, [tb, self.th[e]], [self.th[e]],
                         lambda: nc.vector.scalar_tensor_tensor(
                             out=self.hT[:, e, c0:c0 + cn], in0=bk[:, 0:cn], scalar=0.5,
                             in1=self.hT[:, e, c0:c0 + cn], op0=ALU.mult, op1=ALU.add))

    def proj(self, wname, index, uT, tu, evac):
        w_, tw = self.wget(wname, index, 2048)
        bs = [self.bank() for _ in CH]
        for kt in range(16):
            for ci, (c0, cn) in enumerate(CH):
                self.mm(bs[ci][0][:, 0:cn], w_[:, kt * 128:(kt + 1) * 128], uT[:, kt, c0:c0 + cn],
                        kt == 0, kt == 15, [tw, tu[kt]], [bs[ci][1]])
        for ci, (c0, cn) in enumerate(CH):
            evac(ci, c0, cn, bs[ci][0], bs[ci][1])

    def layer(self, l):
        nc, f = self.nc, self.f
        cfg = self.cfg
        with f.scope() as st:
            uT = f.sbuf("uT", [128, 16, NT], BF16, st)
            tu = [f.tok("u%d" % i) for i in range(16)]
            self.rmsnorm(uT, tu, (l * 4 + 0) * 16, st)
            self.ffn(l, 0, uT, tu, st)
        if cfg.get("stop") == "ffn1":
            return False
        with f.scope() as st_m:
            self._st_m = st_m
            ycT = f.sbuf("ycT", [128, 4, NT], BF16, st_m)
            tyc = f.tok("ycT")
            if cfg.get("skip") or cfg.get("nq", 8) < 8:
                f.op(f.dve, [], [tyc], lambda: nc.vector.memset(ycT[:], 0.0))
            halo = f.sbuf("halo", [128, 68], F32, st_m)
            hown = f.sbuf("hown", [128, 68], F32, st_m)
            thalo, thown = f.tok("halo"), f.tok("hown")
            self._halo = (halo, thalo, hown, thown)
            self.mixer_attention(l, ycT, tyc, hown, thown)
            if l == 0:
                self.dbg_dump("yc%d" % l, ycT[:], tyc, [128, 4, NT], BF16)
            if cfg.get("stop") == "att":
                return False
            br = f.sbuf("br", [128, 12, NT], BF16, st_m)
            tbr = [f.tok("br%d" % i) for i in range(12)]
            self.mixer_ssm(l, br, tbr)
            self.mixer_poolconv(l, br, tbr, halo, thalo)
            if l == 0:
                self.dbg_dump("br%d" % l, br[:], tbr[11], [128, 12, NT], BF16)
            if cfg.get("stop") == "mix":
                return False
            self.gate_phase(l, br, tbr, ycT, tyc)
        if cfg.get("stop") == "gate":
            return False
        self.cross_attention(l)
        if cfg.get("stop") == "xattn":
            return False
        if cfg.get("stop") == "full" and l == 0:
            pass
        with f.scope() as st:
            uT = f.sbuf("uT", [128, 16, NT], BF16, st)
            tu = [f.tok("u%d" % i) for i in range(16)]
            self.rmsnorm(uT, tu, (l * 4 + 3) * 16, st)
            self.ffn(l, 1, uT, tu, st)
        return True

    def headnorm(self, src, tsrc, dh, gcol, out, tout, st, bd):
        nc, f = self.nc, self.f
        if not hasattr(st, "_hn"):
            st._hn = (f.sbuf("hsq", [128, NT], BF16, st), f.sbuf("hrs", [128, NT], F32, st), f.tok("hsq"), f.tok("hrs"))
        sq, rs, tsq, trs = st._hn
        f.op(f.act, [tsrc], [tsq], lambda: nc.scalar.activation(out=sq[:], in_=src, func=AF.Square))
        for (c0, cn) in CH:
            bk, tb = self.bank()
            self.mm(bk[:, 0:cn], bd, sq[:, c0:c0 + cn], True, True, [tsq, self.tc], [tb])
            f.op(f.act, [tb, self.teps], [trs],
                 lambda: nc.scalar.activation(out=rs[:, c0:c0 + cn], in_=bk[:, 0:cn], func=AF.Sqrt,
                                              bias=self.eps_t[:, 0:1], scale=1.0 / dh))
        f.op(f.dve, [trs], [trs], lambda: nc.vector.reciprocal(out=rs[:], in_=rs[:]))
        f.op(f.dve, [trs, tsrc, self.thp], [tout],
             lambda: nc.vector.scalar_tensor_tensor(out=out, in0=src, scalar=gcol, in1=rs[:],
                                                    op0=ALU.mult, op1=ALU.mult))

    def cust(self, ap, dims):
        return bass.AP(ap.tensor, ap.offset, [list(ap.ap[0])] + [list(d) for d in dims])

    def mixer_attention(self, l, ycT, tyc, halo, thalo):
        nc, f, cfg = self.nc, self.f, self.cfg
        PAYW = 2704
        with f.scope() as st_s:
            self._mixer_attention(l, ycT, tyc, halo, thalo, st_s, PAYW)

    def _mixer_attention(self, l, ycT, tyc, halo, thalo, st_s, PAYW):
        nc, f, cfg = self.nc, self.f, self.cfg
        iqa = f.sbuf("iqa", [128, 8, 128], BF16, st_s)
        qa = f.sbuf("qa", [128, 8, 128], BF16, st_s)
        siwbc = f.sbuf("siwbc", [128, 1024], F32, st_s)
        KTn = f.sbuf("KTn", [128, 128], BF16, st_s)
        IKTn = f.sbuf("IKTn", [128, 128], BF16, st_s)
        Vn = f.sbuf("Vn", [128, 65], BF16, st_s)
        tnew = f.tok("newkeys")
        tiqa, tqa, tsiwbc = f.tok("iqa"), f.tok("qa"), f.tok("siwbc")
        with f.scope() as st_a:
            qT = f.sbuf("qT", [128, 4, NT], BF16, st_a)
            iqT = f.sbuf("iqT", [128, 4, NT], BF16, st_a)
            tq, tiq = f.tok("qT"), f.tok("iqT")
            iw_tm = f.sbuf("iw_tm", [128, 9, 8], F32, st_a)
            tiw = f.tok("iw_tm")
            pay = f.sbuf("pay", [128, PAYW], BF16, st_a)
            tpay, tprev = f.tok("pay"), f.tok("prev")
            with f.scope() as st:
                uT = f.sbuf("uT", [128, 16, NT], BF16, st)
                tu = [f.tok("u%d" % i) for i in range(16)]
                self.rmsnorm(uT, tu, (l * 4 + 1) * 16, st)
                vtm = f.sbuf("vtm", [128, 9, 64], F32, st)
                tv = f.tok("vtm")
                pf = f.sbuf("pf", [128, NT], F32, st)
                tpf = f.tok("pf")
                kT, ikT, tkT, tikT = pf, pf, tpf, tpf
                ok, tok_ = self.dout("o_k", [L, 64, NT])
                oik, toik = self.dout("o_ik", [L, 64, NT])
                ov, tov = self.dout("o_v", [L, 9, 128, 64])

                def evac_to(dst, tdst):
                    def ev(ci, c0, cn, bk, tb):
                        f.op(f.act, [tb], [tdst], lambda: nc.scalar.copy(out=dst[:, c0:c0 + cn], in_=bk[:, 0:cn]))
                    return ev
                self.proj("w_in_t", (l, TI_K), uT, tu, evac_to(pf, tpf))
                self.headnorm(pf[:], tpf, 64, self.headp[:, 8 * l:8 * l + 1], pf[:], tpf, st, self.bd64_bf)
                f.dma(f.sp, [tpf], [tok_], lambda: nc.sync.dma_start(out=ok[l], in_=pf[0:64, :]), main=tpf)
                f.op(f.dve, [tpf], [tpay], lambda: nc.vector.tensor_copy(out=pay[:, 0:1024], in_=pf[:, 0:PT]))
                f.op(f.dve, [tpf], [tnew], lambda: nc.vector.tensor_copy(out=KTn[:], in_=pf[:, PT:NT]))
                self.proj("w_in_t", (l, TI_IK), uT, tu, evac_to(pf, tpf))
                f.dma(f.sp, [tpf], [toik], lambda: nc.sync.dma_start(out=oik[l], in_=pf[0:64, :]), main=tpf)
                f.op(f.dve, [tpf], [tpay], lambda: nc.vector.tensor_copy(out=pay[:, 1024:2048], in_=pf[:, 0:PT]))
                f.op(f.dve, [tpf], [tnew], lambda: nc.vector.tensor_copy(out=IKTn[:], in_=pf[:, PT:NT]))
                wv, twv = self.wget("w_v_rhs", (l,), 1024)
                for tt in range(9):
                    bk, tb = self.bank()
                    for kt in range(16):
                        self.mm(bk[:, 0:64], uT[:, kt, tt * 128:(tt + 1) * 128], wv[:, kt * 64:(kt + 1) * 64],
                                kt == 0, kt == 15, [twv, tu[kt]], [tb])
                    f.op(f.act, [tb], [tv], lambda: nc.scalar.copy(out=vtm[:, tt, :], in_=bk[:, 0:64]))
                wiw, twiw = self.wget("w_iw_rhs", (l,), 128)
                for tt in range(9):
                    bk, tb = self.bank()
                    for kt in range(16):
                        self.mm(bk[:, 0:8], uT[:, kt, tt * 128:(tt + 1) * 128], wiw[:, kt * 8:(kt + 1) * 8],
                                kt == 0, kt == 15, [twiw, tu[kt]], [tb])
                    f.op(f.act, [tb], [tiw], lambda: nc.scalar.copy(out=iw_tm[:, tt, :], in_=bk[:, 0:8]))
                f.dma(f.sp, [tv], [tov],
                      lambda: nc.sync.dma_start(out=ov[l].rearrange("t p d -> p t d"), in_=vtm[:]), main=tv)
                vview = pay[:, 2048:2568].rearrange("p (t d) -> p t d", d=65)
                f.op(f.dve, [tv], [tpay], lambda: nc.vector.tensor_copy(out=vview[:, :, 0:64], in_=vtm[:, 0:8, :]))
                f.op(f.dve, [], [tpay], lambda: nc.vector.memset(vview[:, :, 64:65], 1.0))
                f.op(f.dve, [tv], [tnew], lambda: nc.vector.tensor_copy(out=Vn[:, 0:64], in_=vtm[:, 8, :]))
                f.op(f.dve, [], [tnew], lambda: nc.vector.memset(Vn[:, 64:65], 1.0))
                for t in range(4):
                    self.proj("w_in_t", (l, TI_Q + t), uT, tu, evac_to(pf, tpf))
                    self.headnorm(pf[:], tpf, 64, self.headp[:, 8 * l + 1:8 * l + 2], qT[:, t, :], tq, st, self.bd64_bf)

                    def ev_iq(ci, c0, cn, bk, tb, t=t):
                        f.op(f.act, [tb], [tiq], lambda: nc.scalar.copy(out=iqT[:, t, c0:c0 + cn], in_=bk[:, 0:cn]))
                    self.proj("w_in_t", (l, TI_IQ + t), uT, tu, ev_iq)
                payf = halo[:]
                hx = f.sbuf("hx", [128, 3, 16], F32, st)
                thx = f.tok("hx")

                def small_proj(ti, dst):
                    w_, tw = self.wget("w_in_t", (l, ti), 2048)
                    bk, tb = self.bank()
                    for kt in range(16):
                        self.mm(bk[:, 0:16], w_[:, kt * 128:(kt + 1) * 128], uT[:, kt, PT - 16:PT], kt == 0, kt == 15, [tw, tu[kt]], [tb])
                    f.op(f.act, [tb], [thx], lambda: nc.scalar.copy(out=dst, in_=bk[:, 0:16]))
                for t in range(4):
                    small_proj(TI_XP + t, hx[:, 0, :])
                    f.op(f.dve, [thx], [thalo], lambda: nc.vector.tensor_copy(out=payf[:, t * 15:(t + 1) * 15], in_=hx[:, 0, 1:16]))
                    small_proj(TI_XC + t, hx[:, 1, :])
                    small_proj(TI_CG + t, hx[:, 2, :])
                    f.op(f.dve, [thx], [thalo], lambda: nc.vector.tensor_tensor(out=payf[:, 60 + 2 * t:62 + 2 * t], in0=hx[:, 1, 14:16], in1=hx[:, 2, 14:16], op=ALU.mult))
                f.op(f.dve, [], [tpay], lambda: nc.vector.memset(pay[:, 2568:PAYW], 0.0))
            prev = f.sbuf("prev", [128, PAYW], BF16, st_a)
            pay_d = self.dint("x1_pay_%d" % l, [128, PAYW], BF16)
            gat_d = self.dint("x1_gat_%d" % l, [256, PAYW], BF16)
            tpd, tgd = f.tok("x1pay"), f.tok("x1gat")
            f.dma(f.sp, [tpay], [tpd], lambda: nc.sync.dma_start(out=pay_d, in_=pay[:]), main=tpd)
            f.dma(f.pool, [tpd], [tgd], lambda: nc.gpsimd.collective_compute(
                "AllGather", ALU.bypass, replica_groups=self.cfg["groups"], ins=[pay_d], outs=[gat_d]),
                main=tgd, inc=1)
            f.dma(f.sp, [tgd], [tprev], lambda: nc.sync.dma_start(out=prev[:], in_=gat_d[0:128, :]), main=tprev)
            if "P" not in cfg.get("skip", ""):
                with f.scope() as st:
                    self.attention_prompt(l, st, qT, tq, iqT, tiq, iw_tm, tiw, pay, tpay, prev, tprev, ycT, tyc)
            with f.scope() as st:
                Dg = f.sbuf("sDg", [128, 2, 4, 128], F32, st)
                tDg = f.tok("sDg")
                for (src, tsrc, dst, tdst) in ((iqT, tiq, iqa, tiqa), (qT, tq, qa, tqa)):
                    for j in range(4):
                        f.op(f.dve, [tsrc], [tdst], lambda: nc.vector.tensor_copy(out=dst[0:64, 2 * j, :], in_=src[0:64, j, PT:NT]))
                        f.op(f.dve, [tsrc], [tdst], lambda: nc.vector.tensor_copy(out=dst[64:128, 2 * j + 1, :], in_=src[64:128, j, PT:NT]))
                        bk, tb = self.bank(4, 8)
                        self.mm(bk[:, 0:128], self.swap_bf, src[:, j, PT:NT], True, True, [tsrc, self.tc], [tb])
                        f.op(f.act, [tb], [tdst], lambda: nc.scalar.copy(out=dst[0:64, 2 * j + 1, :], in_=bk[0:64, 0:128]))
                        f.op(f.act, [tb], [tdst], lambda: nc.scalar.copy(out=dst[64:128, 2 * j, :], in_=bk[64:128, 0:128]))
                self.iw_broadcast(iw_tm, tiw, 8, Dg, tDg, siwbc, tsiwbc, natural=True)
        if "S" not in cfg.get("skip", ""):
            with f.scope() as st:
                self.attention_sample(l, st, iqa, tiqa, qa, tqa, siwbc, tsiwbc, KTn, IKTn, Vn, tnew, ycT, tyc)

    def iw_broadcast(self, iw_tm, tiw, tt, Dg, tDg, iwbc, tiwbc, natural=False):
        nc, f = self.nc, self.f
        a = iw_tm[:, tt, :]
        in1 = self.cust(a, [(1, 2), (2, 4), (0, 128)])
        if natural:
            in1 = self.cust(a, [(4, 2), (1, 4), (0, 128)])
        idf = self.ident_f
        in0 = self.cust(idf, [(0, 2), (0, 4), (1, 128)])
        f.op(f.dve, [tiw, self.tc], [tDg],
             lambda: nc.vector.tensor_tensor(out=Dg[:], in0=in0, in1=in1, op=ALU.mult))
        dflat = Dg[:].rearrange("p a b q -> p (a b q)")
        for hb in range(2):
            bk, tb = self.bank(4, 8)
            self.mm(bk[:, 0:512], self.c_f32[:, 0:128], dflat[:, hb * 512:(hb + 1) * 512], True, True,
                    [tDg, self.tc], [tb])
            f.op(f.act, [tb], [tiwbc], lambda: nc.scalar.copy(out=iwbc[:, hb * 512:(hb + 1) * 512], in_=bk[:, 0:512]))

    def topk_mask(self, S, tS, WK, tWK, m8, tm8, thr, MK, tMK, nk):
        nc, f = self.nc, self.f
        cur, tcur = S, tS
        for r in range(32):
            f.op(f.dve, [tcur], [tm8], lambda: nc.vector.max(out=m8[:, r * 8:(r + 1) * 8], in_=cur[:, 0:nk]))
            if r < 31:
                f.op(f.dve, [tcur, tm8], [tWK],
                     lambda: nc.vector.match_replace(out=WK[:, 0:nk], in_to_replace=m8[:, r * 8:(r + 1) * 8],
                                                     in_values=cur[:, 0:nk], imm_value=-3.0e38))
                cur, tcur = WK, tWK
        f.op(f.dve, [tm8], [tm8], lambda: nc.vector.tensor_scalar(out=thr[:], in0=m8[:, 255:256], scalar1=-1.0e29,
                                                                   scalar2=None, op0=ALU.max))
        f.op(f.dve, [tS, tm8], [tMK], lambda: nc.vector.tensor_scalar(out=MK[:, 0:nk], in0=S[:, 0:nk], scalar1=thr[:, 0:1],
                                                                       scalar2=None, op0=ALU.is_ge))

    def attention_prompt(self, l, st, qT, tq, iqT, tiq, iw_tm, tiw, pay, tpay, prev, tprev, ycT, tyc):
        nc, f = self.nc, self.f
        ST = f.sbuf("ST", [128, 16, 128], F32, st)
        S = f.sbuf("S", [128, 2048], F32, st)
        m8 = f.sbuf("m8", [128, 256], F32, st)
        thr = f.sbuf("thr", [128, 1], F32, st)
        MK = f.sbuf("MK", [128, 2048], BF16, st)
        MT = f.sbuf("MT", [128, 16, 128], BF16, st)
        RL = f.sbuf("RL", [128, 1024], F32, st)
        E = [f.sbuf("E", [128, 512], BF16, st) for _ in range(2)]
        iwbc = f.sbuf("iwbc", [128, 1024], F32, st)
        Dg = f.sbuf("Dg", [128, 2, 4, 128], F32, st)
        otm = f.sbuf("otm", [128, 512], BF16, st)
        rec = f.sbuf("rec", [128, 8], F32, st)
        tST, tS, tm8, tMK, tMT, tRL, tiwbc, tDg, totm, trec = [f.tok(n) for n in
            ("ST", "S", "m8", "MK", "MT", "RL", "iwbc", "Dg", "otm", "rec")]
        tE = [f.tok("E0"), f.tok("E1")]
        WK = ST[:].rearrange("p a b -> p (a b)")
        pb = self.cbias
        for i in range(self.cfg.get("nq", 8)):
            q0 = i * 128
            kts = list(range(8)) + [8 + j for j in range(i + 1)]
            nk = len(kts) * 128
            self.iw_broadcast(iw_tm, tiw, i, Dg, tDg, iwbc, tiwbc)
            for n, kt in enumerate(kts):
                Ks, tK = (prev, tprev) if kt < 8 else (pay, tpay)
                kc = 1024 + (kt % 8) * 128
                for hb in range(2):
                    bk, tb = self.bank(4, 8)
                    rhs = self.cust(iqT[hb * 64:(hb + 1) * 64, 0, q0:q0 + 128], [(NT, 4), (1, 128)])
                    self.mm(bk[:, 0:512], Ks[hb * 64:(hb + 1) * 64, kc:kc + 128], rhs, True, True, [tK, tiq], [tb])
                    f.op(f.act, [tb], [tRL], lambda: nc.scalar.activation(out=RL[:, hb * 512:(hb + 1) * 512],
                                                                           in_=bk[:, 0:512], func=AF.Relu))
                f.op(f.dve, [tRL, tiwbc], [tRL], lambda: nc.vector.tensor_tensor(out=RL[:], in0=RL[:], in1=iwbc[:], op=ALU.mult))
                f.op(f.dve, [tRL], [tST], lambda: nc.vector.tensor_reduce(
                    out=ST[:, n, :], in_=self.cust(RL[:], [(1, 128), (128, 8)]), axis=AX.X, op=ALU.add))
            for n, kt in enumerate(kts):
                bk, tb = self.bank(4, 8)
                f.op(f.pe, [tST, self.tc], [tb], lambda: nc.tensor.transpose(out=bk[:, 0:128], in_=ST[:, n, :], identity=self.ident_f))
                dst = S[:, n * 128:(n + 1) * 128]
                if kt < 8:
                    f.op(f.act, [tb, self.tcb], [tS], lambda: nc.scalar.activation(out=dst, in_=bk[:, 0:128], func=AF.Identity,
                                                                                   bias=pb[:, 0:1], scale=1.0))
                elif kt == 8 + i:
                    f.op(f.dve, [tb, self.tcb], [tS], lambda: nc.vector.tensor_tensor(out=dst, in0=bk[:, 0:128], in1=pb[:, 1:129], op=ALU.add))
                else:
                    f.op(f.act, [tb], [tS], lambda: nc.scalar.copy(out=dst, in_=bk[:, 0:128]))
            self.topk_mask(S, tS, WK, tST, m8, tm8, thr, MK, tMK, nk)
            for n in range(len(kts)):
                bk, tb = self.bank(4, 8)
                bkb = bk[:].bitcast(BF16)
                f.op(f.pe, [tMK, self.tc], [tb], lambda: nc.tensor.transpose(out=bkb[:, 0:128], in_=MK[:, n * 128:(n + 1) * 128], identity=self.ident_bf))
                f.op(f.act, [tb], [tMT], lambda: nc.scalar.copy(out=MT[:, n, :], in_=bkb[:, 0:128]))
            for hb in range(2):
                accs = [(self.banks[j], self.tbank[j]) for j in range(4)]
                for n, kt in enumerate(kts):
                    Ks, tK = (prev, tprev) if kt < 8 else (pay, tpay)
                    kc = (kt % 8) * 128
                    vb = 2048 + (kt % 8) * 65
                    bk, tb = self.bank(4, 8)
                    rhs = self.cust(qT[hb * 64:(hb + 1) * 64, 0, q0:q0 + 128], [(NT, 4), (1, 128)])
                    self.mm(bk[:, 0:512], Ks[hb * 64:(hb + 1) * 64, kc:kc + 128], rhs, True, True, [tK, tq], [tb])
                    e, te = E[n % 2], tE[n % 2]
                    f.op(f.act, [tb], [te], lambda: nc.scalar.activation(out=e[:], in_=bk[:, 0:512], func=AF.Exp, scale=0.125))
                    ev = e[:].rearrange("p (j q) -> p j q", j=4)
                    f.op(f.dve, [te, tMT], [te], lambda: nc.vector.tensor_tensor(
                        out=ev, in0=ev, in1=self.cust(MT[:, n, :], [(0, 4), (1, 128)]), op=ALU.mult))
                    for j in range(4):
                        self.mm(accs[j][0][:, 0:65], e[:, j * 128:(j + 1) * 128], Ks[:, vb:vb + 65],
                                n == 0, n == len(kts) - 1, [te, tK], [accs[j][1]])
                for j in range(4):
                    c = hb * 4 + j
                    f.op(f.dve, [accs[j][1]], [trec], lambda: nc.vector.reciprocal(out=rec[:, c:c + 1], in_=accs[j][0][:, 64:65]))
                    f.op(f.dve, [accs[j][1], trec], [totm], lambda: nc.vector.tensor_scalar(
                        out=otm[:, (j * 2 + hb) * 64:(j * 2 + hb + 1) * 64], in0=accs[j][0][:, 0:64],
                        scalar1=rec[:, c:c + 1], scalar2=None, op0=ALU.mult))
            for t in range(4):
                bk, tb = self.bank(4, 8)
                bkb = bk[:].bitcast(BF16)
                f.op(f.pe, [totm, self.tc], [tb], lambda: nc.tensor.transpose(out=bkb[:, 0:128], in_=otm[:, t * 128:(t + 1) * 128], identity=self.ident_bf))
                f.op(f.act, [tb], [tyc], lambda: nc.scalar.copy(out=ycT[:, t, q0:q0 + 128], in_=bkb[:, 0:128]))

    def attention_sample(self, l, st, iqa, tiqa, qa, tqa, iwbc, tiwbc, KTn, IKTn, Vn, tnew, ycT, tyc):
        nc, f = self.nc, self.f
        NKT = 17
        ST = f.sbuf("sST", [128, NKT, 128], F32, st)
        S = f.sbuf("sS", [128, NKT * 128], F32, st)
        m8 = f.sbuf("sm8", [128, 256], F32, st)
        thr = f.sbuf("sthr", [128, 1], F32, st)
        MK = f.sbuf("sMK", [128, NKT * 128], BF16, st)
        MT = f.sbuf("sMT", [128, NKT, 128], BF16, st)
        G = [f.sbuf("G", [128, 1024], F32, st) for _ in range(4)]
        tG = [f.tok("G%d" % i) for i in range(4)]
        XT = [f.sbuf("XT", [128, 1024], BF16, st) for _ in range(2)]
        tXT = [f.tok("XT%d" % i) for i in range(2)]
        Va = [f.sbuf("Va", [128, 16, 65], BF16, st) for _ in range(2)]
        tVa = [f.tok("Va%d" % i) for i in range(2)]
        RL = f.sbuf("sRL", [128, NKT, 64], F32, st)
        E = [f.sbuf("sE", [128, NKT, 64], BF16, st) for _ in range(2)]
        tE = [f.tok("sE%d" % i) for i in range(2)]
        oall = f.sbuf("oall", [64, 16, 128], BF16, st)
        rec = f.sbuf("srec", [64, 1], F32, st)
        tST, tS, tm8, tMK, tMT, tRL, toall, trec = [f.tok(n) for n in
            ("sST", "sS", "sm8", "sMK", "sMT", "sRL", "oall", "srec")]
        WK = ST[:].rearrange("p a b -> p (a b)")
        cik = self.dram["cache_ik_%d" % l]
        ck = self.dram["cache_k_%d" % l]
        cv = self.dram["cache_v_%d" % l]
        def gather(cache, s, gi):
            g, tg = G[gi], tG[gi]
            f.dma(f.pool, [self.tidx], [tg], lambda: nc.gpsimd.indirect_dma_start(
                out=g[:], out_offset=None, in_=cache,
                in_offset=bass.IndirectOffsetOnAxis(ap=self.idx[:, s:s + 1], axis=0)))
            return g, tg

        def transpose_keys(g, tg, xt, txt):
            var = self.cfg.get("tkv", "C")
            if var == "B":
                for pr in range(8):
                    bk, tb = self.bank(4, 8)
                    f.op(f.pe, [tg, self.tc], [tb], lambda: nc.tensor.transpose(
                        out=bk[:, 0:128], in_=g[:, pr * 128:(pr + 1) * 128], identity=self.ident_f))
                    f.op(f.act, [tb], [txt], lambda: nc.scalar.copy(out=xt[:, pr * 128:(pr + 1) * 128], in_=bk[:, 0:128]))
                return
            for q4 in range(2):
                bk, tb = self.bank(4, 8)
                for i4 in range(4):
                    pr = q4 * 4 + i4
                    f.op(f.pe, [tg, self.tc], [tb], lambda: nc.tensor.transpose(
                        out=bk[:, i4 * 128:(i4 + 1) * 128], in_=g[:, pr * 128:(pr + 1) * 128],
                        identity=self.ident_f))
                if var == "A":
                    continue
                f.op(f.act, [tb], [txt], lambda: nc.scalar.copy(out=xt[:, q4 * 512:(q4 + 1) * 512], in_=bk[:, 0:512]))

        def key_ops(xt, kt, newT, qsrc, s):
            hp = (kt % 2) * 64 if kt < 16 else 0
            lhsT = xt[hp:hp + 64, (kt // 2) * 128:(kt // 2 + 1) * 128] if kt < 16 else newT[0:64, :]
            rhs = self.cust(qsrc[hp:hp + 64, 0, s * 8:(s + 1) * 8], [(128, 8), (1, 8)])
            return lhsT, rhs

        sstop = self.cfg.get("sstop", 99)
        for s in range(NSEQ if sstop >= 1 else 0):
            g, tg = gather(cik, s, s % 2)
            xt, txt = XT[s % 2], tXT[s % 2]
            if sstop < 2:
                continue
            transpose_keys(g, tg, xt, txt)
            if sstop < 3:
                continue
            for kts in (list(range(0, 16, 2)), list(range(1, 16, 2)), [16]):
                bk, tb = self.bank(4, 8)
                for n, kt in enumerate(kts):
                    lhsT, rhs = key_ops(xt, kt, IKTn, iqa, s)
                    self.mm(bk[:, n * 64:(n + 1) * 64], lhsT, rhs, True, True, [txt, tnew, tiqa], [tb])
                w = len(kts)
                f.op(f.act, [tb], [tRL], lambda: nc.scalar.activation(
                    out=self.cust(RL[:, kts[0], :], [(128 if w > 1 else 64, w), (1, 64)]),
                    in_=bk[:, 0:w * 64].rearrange("p (a b) -> p a b", b=64), func=AF.Relu))
            if sstop < 4:
                continue
            rl4 = RL[:].rearrange("p k (h q) -> p k h q", h=8)
            f.op(f.dve, [tRL, tiwbc], [tRL], lambda: nc.vector.tensor_tensor(
                out=rl4, in0=rl4, in1=self.cust(iwbc[:, s * 8:s * 8 + 8], [(0, NKT), (128, 8), (1, 8)]), op=ALU.mult))
            f.op(f.dve, [tRL], [tST], lambda: nc.vector.tensor_reduce(
                out=ST[:, :, s * 8:(s + 1) * 8], in_=self.cust(RL[:], [(64, NKT), (1, 8), (8, 8)]), axis=AX.X, op=ALU.add))
        if sstop < 5:
            return
        for n in range(NKT):
            bk, tb = self.bank(4, 8)
            f.op(f.pe, [tST, self.tc], [tb], lambda: nc.tensor.transpose(out=bk[:, 0:128], in_=ST[:, n, :], identity=self.ident_f))
            dst = S[:, n * 128:(n + 1) * 128]
            if n == 16:
                f.op(f.dve, [tb, self.tcb], [tS], lambda: nc.vector.tensor_tensor(out=dst, in0=bk[:, 0:128], in1=self.cbias[:, 129:257], op=ALU.add))
            else:
                f.op(f.act, [tb], [tS], lambda: nc.scalar.copy(out=dst, in_=bk[:, 0:128]))
        self.topk_mask(S, tS, WK, tST, m8, tm8, thr, MK, tMK, NKT * 128)
        for n in range(NKT):
            bk, tb = self.bank(4, 8)
            bkb = bk[:].bitcast(BF16)
            f.op(f.pe, [tMK, self.tc], [tb], lambda: nc.tensor.transpose(out=bkb[:, 0:128], in_=MK[:, n * 128:(n + 1) * 128], identity=self.ident_bf))
            f.op(f.act, [tb], [tMT], lambda: nc.scalar.copy(out=MT[:, n, :], in_=bkb[:, 0:128]))
        if sstop < 6:
            return
        for s in range(NSEQ):
            g, tg = gather(ck, s, s % 2)
            gv, tgv = gather(cv, s, 2 + s % 2)
            xt, txt = XT[s % 2], tXT[s % 2]
            transpose_keys(g, tg, xt, txt)
            va, tva = Va[s % 2], tVa[s % 2]
            f.op(f.dve, [tgv], [tva], lambda: nc.vector.tensor_copy(out=va[:, :, 0:64], in_=gv[:].rearrange("p (a b) -> p a b", b=64)))
            f.op(f.dve, [], [tva], lambda: nc.vector.memset(va[:, :, 64:65], 1.0))
            e, te = E[s % 2], tE[s % 2]
            for kts in (list(range(0, 16, 2)), list(range(1, 16, 2)), [16]):
                bk, tb = self.bank(4, 8)
                for n, kt in enumerate(kts):
                    lhsT, rhs = key_ops(xt, kt, KTn, qa, s)
                    self.mm(bk[:, n * 64:(n + 1) * 64], lhsT, rhs, True, True, [txt, tnew, tqa], [tb])
                w = len(kts)
                f.op(f.act, [tb], [te], lambda: nc.scalar.activation(
                    out=self.cust(e[:, kts[0], :], [(128 if w > 1 else 64, w), (1, 64)]),
                    in_=bk[:, 0:w * 64].rearrange("p (a b) -> p a b", b=64), func=AF.Exp, scale=0.125))
            e4 = e[:].rearrange("p k (h q) -> p k h q", h=8)
            f.op(f.dve, [te, tMT], [te], lambda: nc.vector.tensor_tensor(
                out=e4, in0=e4, in1=self.cust(MT[:, 0, s * 8:(s + 1) * 8], [(128, NKT), (0, 8), (1, 8)]), op=ALU.mult))
            acc, tacc = self.banks[0], self.tbank[0]
            for kt in range(NKT):
                rv = va[:, kt, :] if kt < 16 else Vn[:]
                self.mm(acc[0:64, 0:65], e[:, kt, :], rv, kt == 0, kt == NKT - 1, [te, tva, tnew], [tacc])
            f.op(f.dve, [tacc], [trec], lambda: nc.vector.reciprocal(out=rec[:], in_=acc[0:64, 64:65]))
            for d2 in range(2):
                f.op(f.dve, [tacc, trec], [toall], lambda: nc.vector.tensor_scalar(
                    out=oall[:, s, d2 * 64:(d2 + 1) * 64], in0=acc[0:64, 0:64], scalar1=rec[:, 0:1], scalar2=None, op0=ALU.mult))
        if sstop < 7:
            return
        bk, tb = self.bank(4, 8)
        bkb = bk[:].bitcast(BF16)
        for s in range(NSEQ):
            f.op(f.pe, [toall, self.tc], [tb], lambda: nc.tensor.transpose(
                out=bkb[:, s * 64:(s + 1) * 64], in_=oall[:, s, :], identity=self.ident_bf[0:64, 0:64]))
        for h in range(8):
            p0 = (h % 2) * 64
            src = self.cust(bkb[p0:p0 + 64, h * 8:h * 8 + 8], [(64, 16), (1, 8)])
            dst = ycT[p0:p0 + 64, h // 2, PT:NT].rearrange("p (s q) -> p s q", q=8)
            f.op(f.act, [tb], [tyc], lambda: nc.scalar.copy(out=dst, in_=src))

    def apply_norm(self, uT, tu, gcol0):
        nc, f = self.nc, self.f
        for kt in range(16):
            f.op(f.dve, [self.th[kt], self.trstd, self.tg], [tu[kt]],
                 lambda: nc.vector.scalar_tensor_tensor(
                     out=uT[:, kt, :], in0=self.hT[:, kt, :], scalar=self.gains[:, gcol0 + kt:gcol0 + kt + 1],
                     in1=self.rstd[:], op0=ALU.mult, op1=ALU.mult))

    def proj_split(self, wname, index, uT, tu, evac):
        def ev(ci, c0, cn, bk, tb):
            if c0 + cn <= PT:
                evac("p", c0, cn, bk[:, 0:cn], tb)
            else:
                evac("p", c0, PT - c0, bk[:, 0:PT - c0], tb)
                evac("s", 0, 128, bk[:, PT - c0:cn], tb)
        self.proj(wname, index, uT, tu, ev)

    def mixer_poolconv(self, l, br, tbr, halo, thalo):
        nc, f, cfg = self.nc, self.f, self.cfg
        PW = 1040
        with f.scope() as st:
            uT = f.sbuf("uT", [128, 16, NT], BF16, st)
            tu = [f.tok("u%d" % i) for i in range(16)]
            self.apply_norm(uT, tu, (l * 4 + 1) * 16)
            X = f.sbuf("pX", [128, PW + 384], F32, st)
            A = f.sbuf("pA", [128, PW + 384], F32, st)
            B = f.sbuf("pB", [128, PW + 384], F32, st)
            dT = f.sbuf("pdT", [128, NT], BF16, st)
            tmp16 = f.sbuf("ptmp", [128, 16], F32, st)
            tX, tA, tB, tdT, ttmp = [f.tok(n) for n in ("pX", "pA", "pB", "pdT", "ptmp")]
            sp_in = self.din("st_pool", [L, 4, 128, 16, 15])
            sc_in = self.din("st_conv", [L, 4, 128, 16, 2])
            opp, topp = self.dout("o_pool_p", [L, 4, 128, 15])
            ops_, tops = self.dout("o_pool_s", [L, 4, 128, 16, 15])
            ocp, tocp = self.dout("o_conv_p", [L, 4, 128, 2])
            ocs, tocs = self.dout("o_conv_s", [L, 4, 128, 16, 2])

            def SV(buf, a, b):
                return buf[:, PW:PW + 384].rearrange("p (s c) -> p s c", c=24)[:, :, a:b]
            for t in range(4):
                w = 2 ** (t + 1)
                f.op(f.dve, [thalo], [tX], lambda: nc.vector.tensor_copy(out=X[:, 1:16], in_=halo[:, t * 15:(t + 1) * 15]))
                f.dma(f.sp, [], [tX], lambda: nc.sync.dma_start(out=SV(X, 1, 16), in_=sp_in[l, t]))

                def ev_x(kind, t0, n, src, tb):
                    if kind == "p":
                        f.op(f.act, [tb], [tX], lambda: nc.scalar.copy(out=X[:, 16 + t0:16 + t0 + n], in_=src))
                    else:
                        f.op(f.act, [tb], [tX], lambda: nc.scalar.copy(out=SV(X, 16, 24), in_=src.rearrange("p (s q) -> p s q", q=8)))
                self.proj_split("w_in_t", (l, TI_XP + t), uT, tu, ev_x)
                f.dma(f.sp, [tX], [topp], lambda: nc.sync.dma_start(out=opp[l, t], in_=X[:, PW - 15:PW]), main=tX)
                f.dma(f.sp, [tX], [tops], lambda: nc.sync.dma_start(out=ops_[l, t], in_=SV(X, 9, 24)), main=tX)
                src_, tsrc = X, tX
                bufs = [(A, tA), (B, tB)]
                sh = 1
                for lev in range(t + 1):
                    dst, tdst = bufs[lev % 2]
                    lo = 2 * sh
                    f.op(f.dve, [tsrc], [tdst], lambda: nc.vector.tensor_tensor(
                        out=dst[:, lo:PW], in0=src_[:, lo:PW], in1=src_[:, lo - sh:PW - sh], op=ALU.add))
                    f.op(f.dve, [tsrc], [tdst], lambda: nc.vector.tensor_tensor(
                        out=SV(dst, lo, 24), in0=SV(src_, lo, 24), in1=SV(src_, lo - sh, 24 - sh), op=ALU.add))
                    src_, tsrc = dst, tdst
                    sh *= 2
                f.op(f.dve, [tsrc, tX], [tdT], lambda: nc.vector.scalar_tensor_tensor(
                    out=dT[:, 0:PT], in0=src_[:, 16:PW], scalar=1.0 / w, in1=X[:, 16:PW], op0=ALU.mult, op1=ALU.subtract))
                f.op(f.dve, [tsrc, tX], [tdT], lambda: nc.vector.scalar_tensor_tensor(
                    out=dT[:, PT:NT].rearrange("p (s q) -> p s q", q=8), in0=SV(src_, 16, 24), scalar=1.0 / w,
                    in1=SV(X, 16, 24), op0=ALU.mult, op1=ALU.subtract))
                f.op(f.dve, [tsrc, self.tcb2], [ttmp], lambda: nc.vector.tensor_tensor(
                    out=tmp16[:], in0=src_[:, 16:32], in1=self.rcnt[:, t, :], op=ALU.mult))
                f.op(f.dve, [ttmp, tX], [tdT], lambda: nc.vector.tensor_tensor(
                    out=dT[:, 0:16], in0=tmp16[:], in1=X[:, 16:32], op=ALU.subtract))
                wpool, twpool = self.wget("w_pool", (l, t), 128)
                for (c0, cn) in CH:
                    bk, tb = self.bank()
                    self.mm(bk[:, 0:cn], wpool[:, 0:128], dT[:, c0:c0 + cn], True, True, [twpool, tdT], [tb])
                    f.op(f.act, [tb, self.tcb2], [tbr[t]], lambda: nc.scalar.activation(
                        out=br[:, t, c0:c0 + cn], in_=bk[:, 0:cn], func=AF.Identity, scale=self.pscale[:, l * 4 + t:l * 4 + t + 1]))
            VW = 1026
            V = A
            tV = tA
            Y = B
            tY = tB
            CG = X
            tCG = tX

            def VS(a, b):
                return V[:, VW:VW + 160].rearrange("p (s c) -> p s c", c=10)[:, :, a:b]

            def YS():
                return Y[:, PT:NT].rearrange("p (s q) -> p s q", q=8)
            for c in range(4):
                f.op(f.dve, [thalo], [tV], lambda: nc.vector.tensor_copy(out=V[:, 0:2], in_=halo[:, 60 + 2 * c:62 + 2 * c]))
                f.dma(f.sp, [], [tV], lambda: nc.sync.dma_start(out=VS(0, 2), in_=sc_in[l, c]))

                def ev_cg(kind, t0, n, src, tb):
                    o = CG[:, t0:t0 + n] if kind == "p" else CG[:, PT:NT]
                    f.op(f.act, [tb], [tCG], lambda: nc.scalar.copy(out=o, in_=src))
                self.proj_split("w_in_t", (l, TI_CG + c), uT, tu, ev_cg)

                def ev_xc(kind, t0, n, src, tb):
                    if kind == "p":
                        f.op(f.dve, [tb, tCG], [tV], lambda: nc.vector.tensor_tensor(out=V[:, 2 + t0:2 + t0 + n], in0=src, in1=CG[:, t0:t0 + n], op=ALU.mult))
                    else:
                        f.op(f.dve, [tb, tCG], [tV], lambda: nc.vector.tensor_tensor(
                            out=VS(2, 10), in0=src.rearrange("p (s q) -> p s q", q=8),
                            in1=CG[:, PT:NT].rearrange("p (s q) -> p s q", q=8), op=ALU.mult))
                self.proj_split("w_in_t", (l, TI_XC + c), uT, tu, ev_xc)
                f.dma(f.sp, [tV], [tocp], lambda: nc.sync.dma_start(out=ocp[l, c], in_=V[:, VW - 2:VW]), main=tV)
                f.dma(f.sp, [tV], [tocs], lambda: nc.sync.dma_start(out=ocs[l, c], in_=VS(8, 10)), main=tV)
                cw = lambda j: self.convw[:, (l * 4 + c) * 3 + j:(l * 4 + c) * 3 + j + 1]
                f.op(f.dve, [tV, self.tcb2], [tY], lambda: nc.vector.tensor_scalar(out=Y[:, 0:PT], in0=V[:, 2:VW], scalar1=cw(2), scalar2=None, op0=ALU.mult))
                f.op(f.dve, [tV, self.tcb2], [tY], lambda: nc.vector.tensor_scalar(out=YS(), in0=VS(2, 10), scalar1=cw(2), scalar2=None, op0=ALU.mult))
                for j, sh in ((1, 1), (0, 2)):
                    f.op(f.dve, [tV, tY, self.tcb2], [tY], lambda: nc.vector.scalar_tensor_tensor(
                        out=Y[:, 0:PT], in0=V[:, 2 - sh:VW - sh], scalar=cw(j), in1=Y[:, 0:PT], op0=ALU.mult, op1=ALU.add))
                    f.op(f.dve, [tV, tY, self.tcb2], [tY], lambda: nc.vector.scalar_tensor_tensor(
                        out=YS(), in0=VS(2 - sh, 10 - sh), scalar=cw(j), in1=YS(), op0=ALU.mult, op1=ALU.add))

                def ev_bg(kind, t0, n, src, tb):
                    if kind == "p":
                        f.op(f.dve, [tb, tY], [tbr[4 + c]], lambda: nc.vector.tensor_tensor(out=br[:, 4 + c, t0:t0 + n], in0=src, in1=Y[:, t0:t0 + n], op=ALU.mult))
                    else:
                        f.op(f.dve, [tb, tY], [tbr[4 + c]], lambda: nc.vector.tensor_tensor(out=br[:, 4 + c, PT:NT], in0=src, in1=Y[:, PT:NT], op=ALU.mult))
                self.proj_split("w_in_t", (l, TI_BG + c), uT, tu, ev_bg)

    def ssm_coefs(self, src, F, st, tsrc):
        nc, f = self.nc, self.f
        import math
        names = ["step", "decay", "th", "x", "m", "sin", "cos", "abre", "abim", "rden", "nr", "core", "coim", "t1"]
        T = {n: f.sbuf("sc_" + n, [128, F], F32, st) for n in names}
        tk = f.tok("ssmcoef")
        are, aim, lst = src[:, :, 0], src[:, :, 1], src[:, :, 2]
        V, A = nc.vector, nc.scalar

        def dv(fn):
            f.op(f.dve, [tk, tsrc], [tk], fn)

        def ac(fn):
            f.op(f.act, [tk, tsrc], [tk], fn)
        ac(lambda: A.activation(out=T["step"][:], in_=lst, func=AF.Exp))
        dv(lambda: V.tensor_tensor(out=T["t1"][:], in0=T["step"][:], in1=are, op=ALU.mult))
        ac(lambda: A.activation(out=T["decay"][:], in_=T["t1"][:], func=AF.Exp))
        dv(lambda: V.tensor_tensor(out=T["th"][:], in0=T["step"][:], in1=aim, op=ALU.mult))
        for name, shift in (("sin", 0.0), ("cos", math.pi / 2)):
            dv(lambda: V.tensor_scalar(out=T["x"][:], in0=T["th"][:], scalar1=shift, scalar2=None, op0=ALU.add))
            for k in (1, 3, 5, 7, 9):
                dv(lambda: V.tensor_scalar(out=T["m"][:], in0=T["th"][:], scalar1=k * math.pi - shift, scalar2=-2 * math.pi,
                                           op0=ALU.is_gt, op1=ALU.mult))
                dv(lambda: V.tensor_tensor(out=T["x"][:], in0=T["x"][:], in1=T["m"][:], op=ALU.add))
            ac(lambda: A.activation(out=T[name][:], in_=T["x"][:], func=AF.Sin))
        dv(lambda: V.tensor_tensor(out=T["abre"][:], in0=T["decay"][:], in1=T["cos"][:], op=ALU.mult))
        dv(lambda: V.tensor_tensor(out=T["abim"][:], in0=T["decay"][:], in1=T["sin"][:], op=ALU.mult))
        dv(lambda: V.tensor_tensor(out=T["t1"][:], in0=are, in1=are, op=ALU.mult))
        dv(lambda: V.tensor_tensor(out=T["m"][:], in0=aim, in1=aim, op=ALU.mult))
        dv(lambda: V.tensor_tensor(out=T["t1"][:], in0=T["t1"][:], in1=T["m"][:], op=ALU.add))
        dv(lambda: V.reciprocal(out=T["rden"][:], in_=T["t1"][:]))
        dv(lambda: V.tensor_scalar(out=T["nr"][:], in0=T["abre"][:], scalar1=-1.0, scalar2=None, op0=ALU.add))
        dv(lambda: V.tensor_tensor(out=T["t1"][:], in0=T["nr"][:], in1=are, op=ALU.mult))
        dv(lambda: V.tensor_tensor(out=T["m"][:], in0=T["abim"][:], in1=aim, op=ALU.mult))
        dv(lambda: V.tensor_tensor(out=T["t1"][:], in0=T["t1"][:], in1=T["m"][:], op=ALU.add))
        dv(lambda: V.tensor_tensor(out=T["core"][:], in0=T["t1"][:], in1=T["rden"][:], op=ALU.mult))
        dv(lambda: V.tensor_tensor(out=T["t1"][:], in0=T["abim"][:], in1=are, op=ALU.mult))
        dv(lambda: V.tensor_tensor(out=T["m"][:], in0=T["nr"][:], in1=aim, op=ALU.mult))
        dv(lambda: V.tensor_tensor(out=T["t1"][:], in0=T["t1"][:], in1=T["m"][:], op=ALU.subtract))
        dv(lambda: V.tensor_tensor(out=T["coim"][:], in0=T["t1"][:], in1=T["rden"][:], op=ALU.mult))
        return T, tk

    def mixer_ssm(self, l, br, tbr):
        nc, f, cfg = self.nc, self.f, self.cfg
        V, A = nc.vector, nc.scalar
        with f.scope() as st:
            xsf = f.sbuf("xsf", [128, 4, NT], BF16, st)
            txs = f.tok("xsf")
            with f.scope() as st2:
                uT = f.sbuf("uT", [128, 16, NT], BF16, st2)
                tu = [f.tok("u%d" % i) for i in range(16)]
                self.apply_norm(uT, tu, (l * 4 + 1) * 16)
                for c in range(4):
                    def ev(ci, c0, cn, bk, tb, c=c):
                        f.op(f.act, [tb], [txs], lambda: A.copy(out=xsf[:, c, c0:c0 + cn], in_=bk[:, 0:cn]))
                    self.proj("w_in_t", (l, TI_XS + c), uT, tu, ev)
            Bl = [f.sbuf("Bl", [128, 16, 128], BF16, st) for _ in range(2)]
            Cd = [f.sbuf("Cd", [128, 16, 128], BF16, st) for _ in range(2)]
            tBl, tCd = f.tok("Bl"), f.tok("Cd")
            ab = f.sbuf("ssab", [128, 16, 2], F32, st)
            pw = f.sbuf("sspw", [128, 16, 10, 3], F32, st)
            s0 = f.sbuf("sss0", [128, 16, 16, 2], F32, st)
            mid = f.sbuf("ssmid", [128, 16, 2], F32, st)
            endp = f.sbuf("ssendp", [128, 16, 2], F32, st)
            ends = f.sbuf("ssends", [128, 16, 16, 2], F32, st)
            tab, tpw, ts0, tmid, tendp, tends = [f.tok(n) for n in ("ssab", "sspw", "sss0", "ssmid", "ssendp", "ssends")]
            with f.scope() as st2:
                sm_d = self.din("ssm_sm", [L, 128, 16, 3])
                cm_d = self.din("ssm_cm", [L, 128, 256, 3])
                bcm_d = self.din("ssm_bcm", [L, 128, 256, 2])
                csm_d = self.din("ssm_csm", [L, 128, 16, 16, 2])
                s0_d = self.din("st_ssm", [L, 128, 16, 16, 2])
                sm = f.sbuf("sm", [128, 16, 3], F32, st2)
                cm = f.sbuf("cm", [128, 256, 3], F32, st2)
                bcm = f.sbuf("bcm", [128, 256, 2], F32, st2)
                csm = f.sbuf("csm", [128, 16, 16, 2], F32, st2)
                bb = f.sbuf("bb", [128, 256, 2], F32, st2)
                tsm, tcm, tbcm, tcsm, tbb = [f.tok(n) for n in ("sm", "cm", "bcm", "csm", "bb")]
                f.dma(f.sp, [], [tsm], lambda: nc.sync.dma_start(out=sm[:], in_=sm_d[l]))
                f.dma(f.sp, [], [tcm], lambda: nc.sync.dma_start(out=cm[:], in_=cm_d[l]))
                f.dma(f.sp, [], [tbcm], lambda: nc.sync.dma_start(out=bcm[:], in_=bcm_d[l]))
                f.dma(f.sp, [], [tcsm], lambda: nc.sync.dma_start(out=csm[:], in_=csm_d[l]))
                f.dma(f.sp, [], [ts0], lambda: nc.sync.dma_start(out=s0[:], in_=s0_d[l]))
                Ts, tks = self.ssm_coefs(sm[:], 16, st2, tsm)
                f.op(f.dve, [tks], [tab], lambda: V.tensor_copy(out=ab[:, :, 0], in_=Ts["abre"][:]))
                f.op(f.dve, [tks], [tab], lambda: V.tensor_copy(out=ab[:, :, 1], in_=Ts["abim"][:]))
                f.op(f.dve, [tab], [tpw], lambda: V.tensor_copy(out=pw[:, :, 0, 0:2], in_=ab[:]))
                t1 = f.sbuf("pwt", [128, 16, 2], F32, st2)
                tt1 = f.tok("pwt")
                for k in range(1, 10):
                    pr, pi = pw[:, :, k - 1, 0], pw[:, :, k - 1, 1]
                    f.op(f.dve, [tpw], [tt1], lambda: V.tensor_tensor(out=t1[:, :, 0], in0=pr, in1=pr, op=ALU.mult))
                    f.op(f.dve, [tpw], [tt1], lambda: V.tensor_tensor(out=t1[:, :, 1], in0=pi, in1=pi, op=ALU.mult))
                    f.op(f.dve, [tt1], [tpw], lambda: V.tensor_tensor(out=pw[:, :, k, 0], in0=t1[:, :, 0], in1=t1[:, :, 1], op=ALU.subtract))
                    f.op(f.dve, [tpw], [tt1], lambda: V.tensor_tensor(out=t1[:, :, 0], in0=pr, in1=pi, op=ALU.mult))
                    f.op(f.dve, [tt1], [tpw], lambda: V.tensor_scalar(out=pw[:, :, k, 1], in0=t1[:, :, 0], scalar1=2.0, scalar2=None, op0=ALU.mult))
                f.op(f.dve, [tpw], [tpw], lambda: V.tensor_scalar(out=pw[:, :, :, 2], in0=pw[:, :, :, 1], scalar1=-1.0, scalar2=None, op0=ALU.mult))
                Tc, tkc = self.ssm_coefs(cm[:], 256, st2, tcm)
                tA_ = f.sbuf("bbt", [128, 256], F32, st2)
                ttA = f.tok("bbt")
                bre, bim = bcm[:, :, 0], bcm[:, :, 1]
                f.op(f.dve, [tkc, tbcm], [tbb], lambda: V.tensor_tensor(out=bb[:, :, 0], in0=Tc["core"][:], in1=bre, op=ALU.mult))
                f.op(f.dve, [tkc, tbcm], [ttA], lambda: V.tensor_tensor(out=tA_[:], in0=Tc["coim"][:], in1=bim, op=ALU.mult))
                f.op(f.dve, [ttA, tbb], [tbb], lambda: V.tensor_tensor(out=bb[:, :, 0], in0=bb[:, :, 0], in1=tA_[:], op=ALU.subtract))
                f.op(f.dve, [tkc, tbcm], [tbb], lambda: V.tensor_tensor(out=bb[:, :, 1], in0=Tc["core"][:], in1=bim, op=ALU.mult))
                f.op(f.dve, [tkc, tbcm], [ttA], lambda: V.tensor_tensor(out=tA_[:], in0=Tc["coim"][:], in1=bre, op=ALU.mult))
                f.op(f.dve, [ttA, tbb], [tbb], lambda: V.tensor_tensor(out=bb[:, :, 1], in0=bb[:, :, 1], in1=tA_[:], op=ALU.add))
                for ri in range(2):
                    f.op(f.dve, [], [tBl], lambda: V.memset(Bl[ri][:], 0.0))
                    f.op(f.dve, [], [tCd], lambda: V.memset(Cd[ri][:], 0.0))
                for j in range(16):
                    base, c = (j % 4) * 32, j // 4
                    for ri in range(2):
                        in0 = self.cust(bb[base:base + 32, c * 64, ri], [(0, 2), (2, 64)])
                        in1 = self.bmask[base:base + 32, :].rearrange("p (a b) -> p a b", b=64)
                        f.op(f.dve, [tbb, self.tcb2], [tBl], lambda: V.tensor_tensor(
                            out=Bl[ri][base:base + 32, j, :].rearrange("p (a b) -> p a b", b=64), in0=in0, in1=in1, op=ALU.mult))
                        a0 = ((2 * j) % 8) * 16
                        for hf in range(2):
                            f.op(f.dve, [tcsm], [tCd], lambda: V.tensor_scalar(
                                out=Cd[ri][hf * 64:(hf + 1) * 64, j, a0 + 16 * hf:a0 + 16 * hf + 16],
                                in0=csm[hf * 64:(hf + 1) * 64, j, :, ri], scalar1=(1.0 if ri == 0 else -1.0), scalar2=None, op0=ALU.mult))
            with f.scope() as st2:
                H = [f.sbuf("ssH", [128, NT], F32, st2) for _ in range(4)]
                tH = [f.tok("ssH%d" % i) for i in range(4)]
                hb = [f.sbuf("sshb", [128, NT], BF16, st2) for _ in range(2)]
                thb = [f.tok("sshb%d" % i) for i in range(2)]
                z = f.sbuf("ssz", [128, 4, NT], BF16, st2)
                tz = f.tok("ssz")
                xsb, txsb = None, txs
                yv, y2, tyv, ty2 = H[2], H[3], tH[2], tH[3]

                def S3(buf, a, b):
                    return buf[:, PT:NT].rearrange("p (s q) -> p s q", q=8)[:, :, a:b]

                self._ssm_ctx = dict(H=H, tH=tH, hb=hb, thb=thb, xsb=xsb, txsb=txsb, yv=yv, y2=y2, tyv=tyv, ty2=ty2, S3=S3,
                                     ab=ab, tab=tab, pw=pw, tpw=tpw, s0=s0, ts0=ts0, mid=mid, tmid=tmid, endp=endp, tendp=tendp,
                                     ends=ends, tends=tends, Bl=Bl, tBl=tBl, Cd=Cd, tCd=tCd, xsf=xsf, txs=txs, z=z, tz=tz)
                self.ssm_scans(l, br, tbr)

    def ssm_scans(self, l, br, tbr):
        nc, f = self.nc, self.f
        V, A = nc.vector, nc.scalar
        c_ = self._ssm_ctx
        H, tH, hb, thb, xsb, txsb, yv, y2, tyv, ty2, S3 = [c_[k] for k in ("H", "tH", "hb", "thb", "xsb", "txsb", "yv", "y2", "tyv", "ty2", "S3")]
        ab, tab, pw, tpw, s0, ts0, mid, tmid, endp, tendp, ends, tends = [c_[k] for k in
            ("ab", "tab", "pw", "tpw", "s0", "ts0", "mid", "tmid", "endp", "tendp", "ends", "tends")]
        Bl, tBl, Cd, tCd, xsf, txs, z, tz = [c_[k] for k in ("Bl", "tBl", "Cd", "tCd", "xsf", "txs", "z", "tz")]
        ybk = [(self.banks[i], self.tbank[i]) for i in range(3)]

        def cadd(dre, dim, tdre, tdim, sre, sim, tsrc, pr, pi, npi, are_in1, aim_in1):
            f.op(f.dve, tsrc + [tpw, tab], [tdre], lambda: V.scalar_tensor_tensor(out=dre, in0=sre, scalar=pr, in1=are_in1, op0=ALU.mult, op1=ALU.add))
            f.op(f.dve, tsrc + [tpw, tab], [tdre], lambda: V.scalar_tensor_tensor(out=dre, in0=sim, scalar=npi, in1=dre, op0=ALU.mult, op1=ALU.add))
            f.op(f.dve, tsrc + [tpw, tab], [tdim], lambda: V.scalar_tensor_tensor(out=dim, in0=sim, scalar=pr, in1=aim_in1, op0=ALU.mult, op1=ALU.add))
            f.op(f.dve, tsrc + [tpw, tab], [tdim], lambda: V.scalar_tensor_tensor(out=dim, in0=sre, scalar=pi, in1=dim, op0=ALU.mult, op1=ALU.add))

        def scan(j, pass2):
            for ri in range(2):
                for (c0, cn) in CH:
                    bk, tb = self.bank(3, 8)
                    self.mm(bk[:, 0:cn], Bl[ri][:, j, :], xsf[:, j // 4, c0:c0 + cn], True, True, [tBl, txs], [tb])
                    f.op(f.act, [tb], [tH[ri]], lambda: A.copy(out=H[ri][:, c0:c0 + cn], in_=bk[:, 0:cn]))
            pr0, pi0, npi0 = pw[:, j, 0, 0:1], pw[:, j, 0, 1:2], pw[:, j, 0, 2:3]
            if not pass2:
                cur, oth = (0, 1), (2, 3)
                n = PT
                for k in range(10):
                    n2 = n // 2
                    pr, pi, npi = pw[:, j, k, 0:1], pw[:, j, k, 1:2], pw[:, j, k, 2:3]
                    sre, sim, dre, dim = H[cur[0]], H[cur[1]], H[oth[0]], H[oth[1]]
                    ev = lambda b: self.cust(b[:, 0:1], [(2, n2)])
                    od = lambda b: self.cust(b[:, 1:2], [(2, n2)])
                    cadd(dre[:, 0:n2], dim[:, 0:n2], tH[oth[0]], tH[oth[1]], ev(sre), ev(sim), [tH[cur[0]], tH[cur[1]]], pr, pi, npi,
                         od(sre), od(sim))
                    cur, oth = oth, cur
                    n = n2
                for ri in range(2):
                    f.op(f.act, [tH[cur[ri]]], [tendp], lambda: A.copy(out=endp[:, j, ri:ri + 1], in_=H[cur[ri]][:, 0:1]))
                return
            if pass2:
                cadd(H[0][:, 0:1], H[1][:, 0:1], tH[0], tH[1], mid[:, j, 0:1], mid[:, j, 1:2], [tmid], pr0, pi0, npi0,
                     H[0][:, 0:1], H[1][:, 0:1])
                cadd(S3(H[0], 0, 1), S3(H[1], 0, 1), tH[0], tH[1], s0[:, j, :, 0:1], s0[:, j, :, 1:2], [ts0], pr0, pi0, npi0,
                     S3(H[0], 0, 1), S3(H[1], 0, 1))
            cur, oth = (0, 1), (2, 3)
            for k in range(10):
                sh = 1 << k
                pr, pi, npi = pw[:, j, k, 0:1], pw[:, j, k, 1:2], pw[:, j, k, 2:3]
                sre, sim, dre, dim = H[cur[0]], H[cur[1]], H[oth[0]], H[oth[1]]
                tsrc = [tH[cur[0]], tH[cur[1]]]
                cadd(dre[:, sh:PT], dim[:, sh:PT], tH[oth[0]], tH[oth[1]], sre[:, 0:PT - sh], sim[:, 0:PT - sh], tsrc, pr, pi, npi,
                     sre[:, sh:PT], sim[:, sh:PT])
                f.op(f.act, [tH[cur[0]]], [tH[oth[0]]], lambda: A.copy(out=dre[:, 0:sh], in_=sre[:, 0:sh]))
                f.op(f.act, [tH[cur[1]]], [tH[oth[1]]], lambda: A.copy(out=dim[:, 0:sh], in_=sim[:, 0:sh]))
                if pass2 and k < 3:
                    cadd(S3(dre, sh, 8), S3(dim, sh, 8), tH[oth[0]], tH[oth[1]], S3(sre, 0, 8 - sh), S3(sim, 0, 8 - sh), tsrc, pr, pi, npi,
                         S3(sre, sh, 8), S3(sim, sh, 8))
                    f.op(f.act, [tH[cur[0]]], [tH[oth[0]]], lambda: A.copy(out=S3(dre, 0, sh), in_=S3(sre, 0, sh)))
                    f.op(f.act, [tH[cur[1]]], [tH[oth[1]]], lambda: A.copy(out=S3(dim, 0, sh), in_=S3(sim, 0, sh)))
                cur, oth = oth, cur
            for ri in range(2):
                f.op(f.act, [tH[ri]], [tendp], lambda: A.copy(out=endp[:, j, ri:ri + 1], in_=H[ri][:, PT - 1:PT]))
            if not pass2:
                return
            for ri in range(2):
                f.op(f.act, [tH[2 + ri]], [tends], lambda: A.copy(out=ends[:, j, :, ri:ri + 1], in_=S3(H[2 + ri], 7, 8)))
                f.op(f.act, [tH[ri]], [thb[ri]], lambda: A.copy(out=hb[ri][:, 0:PT], in_=H[ri][:, 0:PT]))
                f.op(f.dve, [tH[2 + ri]], [thb[ri]], lambda: V.tensor_copy(out=hb[ri][:, PT:NT], in_=H[2 + ri][:, PT:NT]))
            for ri in range(2):
                for ci, (c0, cn) in enumerate(CH):
                    self.mm(ybk[ci][0][:, 0:cn], Cd[ri][:, j, :], hb[ri][:, c0:c0 + cn],
                            (j % 4 == 0 and ri == 0), (j % 4 == 3 and ri == 1), [tCd, thb[ri]], [ybk[ci][1]])

        def load_xsb(c):
            pass
        for j in range(16):
            if j % 4 == 0:
                load_xsb(j // 4)
            scan(j, False)
        halo, thalo, hown, thown = self._halo
        p_d = self.dint("x2_pay_%d" % l, [128, 100], F32)
        g_d = self.dint("x2_gat_%d" % l, [256, 100], F32)
        tpd, tgd = f.tok("x2pay"), f.tok("x2gat")
        f.dma(f.sp, [tendp], [tpd], lambda: nc.sync.dma_start(out=p_d[:, 0:32], in_=endp[:].rearrange("p a b -> p (a b)")), main=tpd)
        f.dma(f.sp, [thown], [tpd], lambda: nc.sync.dma_start(out=p_d[:, 32:100], in_=hown[:]), main=tpd)
        f.dma(f.pool, [tpd], [tgd], lambda: nc.gpsimd.collective_compute(
            "AllGather", ALU.bypass, replica_groups=self.cfg["groups"], ins=[p_d], outs=[g_d]), main=tgd, inc=1)
        f.dma(f.sp, [tgd], [tmid], lambda: nc.sync.dma_start(out=mid[:].rearrange("p a b -> p (a b)"), in_=g_d[0:128, 0:32]), main=tmid)
        f.dma(f.sp, [tgd], [thalo], lambda: nc.sync.dma_start(out=halo[:], in_=g_d[0:128, 32:100]), main=thalo)
        f.op(f.dve, [tmid, self.tcb], [tmid], lambda: V.tensor_scalar(out=mid[:], in0=mid[:], scalar1=self.cbias[:, 257:258], scalar2=None, op0=ALU.mult))
        f.op(f.dve, [thalo, self.tcb], [thalo], lambda: V.tensor_scalar(out=halo[:], in0=halo[:], scalar1=self.cbias[:, 257:258], scalar2=None, op0=ALU.mult))
        for j in range(16):
            c = j // 4
            if j % 4 == 0:
                load_xsb(c)
            scan(j, True)
            if j % 4 == 3:
                for ci, (c0, cn) in enumerate(CH):
                    f.op(f.dve, [ybk[ci][1], txs, self.tcb2], [tyv], lambda: V.scalar_tensor_tensor(
                        out=yv[:, c0:c0 + cn], in0=xsf[:, c, c0:c0 + cn], scalar=self.ssd[:, l * 4 + c:l * 4 + c + 1],
                        in1=ybk[ci][0][:, 0:cn], op0=ALU.mult, op1=ALU.add))
                f.op(f.dve, [tyv], [ty2], lambda: V.tensor_tensor(out=y2[:], in0=yv[:], in1=yv[:], op=ALU.mult))
                f.op(f.dve, [ty2], [ty2], lambda: V.tensor_scalar(out=y2[:], in0=y2[:], scalar1=0.044715, scalar2=1.0, op0=ALU.mult, op1=ALU.add))
                f.op(f.dve, [ty2, tyv], [ty2], lambda: V.tensor_tensor(out=y2[:], in0=y2[:], in1=yv[:], op=ALU.mult))
                f.op(f.act, [ty2], [ty2], lambda: A.activation(out=y2[:], in_=y2[:], func=AF.Sigmoid, scale=1.5957691216057308))
                f.op(f.dve, [ty2, tyv], [tz], lambda: V.tensor_tensor(out=z[:, c, :], in0=y2[:], in1=yv[:], op=ALU.mult))
        osp, tosp = self.dout("o_ssm_p", [L, 128, 32])
        oss, toss = self.dout("o_ssm_s", [L, 128, 512])
        f.dma(f.sp, [tendp], [tosp], lambda: nc.sync.dma_start(out=osp[l], in_=endp[:].rearrange("p a b -> p (a b)")), main=tendp)
        f.dma(f.sp, [tends], [toss], lambda: nc.sync.dma_start(out=oss[l], in_=ends[:].rearrange("p a b c -> p (a b c)")), main=tends)
        for e in range(4):
            wgl, twgl = self.wget("w_glu", (l, e), 512)
            for (c0, cn) in CH:
                bk, tb = self.bank(3, 8)
                for kt in range(4):
                    self.mm(bk[:, 0:cn], wgl[:, kt * 128:(kt + 1) * 128], z[:, kt, c0:c0 + cn], kt == 0, kt == 3, [twgl, tz], [tb])
                f.op(f.act, [tb, self.tcb2], [tyv], lambda: A.activation(out=yv[:, c0:c0 + cn], in_=bk[:, 0:cn], func=AF.Sigmoid,
                                                                         bias=self.glub[:, l * 4 + e:l * 4 + e + 1], scale=1.0))
                f.op(f.dve, [tyv, tz], [tbr[8 + e]], lambda: V.tensor_tensor(out=br[:, 8 + e, c0:c0 + cn], in0=yv[:, c0:c0 + cn], in1=z[:, e, c0:c0 + cn], op=ALU.mult))

    def gate_phase(self, l, br, tbr, ycT, tyc):
        nc, f = self.nc, self.f
        V, A = nc.vector, nc.scalar
        with f.scope() as st:
            uT = f.sbuf("uT", [128, 16, NT], BF16, st)
            tu = [f.tok("u%d" % i) for i in range(16)]
            self.apply_norm(uT, tu, (l * 4 + 1) * 16)
            mg = f.sbuf("mg", [128, 8, NT], BF16, st)
            tmg = [f.tok("mg%d" % i) for i in range(8)]
            m32 = f.sbuf("m32", [128, NT], F32, st)
            sg = f.sbuf("sg", [128, NT], F32, st)
            tm32, tsg = f.tok("m32"), f.tok("sg")

            def branch_rhs(i, kt, c0, cn):
                if i == 0:
                    return br[:, kt, c0:c0 + cn], tbr[kt]
                if i == 1:
                    return br[:, 4 + kt, c0:c0 + cn], tbr[4 + kt]
                if i == 2:
                    return ycT[:, kt, c0:c0 + cn], tyc
                return br[:, 8 + kt, c0:c0 + cn], tbr[8 + kt]
            for half in range(2):
                for dd in range(8):
                    d = half * 8 + dd
                    for i in range(4):
                        wg, twg = self.wget("w_gate", (l, i, d), 2048)
                        gb = [self.bank() for _ in CH]
                        for kt in range(16):
                            for ci, (c0, cn) in enumerate(CH):
                                self.mm(gb[ci][0][:, 0:cn], wg[:, kt * 128:(kt + 1) * 128], uT[:, kt, c0:c0 + cn],
                                        kt == 0, kt == 15, [twg, tu[kt]], [gb[ci][1]])
                        wb, twb = self.wget("w_branch", (l, i, d), 512)
                        ob = [self.bank() for _ in CH]
                        for kt in range(4):
                            for ci, (c0, cn) in enumerate(CH):
                                rhs, trhs = branch_rhs(i, kt, c0, cn)
                                self.mm(ob[ci][0][:, 0:cn], wb[:, kt * 128:(kt + 1) * 128], rhs, kt == 0, kt == 3, [twb, trhs], [ob[ci][1]])
                        for ci, (c0, cn) in enumerate(CH):
                            f.op(f.act, [gb[ci][1], self.tcb2], [tsg], lambda: A.activation(
                                out=sg[:, c0:c0 + cn], in_=gb[ci][0][:, 0:cn], func=AF.Sigmoid,
                                bias=self.bgate[:, (l * 4 + i) * 16 + d:(l * 4 + i) * 16 + d + 1], scale=1.0))
                            if i == 0:
                                f.op(f.dve, [tsg, ob[ci][1]], [tm32], lambda: V.tensor_tensor(out=m32[:, c0:c0 + cn], in0=ob[ci][0][:, 0:cn], in1=sg[:, c0:c0 + cn], op=ALU.mult))
                            else:
                                f.op(f.dve, [tsg, ob[ci][1]], [tsg], lambda: V.tensor_tensor(out=sg[:, c0:c0 + cn], in0=ob[ci][0][:, 0:cn], in1=sg[:, c0:c0 + cn], op=ALU.mult))
                                if i < 3:
                                    f.op(f.dve, [tsg, tm32], [tm32], lambda: V.tensor_tensor(out=m32[:, c0:c0 + cn], in0=m32[:, c0:c0 + cn], in1=sg[:, c0:c0 + cn], op=ALU.add))
                                else:
                                    f.op(f.dve, [tsg, tm32], [tmg[dd]], lambda: V.tensor_tensor(out=mg[:, dd, c0:c0 + cn], in0=m32[:, c0:c0 + cn], in1=sg[:, c0:c0 + cn], op=ALU.add))
                if l == 0 and half == 0:
                    self.dbg_dump("mg0", mg[:], tmg[7], [128, 8, NT], BF16)
                for e in range(16):
                    wo, two = self.wget("w_o", (l, half, e), 1024)
                    for (c0, cn) in CH:
                        bk, tb = self.bank()
                        for dd in range(8):
                            self.mm(bk[:, 0:cn], wo[:, dd * 128:(dd + 1) * 128], mg[:, dd, c0:c0 + cn], dd == 0, dd == 7, [two, tmg[dd]], [tb])
                        f.op(f.dve, [tb, self.th[e]], [self.th[e]], lambda: V.tensor_tensor(
                            out=self.hT[:, e, c0:c0 + cn], in0=bk[:, 0:cn], in1=self.hT[:, e, c0:c0 + cn], op=ALU.add))

    def cross_attention(self, l):
        nc, f = self.nc, self.f
        V, A = nc.vector, nc.scalar
        XS = 128 ** -0.5
        with f.scope() as st_x:
            qx = f.sbuf("qx", [128, 4, NT], BF16, st_x)
            ox = f.sbuf("ox", [128, 4, NT], BF16, st_x)
            tqx, tox = f.tok("qx"), f.tok("ox")
            kxb = f.sbuf("kxb", [128, 4, 256], BF16, st_x)
            vxb = f.sbuf("vxb", [128, 2, 512], BF16, st_x)
            tkx, tvx = f.tok("kxb"), f.tok("vxb")
            with f.scope() as st:
                uT = f.sbuf("uT", [128, 16, NT], BF16, st)
                tu = [f.tok("u%d" % i) for i in range(16)]
                self.rmsnorm(uT, tu, (l * 4 + 2) * 16, st)
                pf = f.sbuf("pfx", [128, NT], F32, st)
                tpf = f.tok("pfx")
                for hh in range(4):
                    def ev(ci, c0, cn, bk, tb):
                        f.op(f.act, [tb], [tpf], lambda: A.copy(out=pf[:, c0:c0 + cn], in_=bk[:, 0:cn]))
                    self.proj("w_xq", (l, hh), uT, tu, ev)
                    self.headnorm(pf[:], tpf, 128, self.xqg[:, l:l + 1], qx[:, hh, :], tqx, st, self.ones_bf)
                memT = self.din("memT", [16, 128, 256])
                mm_ = f.sbuf("memf", [128, 16, 256], F32, st)
                mu = f.sbuf("memu", [128, 16, 256], BF16, st)
                tmm, tmu = f.tok("memf"), f.tok("memu")
                for kt in range(16):
                    f.dma(f.sp, [], [tmm], lambda: nc.sync.dma_start(out=mm_[:, kt, :], in_=memT[kt]))
                msq = f.sbuf("msq", [128, 256], BF16, st)
                mrs = f.sbuf("mrs", [128, 256], F32, st)
                tmsq, tmrs = f.tok("msq"), f.tok("mrs")
                bk, tb = self.bank()
                for kt in range(16):
                    f.op(f.act, [tmm], [tmsq], lambda: A.activation(out=msq[:], in_=mm_[:, kt, :], func=AF.Square))
                    self.mm(bk[:, 0:256], self.ones_bf, msq[:], kt == 0, kt == 15, [tmsq, self.tc], [tb])
                f.op(f.act, [tb, self.teps], [tmrs], lambda: A.activation(out=mrs[:], in_=bk[:, 0:256], func=AF.Sqrt, bias=self.eps_t[:, 0:1], scale=1.0 / D))
                f.op(f.dve, [tmrs], [tmrs], lambda: V.reciprocal(out=mrs[:], in_=mrs[:]))
                for kt in range(16):
                    f.op(f.dve, [tmm, tmrs, self.tcb2], [tmu], lambda: V.scalar_tensor_tensor(
                        out=mu[:, kt, :], in0=mm_[:, kt, :], scalar=self.memg[:, l * 16 + kt:l * 16 + kt + 1], in1=mrs[:], op0=ALU.mult, op1=ALU.mult))
                kxf = f.sbuf("kxf", [128, 4, 256], F32, st)
                vxf = f.sbuf("vxf", [128, 2, 512], F32, st)
                tkf, tvf = f.tok("kxf"), f.tok("vxf")
                ksq = f.sbuf("kxsq", [128, 256], BF16, st)
                tksq = f.tok("kxsq")
                for hh in range(4):
                    wk, twk = self.wget("w_xk", (l, hh), 2048)
                    bk, tb = self.bank()
                    for kt in range(16):
                        self.mm(bk[:, 0:256], wk[:, kt * 128:(kt + 1) * 128], mu[:, kt, :], kt == 0, kt == 15, [twk, tmu], [tb])
                    f.op(f.act, [tb], [tkf], lambda: A.copy(out=kxf[:, hh, :], in_=bk[:, 0:256]))
                    f.op(f.act, [tkf], [tksq], lambda: A.activation(out=ksq[:], in_=kxf[:, hh, :], func=AF.Square))
                    bk2, tb2 = self.bank()
                    self.mm(bk2[:, 0:256], self.ones_bf, ksq[:], True, True, [tksq, self.tc], [tb2])
                    f.op(f.act, [tb2, self.teps], [tmrs], lambda: A.activation(out=mrs[:], in_=bk2[:, 0:256], func=AF.Sqrt, bias=self.eps_t[:, 0:1], scale=1.0 / 128))
                    f.op(f.dve, [tmrs], [tmrs], lambda: V.reciprocal(out=mrs[:], in_=mrs[:]))
                    f.op(f.dve, [tmrs, tkf, self.tcb2], [tkf], lambda: V.scalar_tensor_tensor(
                        out=kxf[:, hh, :], in0=kxf[:, hh, :], scalar=self.xkg[:, l:l + 1], in1=mrs[:], op0=ALU.mult, op1=ALU.mult))
                f.op(f.dve, [tkf], [tkx], lambda: V.tensor_copy(out=kxb[:], in_=kxf[:]))
                for mt in range(2):
                    bk, tb = self.bank()
                    for q4 in range(4):
                        wv, twv = self.wget("w_xv_rhs", (l, q4), 2048)
                        for k4 in range(4):
                            kt = q4 * 4 + k4
                            self.mm(bk[:, 0:512], mu[:, kt, mt * 128:(mt + 1) * 128], wv[:, k4 * 512:(k4 + 1) * 512], kt == 0, kt == 15, [twv, tmu], [tb])
                    f.op(f.act, [tb], [tvf], lambda: A.copy(out=vxf[:, mt, :], in_=bk[:, 0:512]))
                f.op(f.dve, [tvf], [tvx], lambda: V.tensor_copy(out=vxb[:], in_=vxf[:]))
                omk, tomk = self.dout("o_mk", [L, 128, 4, 256])
                omv, tomv = self.dout("o_mv", [L, 128, 2, 512])
                f.dma(f.sp, [tkf], [tomk], lambda: nc.sync.dma_start(out=omk[l], in_=kxf[:]), main=tkf)
                f.dma(f.sp, [tvf], [tomv], lambda: nc.sync.dma_start(out=omv[l], in_=vxf[:]), main=tvf)
            with f.scope() as st:
                E = [f.sbuf("xE", [128, 2, 512], BF16, st) for _ in range(2)]
                tE = [f.tok("xE0"), f.tok("xE1")]
                rd = f.sbuf("xrd", [128, 512], F32, st)
                trd = f.tok("xrd")
                ei = 0
                for hh in range(4):
                    for qc in range(2):
                        q0 = qc * 512
                        e, te = E[ei % 2], tE[ei % 2]
                        ei += 1
                        for mt in range(2):
                            bk, tb = self.bank()
                            self.mm(bk[:, 0:512], kxb[:, hh, mt * 128:(mt + 1) * 128], qx[:, hh, q0:q0 + 512], True, True, [tkx, tqx], [tb])
                            f.op(f.act, [tb], [te], lambda: A.activation(out=e[:, mt, :], in_=bk[:, 0:512], func=AF.Exp, scale=XS))
                        bo, tbo = self.bank()
                        bd_, tbd = self.bank()
                        for mt in range(2):
                            self.mm(bo[:, 0:512], vxb[:, mt, hh * 128:(hh + 1) * 128], e[:, mt, :], mt == 0, mt == 1, [tvx, te], [tbo])
                            self.mm(bd_[:, 0:512], self.ones_bf, e[:, mt, :], mt == 0, mt == 1, [self.tc, te], [tbd])
                        f.op(f.dve, [tbd], [trd], lambda: V.reciprocal(out=rd[:], in_=bd_[:, 0:512]))
                        f.op(f.dve, [tbo, trd], [tox], lambda: V.tensor_tensor(out=ox[:, hh, q0:q0 + 512], in0=bo[:, 0:512], in1=rd[:], op=ALU.mult))
                ck_d = self.din("cmem_k", [L, NSEQ, 2, 128, 512])
                cv_d = self.din("cmem_v", [L, NSEQ, 2, 128, 512])
                MKs = [f.sbuf("xMK", [128, 2, 512], F32, st) for _ in range(2)]
                MVs = [f.sbuf("xMV", [128, 2, 512], F32, st) for _ in range(2)]
                tMKs = [f.tok("xMK0"), f.tok("xMK1")]
                tMVs = [f.tok("xMV0"), f.tok("xMV1")]
                kts_ = [f.sbuf("xkT", [128, 4, 256], BF16, st) for _ in range(2)]
                vbs = [f.sbuf("xvb", [128, 2, 512], BF16, st) for _ in range(2)]
                tkts = [f.tok("xkT0"), f.tok("xkT1")]
                tvbs = [f.tok("xvb0"), f.tok("xvb1")]
                es = [f.sbuf("xes", [128, 64], BF16, st) for _ in range(2)]
                tes = [f.tok("xes0"), f.tok("xes1")]
                rds = f.sbuf("xrds", [128, 32], F32, st)
                trds = f.tok("xrds")
                for s_ in range(NSEQ):
                    b2 = s_ % 2
                    mk, mv, tmk, tmv = MKs[b2], MVs[b2], tMKs[b2], tMVs[b2]
                    f.dma(f.sp, [], [tmk], lambda: nc.sync.dma_start(out=mk[:], in_=ck_d[l, s_].rearrange("t p d -> p t d")))
                    f.dma(f.sp, [], [tmv], lambda: nc.sync.dma_start(out=mv[:], in_=cv_d[l, s_].rearrange("t p d -> p t d")))
                    kt_, vb_, tkt_, tvb_ = kts_[b2], vbs[b2], tkts[b2], tvbs[b2]
                    for mt in range(2):
                        bk, tb = self.bank()
                        for hh in range(4):
                            f.op(f.pe, [tmk, self.tc], [tb], lambda: nc.tensor.transpose(out=bk[:, hh * 128:(hh + 1) * 128], in_=mk[:, mt, hh * 128:(hh + 1) * 128], identity=self.ident_f))
                        f.op(f.act, [tb], [tkt_], lambda: A.copy(out=kt_[:, :, mt * 128:(mt + 1) * 128], in_=bk[:, 0:512].rearrange("p (h m) -> p h m", h=4)))
                    f.op(f.dve, [tmv], [tvb_], lambda: V.tensor_copy(out=vb_[:], in_=mv[:]))
                    c0 = PT + s_ * 8
                    e, te = es[b2], tes[b2]
                    bk, tb = self.bank()
                    for hh in range(4):
                        for mt in range(2):
                            self.mm(bk[:, (hh * 2 + mt) * 8:(hh * 2 + mt + 1) * 8], kt_[:, hh, mt * 128:(mt + 1) * 128], qx[:, hh, c0:c0 + 8], True, True, [tkt_, tqx], [tb])
                    f.op(f.act, [tb], [te], lambda: A.activation(out=e[:], in_=bk[:, 0:64], func=AF.Exp, scale=XS))
                    bo, tbo = self.bank()
                    for hh in range(4):
                        for mt in range(2):
                            self.mm(bo[:, hh * 8:(hh + 1) * 8], vb_[:, mt, hh * 128:(hh + 1) * 128], e[:, (hh * 2 + mt) * 8:(hh * 2 + mt + 1) * 8], mt == 0, mt == 1, [tvb_, te], [tbo], skip_group_check=True)
                        for mt in range(2):
                            self.mm(bo[:, 32 + hh * 8:32 + (hh + 1) * 8], self.ones_bf, e[:, (hh * 2 + mt) * 8:(hh * 2 + mt + 1) * 8], mt == 0, mt == 1, [self.tc, te], [tbo], skip_group_check=True)
                    f.op(f.dve, [tbo], [trds], lambda: V.reciprocal(out=rds[:], in_=bo[:, 32:64]))
                    f.op(f.dve, [tbo, trds], [tox], lambda: V.tensor_tensor(
                        out=self.cust(ox[:, 0, c0:c0 + 8], [(NT, 4), (1, 8)]), in0=bo[:, 0:32].rearrange("p (h q) -> p h q", q=8),
                        in1=rds[:].rearrange("p (h q) -> p h q", q=8), op=ALU.mult))
            for e_ in range(16):
                wxo, twxo = self.wget("w_xo", (l, e_), 512)
                for (c0, cn) in CH:
                    bk, tb = self.bank()
                    for hh in range(4):
                        self.mm(bk[:, 0:cn], wxo[:, hh * 128:(hh + 1) * 128], ox[:, hh, c0:c0 + cn], hh == 0, hh == 3, [twxo, tox], [tb])
                    f.op(f.dve, [tb, self.th[e_]], [self.th[e_]], lambda: V.tensor_tensor(
                        out=self.hT[:, e_, c0:c0 + cn], in0=bk[:, 0:cn], in1=self.hT[:, e_, c0:c0 + cn], op=ALU.add))

    def dint(self, name, shape, dtype=F32):
        if name not in self.dram:
            self.dram[name] = self.nc.dram_tensor(name, list(shape), dtype, kind="Internal").ap()
        return self.dram[name]

    def run(self):
        if "custom_run" in self.cfg:
            return self.cfg["custom_run"](self)
        self.setup()
        for l in range(L):
            if not self.layer(l):
                break
            if self.cfg.get("stop") == "full1":
                break
        if self.cfg.get("debug"):
            for kt in range(16):
                pass
        oy, toy = self.dout("o_y", [16, 128, NT])
        for kt in range(16):
            self.f.dma(self.f.sp, [self.th[kt]], [toy],
                       lambda: self.nc.sync.dma_start(out=oy[kt], in_=self.hT[:, kt, :]), main=self.th[kt])


def build_program(cfg):
    nc0 = bass.Bass("TRN2", target_bir_lowering=False)
    f0 = FW(nc0, dry=True)
    k0 = Kern(nc0, f0, None, cfg)
    k0.run()
    plan = k0.wkeys
    f0.close()
    nc = bass.Bass("TRN2", target_bir_lowering=False)
    f = FW(nc, dry=False)
    k = Kern(nc, f, plan, cfg)
    k.run()
    f.finish(k.outs)
    f.close()
    return nc, k


def tile_k(W):
    K, N = W.shape
    KT, NC = K // 128, N // 128
    return np.ascontiguousarray(W.reshape(KT, 128, NC, 128).transpose(2, 1, 0, 3).reshape(NC, 128, KT * 128))


def rhs_k(W):
    K, N = W.shape
    KT = K // 128
    return np.ascontiguousarray(W.reshape(KT, 128, N).transpose(1, 0, 2).reshape(128, KT * N))


def pad_cols(W, dup=False):
    K, n = W.shape
    out = np.zeros((K, 128), W.dtype)
    out[:, :n] = W
    if dup:
        out[:, 64:64 + n] = W
    return out


def prep_shared(inp):
    sh = {}
    f_in = inp["ffn_in"]
    sh["w_ffn_in"] = np.stack([np.stack([tile_k(f_in[l, w]) for w in range(2)]) for l in range(L)])
    f_out = inp["ffn_out"]
    sh["w_ffn_out"] = np.stack([np.stack([np.stack([tile_k(f_out[l, w, hh * 2048:(hh + 1) * 2048]) for hh in range(2)])
                                          for w in range(2)]) for l in range(L)])
    w_in = inp["w_in"]
    tiles = []
    for l in range(L):
        W = w_in[l]
        cols = []
        for base in (0, 512, 1024, 1536, 2048):
            cols += [W[:, base + i * 128: base + (i + 1) * 128] for i in range(4)]
        cols.append(pad_cols(W[:, 2560:2624], dup=True))
        cols += [W[:, 2688 + i * 128: 2688 + (i + 1) * 128] for i in range(4)]
        cols.append(pad_cols(W[:, 3200:3264], dup=True))
        cols += [W[:, 3272 + i * 128: 3272 + (i + 1) * 128] for i in range(4)]
        Wt = np.concatenate(cols, axis=1)
        tiles.append(tile_k(Wt))
    sh["w_in_t"] = np.stack(tiles)
    sh["w_v_rhs"] = np.stack([rhs_k(w_in[l][:, 2624:2688]) for l in range(L)])
    sh["w_iw_rhs"] = np.stack([rhs_k(w_in[l][:, 3264:3272]) for l in range(L)])
    for l in range(L):
        sh["cache_ik_%d" % l] = inp["cache_idx_k"][l].reshape(20480, 1024)
        sh["cache_k_%d" % l] = inp["cache_attn_k"][l].reshape(20480, 1024)
        sh["cache_v_%d" % l] = inp["cache_attn_v"][l].reshape(20480, 1024)
    sh["w_pool"] = np.ascontiguousarray(inp["pool_mix"])
    sh["w_glu"] = np.stack([tile_k(inp["ssm_glu_w"][l]) for l in range(L)])
    sh["w_gate"] = np.stack([np.stack([tile_k(inp["w_gate"][l][:, i * 2048:(i + 1) * 2048]) for i in range(4)]) for l in range(L)])
    sh["w_branch"] = np.stack([np.stack([tile_k(inp["w_branch"][l, i]) for i in range(4)]) for l in range(L)])
    sh["w_o"] = np.stack([np.stack([tile_k(inp["w_o"][l][hf * 1024:(hf + 1) * 1024]) for hf in range(2)]) for l in range(L)])
    sh["w_xq"] = np.stack([tile_k(inp["w_xq"][l]) for l in range(L)])
    sh["w_xk"] = np.stack([tile_k(inp["w_xk"][l]) for l in range(L)])
    sh["w_xv_rhs"] = np.stack([np.ascontiguousarray(inp["w_xv"][l].reshape(4, 4, 128, 512).transpose(0, 2, 1, 3).reshape(4, 128, 2048)) for l in range(L)])
    sh["w_xo"] = np.stack([tile_k(inp["w_xo"][l]) for l in range(L)])
    are, aim = inp["ssm_a_re"], inp["ssm_a_im"]
    lst = np.broadcast_to(inp["ssm_log_step"][:, :, None], are.shape)
    def sm(a):
        return a.reshape(L, 16, 2, 64).transpose(0, 2, 3, 1).reshape(L, 128, 16)
    def cmaj(a):
        x = a.reshape(L, 4, 8, 64).transpose(0, 2, 1, 3)
        x = np.broadcast_to(x[:, :, None], (L, 8, 16, 4, 64))
        return x.reshape(L, 128, 256)
    sh["ssm_sm"] = np.ascontiguousarray(np.stack([sm(are), sm(aim), sm(lst)], axis=-1))
    sh["ssm_cm"] = np.ascontiguousarray(np.stack([cmaj(are), cmaj(aim), cmaj(lst)], axis=-1))
    def bcm(b):
        return b.reshape(L, 4, 8, 64, 16).transpose(0, 2, 4, 1, 3).reshape(L, 128, 256)
    sh["ssm_bcm"] = np.ascontiguousarray(np.stack([bcm(inp["ssm_b_re"]), bcm(inp["ssm_b_im"])], axis=-1))
    def csm(c_):
        return c_.reshape(L, 16, 2, 16, 64).transpose(0, 2, 4, 1, 3).reshape(L, 128, 16, 16)
    sh["ssm_csm"] = np.ascontiguousarray(np.stack([csm(inp["ssm_c_re"]), csm(inp["ssm_c_im"])], axis=-1))
    c2 = np.zeros((128, 404), np.float32)
    def fm(v, nt):
        return v.reshape(nt, 128).T
    for l in range(L):
        c2[:, 64 + 4 * l:68 + 4 * l] = fm(inp["pool_scale"][l], 4)
        cw = inp["conv_w"][l]
        for c_ in range(4):
            for j in range(3):
                c2[:, 72 + (l * 4 + c_) * 3 + j] = cw[j, c_ * 128:(c_ + 1) * 128]
        c2[:, 96 + 4 * l:100 + 4 * l] = fm(inp["ssm_d"][l], 4)
        c2[:, 104 + 4 * l:108 + 4 * l] = fm(inp["ssm_glu_b"][l], 4)
        c2[:, 112 + l] = inp["xq_norm_g"][l]
        c2[:, 114 + l] = inp["xk_norm_g"][l]
        c2[:, 116 + 16 * l:132 + 16 * l] = fm(inp["mem_norm_g"][l], 16)
        for i in range(4):
            c2[:, 148 + (l * 4 + i) * 16:148 + (l * 4 + i + 1) * 16] = fm(inp["b_gate"][l][i * 2048:(i + 1) * 2048], 16)
    r_ = np.arange(128)
    c2[:, 276:404] = ((r_[:, None] % 32) // 16 == (r_[None, :] // 64)).astype(np.float32)
    sh["cst2_base"] = c2
    c = np.zeros((128, 512), np.float32)
    c[:, 0:128] = 1.0
    c[0:64, 128:192] = 1.0
    c[64:128, 192:256] = 1.0
    c[:, 256:384] = np.eye(128, dtype=np.float32)
    c[0:64, 384 + 64:384 + 128] = np.eye(64, dtype=np.float32)
    c[64:128, 384:384 + 64] = np.eye(64, dtype=np.float32)
    sh["consts_bf"] = c
    ng = inp["norm_g"]
    sh["gains"] = np.ascontiguousarray(ng.reshape(L, 4, 16, 128).transpose(3, 0, 1, 2).reshape(128, L * 4 * 16))
    hp = np.zeros((128, 8 * L), np.float32)
    for l in range(L):
        hp[:, 8 * l + 0] = np.tile(inp["k_norm_g"][l], 2)
        hp[:, 8 * l + 1] = np.tile(inp["q_norm_g"][l], 2)
    sh["headp"] = hp
    return sh


def prep_core(inp, c, sh=None):
    b, hf = c // 2, c % 2
    xp = inp["x_prompt"][b, hf * PT:(hf + 1) * PT]
    xs = inp["x_sample"][c * NSEQ:(c + 1) * NSEQ].reshape(NSEQ * 8, D)
    x = np.concatenate([xp, xs], axis=0)
    d = {"xT": np.ascontiguousarray(x.T.reshape(16, 128, NT))}
    cb = np.zeros((128, 259), np.float32)
    cb[:, 0] = 0.0 if hf == 1 else NEG
    qi = np.arange(128)
    cb[:, 1:129] = np.where(qi[None, :] <= qi[:, None], 0.0, NEG)
    sq, qq = qi // 8, qi % 8
    cb[:, 129:257] = np.where((sq[None, :] == sq[:, None]) & (qq[None, :] <= qq[:, None]), 0.0, NEG)
    cb[:, 257] = 1.0 if hf == 1 else 0.0
    cb[:, 258] = qi % 8
    d["cbias"] = cb
    c2 = sh["cst2_base"].copy()
    for gi in range(4):
        w = 2 ** (gi + 1)
        pos = np.arange(16)
        cnt = np.minimum(pos + 1, w) if hf == 0 else np.full(16, w)
        c2[:, gi * 16:(gi + 1) * 16] = (1.0 / cnt)[None, :]
    d["cst2"] = c2
    d["memT"] = np.ascontiguousarray(inp["mem_prompt"][b].T.reshape(16, 128, 256))
    sl = slice(c * NSEQ, (c + 1) * NSEQ)
    d["cmem_k"] = np.ascontiguousarray(inp["cache_mem_k"][:, sl].reshape(L, NSEQ, 2, 128, 512))
    d["cmem_v"] = np.ascontiguousarray(inp["cache_mem_v"][:, sl].reshape(L, NSEQ, 2, 128, 512))
    d["st_pool"] = np.ascontiguousarray(inp["state_pool"][:, sl].reshape(L, NSEQ, 15, 4, 128).transpose(0, 3, 4, 1, 2))
    d["st_conv"] = np.ascontiguousarray(inp["state_conv"][:, sl].reshape(L, NSEQ, 2, 4, 128).transpose(0, 3, 4, 1, 2))
    def ssm_st(a):
        return a.reshape(L, NSEQ, 16, 2, 64).transpose(0, 3, 4, 2, 1).reshape(L, 128, 16, NSEQ)
    d["st_ssm"] = np.ascontiguousarray(np.stack([ssm_st(inp["state_ssm_re"][:, sl]), ssm_st(inp["state_ssm_im"][:, sl])], axis=-1))
    pt = inp["page_table"][c * NSEQ:(c + 1) * NSEQ]
    d["pt_rep"] = np.ascontiguousarray(pt.T[qi // 8, :]).astype(np.int32)
    return d


_PROGRAM_CACHE = {}


def kernel(**inputs):
    inp = {k: np.asarray(v) for k, v in inputs.items()}
    cfg = {"groups": [[0, 1], [2, 3], [4, 5], [6, 7]]}
    if "prog" not in _PROGRAM_CACHE:
        _PROGRAM_CACHE["prog"] = build_program(cfg)
    nc, k = _PROGRAM_CACHE["prog"]
    sh = prep_shared(inp)
    need = [n for n in k.dram if not n.endswith("__tok") and not n.startswith("o_") and not n.startswith("dbg_")
            and not n.startswith("x1_") and not n.startswith("x2_")]
    in_maps = []
    for c in range(8):
        d = prep_core(inp, c, sh)
        in_maps.append({n: (d[n] if n in d else sh[n]) for n in need})
    res = run_bass_kernel_spmd(nc, in_maps, core_ids=list(range(8)))
    R = res.results
    f32 = np.float32
    B, S = 4, 2048
    y_p = np.zeros((B, S, D), f32); y_s = np.zeros((128, 8, D), f32)
    pk = np.zeros((L, B, S, 64), f32); pv = np.zeros((L, B, S, 64), f32); pik = np.zeros((L, B, S, 64), f32)
    sk = np.zeros((L, 128, 8, 64), f32); sv = np.zeros((L, 128, 8, 64), f32); sik = np.zeros((L, 128, 8, 64), f32)
    pmk = np.zeros((L, B, 256, 4, 128), f32); pmv = np.zeros((L, B, 256, 4, 128), f32)
    ppool = np.zeros((L, B, 15, 512), f32); pconv = np.zeros((L, B, 2, 512), f32)
    pre = np.zeros((L, B, 32, 64), f32); pim = np.zeros((L, B, 32, 64), f32)
    spool = np.zeros((L, 128, 15, 512), f32); sconv = np.zeros((L, 128, 2, 512), f32)
    sre = np.zeros((L, 128, 32, 64), f32); sim_ = np.zeros((L, 128, 32, 64), f32)
    for c in range(8):
        r = R[c]
        b, hf = c // 2, c % 2
        ps = slice(hf * PT, (hf + 1) * PT)
        ss = slice(c * NSEQ, (c + 1) * NSEQ)
        yt = np.asarray(r["o_y"], f32).reshape(D, NT).T
        y_p[b, ps] = yt[:PT]
        y_s[ss] = yt[PT:].reshape(NSEQ, 8, D)
        ok = np.asarray(r["o_k"], f32); oik = np.asarray(r["o_ik"], f32); ov = np.asarray(r["o_v"], f32).reshape(L, NT, 64)
        for l in range(L):
            kt = ok[l].T; ikt = oik[l].T
            pk[l, b, ps] = kt[:PT]; sk[l, ss] = kt[PT:].reshape(NSEQ, 8, 64)
            pik[l, b, ps] = ikt[:PT]; sik[l, ss] = ikt[PT:].reshape(NSEQ, 8, 64)
            pv[l, b, ps] = ov[l, :PT]; sv[l, ss] = ov[l, PT:].reshape(NSEQ, 8, 64)
            sp_ = np.asarray(r["o_pool_s"], f32)[l]
            spool[l, ss] = sp_.transpose(2, 3, 0, 1).reshape(NSEQ, 15, 512)
            sc_ = np.asarray(r["o_conv_s"], f32)[l]
            sconv[l, ss] = sc_.transpose(2, 3, 0, 1).reshape(NSEQ, 2, 512)
            st_ = np.asarray(r["o_ssm_s"], f32)[l].reshape(2, 64, 16, NSEQ, 2)
            st_ = st_.transpose(3, 2, 0, 1, 4).reshape(NSEQ, 32, 64, 2)
            sre[l, ss] = st_[..., 0]; sim_[l, ss] = st_[..., 1]
            if hf == 0:
                mk = np.asarray(r["o_mk"], f32)[l]
                pmk[l, b] = mk.transpose(2, 1, 0)
                mv = np.asarray(r["o_mv"], f32)[l]
                pmv[l, b] = mv.transpose(1, 0, 2).reshape(256, 4, 128)
            else:
                pp = np.asarray(r["o_pool_p"], f32)[l]
                ppool[l, b] = pp.transpose(2, 0, 1).reshape(15, 512)
                pc = np.asarray(r["o_conv_p"], f32)[l]
                pconv[l, b] = pc.transpose(2, 0, 1).reshape(2, 512)
                e_ = np.asarray(r["o_ssm_p"], f32)[l].reshape(2, 64, 16, 2).transpose(2, 0, 1, 3).reshape(32, 64, 2)
                pre[l, b] = e_[..., 0]; pim[l, b] = e_[..., 1]
    return (y_p, y_s, pk, pv, pik, pmk, pmv, ppool, pconv, pre, pim, sk, sv, sik, spool, sconv, sre, sim_)
```

```python
import contextlib
import numpy as np
import concourse.bass as bass
import concourse.mybir as mybir
from concourse.bass_utils import run_bass_kernel_spmd

F32 = mybir.dt.float32
BF16 = mybir.dt.bfloat16
I32 = mybir.dt.int32
ALU = mybir.AluOpType
AF = mybir.ActivationFunctionType
AX = mybir.AxisListType

L = 2
D = 2048
NT = 1152
PT = 1024
NSEQ = 16
CH = [(0, 384), (384, 384), (768, 384)]
NEG = -1.0e30
EPS = 1e-6


class Eng:
    def __init__(self, fw, name, handle, is_pe=False):
        self.fw = fw
        self.name = name
        self.h = handle
        self.is_pe = is_pe
        self.sem = fw.new_sem("e_" + name)
        self.count = 0
        self.seen = {}

    def wait(self, sem, val):
        key = id(sem)
        if self.seen.get(key, 0) >= val:
            return
        self.seen[key] = val
        if not self.fw.dry:
            self.h.wait_ge(sem, val)


class T:
    __slots__ = ("name", "w", "rd", "sem", "ndma", "pend")

    def __init__(self, name):
        self.name = name
        self.w = None
        self.rd = {}
        self.sem = None
        self.ndma = 0
        self.pend = {}


class FW:
    def __init__(self, nc, dry=False):
        self.nc = nc
        self.dry = dry
        self.stack = contextlib.ExitStack()
        self.nsem = 0
        self.pe = Eng(self, "pe", nc.tensor, is_pe=True)
        self.act = Eng(self, "act", nc.scalar)
        self.dve = Eng(self, "dve", nc.vector)
        self.pool = Eng(self, "pool", nc.gpsimd)
        self.sp = Eng(self, "sp", nc.sync)
        self.engs = [self.pe, self.act, self.dve, self.pool, self.sp]
        self.dsems = {}
        self.nalloc = 0

    def new_sem(self, name):
        self.nsem += 1
        if self.dry:
            return object()
        return self.stack.enter_context(self.nc.semaphore(name + "_%d" % self.nsem))

    def sbuf(self, name, shape, dtype, stack=None):
        self.nalloc += 1
        return (stack or self.stack).enter_context(
            self.nc.sbuf_tensor("%s_%d" % (name, self.nalloc), list(shape), dtype))

    def psum(self, name, shape, dtype):
        return self.stack.enter_context(self.nc.psum_tensor(name, list(shape), dtype))

    def tok(self, name="t"):
        return T(name)

    def _deps(self, eng, reads, writes):
        for t in reads:
            if t.w is not None:
                e, c = t.w
                if not (e is eng and eng.is_pe):
                    eng.wait(e.sem, c)
            for sem, val in t.pend.values():
                eng.wait(sem, val)
        for t in writes:
            if t.w is not None:
                e, c = t.w
                if not (e is eng and eng.is_pe):
                    eng.wait(e.sem, c)
            for e, c in t.rd.items():
                if not (e is eng and eng.is_pe):
                    eng.wait(e.sem, c)
            for sem, val in t.pend.values():
                eng.wait(sem, val)

    def op(self, eng, reads, writes, fn):
        self._deps(eng, reads, writes)
        eng.count += 1
        if not self.dry:
            inst = fn()
            inst.then_inc(eng.sem, 1)
        for t in reads:
            t.rd[eng] = eng.count
        for t in writes:
            t.w = (eng, eng.count)
            t.rd = {}

    def dma(self, eng, reads, writes, fn, main=None, inc=16):
        self._deps(eng, reads, writes)
        toks = list(reads) + list(writes)
        if main is None:
            main = writes[0] if writes else reads[0]
        if main.sem is None:
            main.sem = self.new_sem("d_" + main.name)
        if not self.dry:
            inst = fn()
            inst.then_inc(main.sem, inc)
        main.ndma += 1
        ent = (main.sem, inc * main.ndma)
        self.dsems[id(main.sem)] = (main.sem, inc * main.ndma, eng)
        for t in toks + [main]:
            t.pend[id(main.sem)] = ent

    def barrier(self):
        for e in self.engs:
            for x in self.engs:
                if x is not e and x.count:
                    e.wait(x.sem, x.count)
            for sem, val, iss in self.dsems.values():
                if iss is e and e is self.pool:
                    continue
                e.wait(sem, val)

    @contextlib.contextmanager
    def scope(self):
        st = contextlib.ExitStack()
        try:
            yield st
        finally:
            self.barrier()
            st.close()

    def finish(self, out_tokens):
        for t in out_tokens:
            for sem, val in t.pend.values():
                self.sp.wait(sem, val)
        for e in self.engs:
            if e is not self.sp and e.count:
                self.sp.wait(e.sem, e.count)
        for sem, val, iss in self.dsems.values():
            self.sp.wait(sem, val)

    def close(self):
        self.stack.close()


W_IN_TILES = 30
TI_XP, TI_XC, TI_BG, TI_CG, TI_Q, TI_K, TI_IQ, TI_IK, TI_XS = 0, 4, 8, 12, 16, 20, 21, 25, 26


class Kern:
    def __init__(self, nc, f, plan, cfg):
        self.nc = nc
        self.f = f
        self.plan = plan
        self.cfg = cfg
        self.wkeys = []
        self.wi = 0
        self.wissued = 0
        self.outs = []
        self.dram = {}
        self.dbg = {}

    def din(self, name, shape, dtype=F32):
        if name not in self.dram:
            self.dram[name] = self.nc.dram_tensor(name, list(shape), dtype, kind="ExternalInput").ap()
        return self.dram[name]

    def dout(self, name, shape, dtype=F32):
        if name not in self.dram:
            self.dram[name] = self.nc.dram_tensor(name, list(shape), dtype, kind="ExternalOutput").ap()
            t = self.f.tok(name)
            self.outs.append(t)
            self.dram[name + "__tok"] = t
        return self.dram[name], self.dram[name + "__tok"]

    def init_banks(self):
        self.banks = [self.f.psum("bank%d" % i, [128, 512], F32) for i in range(8)]
        self.tbank = [self.f.tok("bank%d" % i) for i in range(8)]
        self.bi = 0

    def bank(self, lo=0, hi=8):
        key = (lo, hi)
        if not hasattr(self, "_bk"):
            self._bk = {}
        i = self._bk.get(key, lo)
        self._bk[key] = lo + (i + 1 - lo) % (hi - lo)
        return self.banks[i], self.tbank[i]

    NSLOT = 4

    def init_wring(self):
        self.wslots = [self.f.sbuf("wslot", [128, 2048], BF16) for _ in range(self.NSLOT)]
        self.twslot = [self.f.tok("wslot%d" % i) for i in range(self.NSLOT)]

    def _issue(self, idx):
        key = self.plan[idx]
        name, index, n = key
        self.din(name, self.WSHAPES[name])
        s = idx % self.NSLOT
        src = self.dram[name]
        for i in index:
            src = src[i]
        slot = self.wslots[s]
        self.f.dma(self.f.pool, [], [self.twslot[s]],
                   lambda: self.nc.gpsimd.dma_start(out=slot[:, 0:n], in_=src))

    WSHAPES = {"w_ffn_in": [L, 2, 64, 128, 2048], "w_ffn_out": [L, 2, 2, 16, 128, 2048],
               "w_in_t": [L, W_IN_TILES, 128, 2048], "w_v_rhs": [L, 128, 1024], "w_iw_rhs": [L, 128, 128],
               "w_pool": [L, 4, 128, 128], "w_glu": [L, 4, 128, 512], "w_gate": [L, 4, 16, 128, 2048],
               "w_branch": [L, 4, 16, 128, 512], "w_o": [L, 2, 16, 128, 1024], "w_xq": [L, 4, 128, 2048],
               "w_xk": [L, 4, 128, 2048], "w_xv_rhs": [L, 4, 128, 2048], "w_xo": [L, 16, 128, 512]}

    def wget(self, name, index, n):
        self.din(name, self.WSHAPES[name])
        key = (name, tuple(index), n)
        if self.f.dry:
            self.wkeys.append(key)
            s = len(self.wkeys) - 1
            return self.wslots[s % self.NSLOT], self.twslot[s % self.NSLOT]
        assert self.plan[self.wi] == key, (self.plan[self.wi], key)
        i = self.wi
        self.wi += 1
        while self.wissued < min(len(self.plan), i + self.NSLOT):
            self._issue(self.wissued)
            self.wissued += 1
        return self.wslots[i % self.NSLOT], self.twslot[i % self.NSLOT]

    def mm(self, out, lhsT, rhs, start, stop, reads, writes, **kw):
        self.f.op(self.f.pe, reads, writes,
                  lambda: self.nc.tensor.matmul(out, lhsT=lhsT, rhs=rhs, start=start, stop=stop, **kw))

    def dbg_dump(self, name, ap, tok, shape, dtype=F32):
        if not self.cfg.get("debug"):
            return
        d, t = self.dout("dbg_" + name, shape, dtype)
        self.f.dma(self.f.sp, [tok], [t], lambda: self.nc.sync.dma_start(out=d, in_=ap), main=tok)

    def setup(self):
        nc, f = self.nc, self.f
        self.init_banks()
        self.init_wring()
        self.hT = f.sbuf("hT", [128, 16, NT], F32)
        self.th = [f.tok("h%d" % i) for i in range(16)]
        xT = self.din("xT", [16, 128, NT])
        for kt in range(16):
            f.dma(f.sp, [], [self.th[kt]], lambda: nc.sync.dma_start(out=self.hT[:, kt, :], in_=xT[kt]))
        cst = self.din("consts_bf", [128, 4 * 128], F32)
        self.c_f32 = f.sbuf("c_f32", [128, 512], F32)
        self.tc = f.tok("consts")
        f.dma(f.sp, [], [self.tc], lambda: nc.sync.dma_start(out=self.c_f32[:], in_=cst))
        self.c_bf = f.sbuf("c_bf", [128, 512], BF16)
        f.op(f.dve, [self.tc], [self.tc], lambda: nc.vector.tensor_copy(out=self.c_bf[:], in_=self.c_f32[:]))
        self.ones_bf = self.c_bf[:, 0:128]
        self.bd64_bf = self.c_bf[:, 128:256]
        self.ident_bf = self.c_bf[:, 256:384]
        self.ident_f = self.c_f32[:, 256:384]
        self.swap_bf = self.c_bf[:, 384:512]
        for nm in ("cache_ik", "cache_k", "cache_v"):
            for l_ in range(L):
                self.din("%s_%d" % (nm, l_), [self.cfg.get("npool8", 20480), 1024])
        sp_ = self.din("gains", [128, L * 4 * 16])
        self.gains = f.sbuf("gains", [128, L * 4 * 16], F32)
        self.tg = f.tok("gains")
        f.dma(f.sp, [], [self.tg], lambda: nc.sync.dma_start(out=self.gains[:], in_=sp_))
        hp = self.din("headp", [128, 8 * L])
        self.headp = f.sbuf("headp", [128, 8 * L], F32)
        self.thp = f.tok("headp")
        f.dma(f.sp, [], [self.thp], lambda: nc.sync.dma_start(out=self.headp[:], in_=hp))
        cb = self.din("cbias", [128, 259])
        self.cbias = f.sbuf("cbias", [128, 259], F32)
        self.tcb = f.tok("cbias")
        f.dma(f.sp, [], [self.tcb], lambda: nc.sync.dma_start(out=self.cbias[:], in_=cb))
        ptr = self.din("pt_rep", [128, 16], I32)
        ptsb = f.sbuf("ptsb", [128, 16], I32)
        self.idx = f.sbuf("idx", [128, 16], I32)
        self.tidx = f.tok("idx")
        f.dma(f.sp, [], [self.tidx], lambda: nc.sync.dma_start(out=ptsb[:], in_=ptr))
        f.op(f.dve, [self.tidx, self.tcb], [self.tidx], lambda: nc.vector.tensor_scalar(
            out=self.idx[:], in0=ptsb[:], scalar1=8.0, scalar2=self.cbias[:, 258:259], op0=ALU.mult, op1=ALU.add))
        c2 = self.din("cst2", [128, 404])
        self.cst2 = f.sbuf("cst2", [128, 404], F32)
        self.tcb2 = f.tok("cst2")
        f.dma(f.sp, [], [self.tcb2], lambda: nc.sync.dma_start(out=self.cst2[:], in_=c2))
        C2 = self.cst2
        self.rcnt = C2[:, 0:64].rearrange("p (a b) -> p a b", b=16)
        self.pscale = C2[:, 64:72]
        self.convw = C2[:, 72:96]
        self.ssd = C2[:, 96:104]
        self.glub = C2[:, 104:112]
        self.xqg = C2[:, 112:114]
        self.xkg = C2[:, 114:116]
        self.memg = C2[:, 116:148]
        self.bgate = C2[:, 148:276]
        self.bmask = C2[:, 276:404]
        self.eps_t = f.sbuf("eps", [128, 1], F32)
        self.teps = f.tok("eps")
        f.op(f.dve, [], [self.teps], lambda: nc.vector.memset(self.eps_t[:], EPS))
        self.rstd = f.sbuf("rstd", [128, NT], F32)
        self.trstd = f.tok("rstd")

    def rmsnorm(self, uT, tu, gcol0, st):
        nc, f = self.nc, self.f
        sq = [f.sbuf("sq", [128, 384], BF16, st) for _ in range(4)]
        tsq = [f.tok("sq%d" % i) for i in range(4)]
        qi = 0
        for (c0, cn) in CH:
            bk, tb = self.bank()
            for kt in range(16):
                s_, ts_ = sq[qi % 4], tsq[qi % 4]
                qi += 1
                f.op(f.act, [self.th[kt]], [ts_],
                     lambda: nc.scalar.activation(out=s_[:], in_=self.hT[:, kt, c0:c0 + cn], func=AF.Square))
                self.mm(bk[:, 0:cn], self.ones_bf, s_[:], kt == 0, kt == 15, [ts_, self.tc], [tb])
            f.op(f.act, [tb, self.teps], [self.trstd],
                 lambda: nc.scalar.activation(out=self.rstd[:, c0:c0 + cn], in_=bk[:, 0:cn], func=AF.Sqrt,
                                              bias=self.eps_t[:, 0:1], scale=1.0 / D))
        f.op(f.dve, [self.trstd], [self.trstd], lambda: nc.vector.reciprocal(out=self.rstd[:], in_=self.rstd[:]))
        for kt in range(16):
            f.op(f.dve, [self.th[kt], self.trstd, self.tg], [tu[kt]],
                 lambda: nc.vector.scalar_tensor_tensor(
                     out=uT[:, kt, :], in0=self.hT[:, kt, :], scalar=self.gains[:, gcol0 + kt:gcol0 + kt + 1],
                     in1=self.rstd[:], op0=ALU.mult, op1=ALU.mult))

    def ffn(self, l, w, uT, tu, st):
        nc, f = self.nc, self.f
        g = f.sbuf("g", [128, 16, NT], BF16, st)
        tgt = [f.tok("g%d" % i) for i in range(16)]
        sa = [f.sbuf("sa", [128, NT], F32, st) for _ in range(2)]
        tsa = [f.tok("sa%d" % i) for i in range(2)]
        for hh in range(2):
            for jt in range(self.cfg.get("ffn_nj", 16)):
                j = hh * 16 + jt
                wa, twa = self.wget("w_ffn_in", (l, w, j), 2048)
                ba = [self.bank() for _ in CH]
                for kt in range(16):
                    for ci, (c0, cn) in enumerate(CH):
                        self.mm(ba[ci][0][:, 0:cn], wa[:, kt * 128:(kt + 1) * 128], uT[:, kt, c0:c0 + cn],
                                kt == 0, kt == 15, [twa, tu[kt]], [ba[ci][1]])
                s_, ts_ = sa[j % 2], tsa[j % 2]
                for ci, (c0, cn) in enumerate(CH):
                    f.op(f.act, [ba[ci][1]], [ts_],
                         lambda: nc.scalar.activation(out=s_[:, c0:c0 + cn], in_=ba[ci][0][:, 0:cn], func=AF.Silu))
                wb, twb = self.wget("w_ffn_in", (l, w, 32 + j), 2048)
                bb = [self.bank() for _ in CH]
                for kt in range(16):
                    for ci, (c0, cn) in enumerate(CH):
                        self.mm(bb[ci][0][:, 0:cn], wb[:, kt * 128:(kt + 1) * 128], uT[:, kt, c0:c0 + cn],
                                kt == 0, kt == 15, [twb, tu[kt]], [bb[ci][1]])
                for ci, (c0, cn) in enumerate(CH):
                    f.op(f.dve, [bb[ci][1], ts_], [tgt[jt]],
                         lambda: nc.vector.tensor_tensor(out=g[:, jt, c0:c0 + cn], in0=bb[ci][0][:, 0:cn],
                                                         in1=s_[:, c0:c0 + cn], op=ALU.mult))
            for e in range(self.cfg.get("ffn_ne", 16)):
                wo, two = self.wget("w_ffn_out", (l, w, hh, e), 2048)
                for ci, (c0, cn) in enumerate(CH):
                    bk, tb = self.bank()
                    nj = self.cfg.get("ffn_nj", 16)
                    for jt in range(nj):
                        self.mm(bk[:, 0:cn], wo[:, jt * 128:(jt + 1) * 128], g[:, jt, c0:c0 + cn],
                                jt == 0, jt == nj - 1, [two, tgt[jt]], [tb])
                    f.op(f.dve, [tb, self.th[e]], [self.th[e]],
                         lambda: nc.vector.scalar_tensor_tensor(
                             out=self.hT[:, e, c0:c0 + cn], in0=bk[:, 0:cn], scalar=0.5,
                             in1=self.hT[:, e, c0:c0 + cn], op0=ALU.mult, op1=ALU.add))

    def proj(self, wname, index, uT, tu, evac):
        w_, tw = self.wget(wname, index, 2048)
        bs = [self.bank() for _ in CH]
        for kt in range(16):
            for ci, (c0, cn) in enumerate(CH):
                self.mm(bs[ci][0][:, 0:cn], w_[:, kt * 128:(kt + 1) * 128], uT[:, kt, c0:c0 + cn],
                        kt == 0, kt == 15, [tw, tu[kt]], [bs[ci][1]])
        for ci, (c0, cn) in enumerate(CH):
            evac(ci, c0, cn, bs[ci][0], bs[ci][1])

    def layer(self, l):
        nc, f = self.nc, self.f
        cfg = self.cfg
        with f.scope() as st:
            uT = f.sbuf("uT", [128, 16, NT], BF16, st)
            tu = [f.tok("u%d" % i) for i in range(16)]
            self.rmsnorm(uT, tu, (l * 4 + 0) * 16, st)
            self.ffn(l, 0, uT, tu, st)
        if cfg.get("stop") == "ffn1":
            return False
        with f.scope() as st_m:
            self._st_m = st_m
            ycT = f.sbuf("ycT", [128, 4, NT], BF16, st_m)
            tyc = f.tok("ycT")
            if cfg.get("skip") or cfg.get("nq", 8) < 8:
                f.op(f.dve, [], [tyc], lambda: nc.vector.memset(ycT[:], 0.0))
            halo = f.sbuf("halo", [128, 68], F32, st_m)
            hown = f.sbuf("hown", [128, 68], F32, st_m)
            thalo, thown = f.tok("halo"), f.tok("hown")
            self._halo = (halo, thalo, hown, thown)
            self.mixer_attention(l, ycT, tyc, hown, thown)
            if l == 0:
                self.dbg_dump("yc%d" % l, ycT[:], tyc, [128, 4, NT], BF16)
            if cfg.get("stop") == "att":
                return False
            br = f.sbuf("br", [128, 12, NT], BF16, st_m)
            tbr = [f.tok("br%d" % i) for i in range(12)]
            self.mixer_ssm(l, br, tbr)
            self.mixer_poolconv(l, br, tbr, halo, thalo)
            if l == 0:
                self.dbg_dump("br%d" % l, br[:], tbr[11], [128, 12, NT], BF16)
            if cfg.get("stop") == "mix":
                return False
            self.gate_phase(l, br, tbr, ycT, tyc)
        if cfg.get("stop") == "gate":
            return False
        self.cross_attention(l)
        if cfg.get("stop") == "xattn":
            return False
        if cfg.get("stop") == "full" and l == 0:
            pass
        with f.scope() as st:
            uT = f.sbuf("uT", [128, 16, NT], BF16, st)
            tu = [f.tok("u%d" % i) for i in range(16)]
            self.rmsnorm(uT, tu, (l * 4 + 3) * 16, st)
            self.ffn(l, 1, uT, tu, st)
        return True

    def headnorm(self, src, tsrc, dh, gcol, out, tout, st, bd):
        nc, f = self.nc, self.f
        if not hasattr(st, "_hn"):
            st._hn = (f.sbuf("hsq", [128, NT], BF16, st), f.sbuf("hrs", [128, NT], F32, st), f.tok("hsq"), f.tok("hrs"))
        sq, rs, tsq, trs = st._hn
        f.op(f.act, [tsrc], [tsq], lambda: nc.scalar.activation(out=sq[:], in_=src, func=AF.Square))
        for (c0, cn) in CH:
            bk, tb = self.bank()
            self.mm(bk[:, 0:cn], bd, sq[:, c0:c0 + cn], True, True, [tsq, self.tc], [tb])
            f.op(f.act, [tb, self.teps], [trs],
                 lambda: nc.scalar.activation(out=rs[:, c0:c0 + cn], in_=bk[:, 0:cn], func=AF.Sqrt,
                                              bias=self.eps_t[:, 0:1], scale=1.0 / dh))
        f.op(f.dve, [trs], [trs], lambda: nc.vector.reciprocal(out=rs[:], in_=rs[:]))
        f.op(f.dve, [trs, tsrc, self.thp], [tout],
             lambda: nc.vector.scalar_tensor_tensor(out=out, in0=src, scalar=gcol, in1=rs[:],
                                                    op0=ALU.mult, op1=ALU.mult))

    def cust(self, ap, dims):
        return bass.AP(ap.tensor, ap.offset, [list(ap.ap[0])] + [list(d) for d in dims])

    def mixer_attention(self, l, ycT, tyc, halo, thalo):
        nc, f, cfg = self.nc, self.f, self.cfg
        PAYW = 2704
        with f.scope() as st_s:
            self._mixer_attention(l, ycT, tyc, halo, thalo, st_s, PAYW)

    def _mixer_attention(self, l, ycT, tyc, halo, thalo, st_s, PAYW):
        nc, f, cfg = self.nc, self.f, self.cfg
        iqa = f.sbuf("iqa", [128, 8, 128], BF16, st_s)
        qa = f.sbuf("qa", [128, 8, 128], BF16, st_s)
        siwbc = f.sbuf("siwbc", [128, 1024], F32, st_s)
        KTn = f.sbuf("KTn", [128, 128], BF16, st_s)
        IKTn = f.sbuf("IKTn", [128, 128], BF16, st_s)
        Vn = f.sbuf("Vn", [128, 65], BF16, st_s)
        tnew = f.tok("newkeys")
        tiqa, tqa, tsiwbc = f.tok("iqa"), f.tok("qa"), f.tok("siwbc")
        with f.scope() as st_a:
            qT = f.sbuf("qT", [128, 4, NT], BF16, st_a)
            iqT = f.sbuf("iqT", [128, 4, NT], BF16, st_a)
            tq, tiq = f.tok("qT"), f.tok("iqT")
            iw_tm = f.sbuf("iw_tm", [128, 9, 8], F32, st_a)
            tiw = f.tok("iw_tm")
            pay = f.sbuf("pay", [128, PAYW], BF16, st_a)
            tpay, tprev = f.tok("pay"), f.tok("prev")
            with f.scope() as st:
                uT = f.sbuf("uT", [128, 16, NT], BF16, st)
                tu = [f.tok("u%d" % i) for i in range(16)]
                self.rmsnorm(uT, tu, (l * 4 + 1) * 16, st)
                vtm = f.sbuf("vtm", [128, 9, 64], F32, st)
                tv = f.tok("vtm")
                pf = f.sbuf("pf", [128, NT], F32, st)
                tpf = f.tok("pf")
                kT, ikT, tkT, tikT = pf, pf, tpf, tpf
                ok, tok_ = self.dout("o_k", [L, 64, NT])
                oik, toik = self.dout("o_ik", [L, 64, NT])
                ov, tov = self.dout("o_v", [L, 9, 128, 64])

                def evac_to(dst, tdst):
                    def ev(ci, c0, cn, bk, tb):
                        f.op(f.act, [tb], [tdst], lambda: nc.scalar.copy(out=dst[:, c0:c0 + cn], in_=bk[:, 0:cn]))
                    return ev
                self.proj("w_in_t", (l, TI_K), uT, tu, evac_to(pf, tpf))
                self.headnorm(pf[:], tpf, 64, self.headp[:, 8 * l:8 * l + 1], pf[:], tpf, st, self.bd64_bf)
                f.dma(f.sp, [tpf], [tok_], lambda: nc.sync.dma_start(out=ok[l], in_=pf[0:64, :]), main=tpf)
                f.op(f.dve, [tpf], [tpay], lambda: nc.vector.tensor_copy(out=pay[:, 0:1024], in_=pf[:, 0:PT]))
                f.op(f.dve, [tpf], [tnew], lambda: nc.vector.tensor_copy(out=KTn[:], in_=pf[:, PT:NT]))
                self.proj("w_in_t", (l, TI_IK), uT, tu, evac_to(pf, tpf))
                f.dma(f.sp, [tpf], [toik], lambda: nc.sync.dma_start(out=oik[l], in_=pf[0:64, :]), main=tpf)
                f.op(f.dve, [tpf], [tpay], lambda: nc.vector.tensor_copy(out=pay[:, 1024:2048], in_=pf[:, 0:PT]))
                f.op(f.dve, [tpf], [tnew], lambda: nc.vector.tensor_copy(out=IKTn[:], in_=pf[:, PT:NT]))
                wv, twv = self.wget("w_v_rhs", (l,), 1024)
                for tt in range(9):
                    bk, tb = self.bank()
                    for kt in range(16):
                        self.mm(bk[:, 0:64], uT[:, kt, tt * 128:(tt + 1) * 128], wv[:, kt * 64:(kt + 1) * 64],
                                kt == 0, kt == 15, [twv, tu[kt]], [tb])
                    f.op(f.act, [tb], [tv], lambda: nc.scalar.copy(out=vtm[:, tt, :], in_=bk[:, 0:64]))
                wiw, twiw = self.wget("w_iw_rhs", (l,), 128)
                for tt in range(9):
                    bk, tb = self.bank()
                    for kt in range(16):
                        self.mm(bk[:, 0:8], uT[:, kt, tt * 128:(tt + 1) * 128], wiw[:, kt * 8:(kt + 1) * 8],
                                kt == 0, kt == 15, [twiw, tu[kt]], [tb])
                    f.op(f.act, [tb], [tiw], lambda: nc.scalar.copy(out=iw_tm[:, tt, :], in_=bk[:, 0:8]))
                f.dma(f.sp, [tv], [tov],
                      lambda: nc.sync.dma_start(out=ov[l].rearrange("t p d -> p t d"), in_=vtm[:]), main=tv)
                vview = pay[:, 2048:2568].rearrange("p (t d) -> p t d", d=65)
                f.op(f.dve, [tv], [tpay], lambda: nc.vector.tensor_copy(out=vview[:, :, 0:64], in_=vtm[:, 0:8, :]))
                f.op(f.dve, [], [tpay], lambda: nc.vector.memset(vview[:, :, 64:65], 1.0))
                f.op(f.dve, [tv], [tnew], lambda: nc.vector.tensor_copy(out=Vn[:, 0:64], in_=vtm[:, 8, :]))
                f.op(f.dve, [], [tnew], lambda: nc.vector.memset(Vn[:, 64:65], 1.0))
                for t in range(4):
                    self.proj("w_in_t", (l, TI_Q + t), uT, tu, evac_to(pf, tpf))
                    self.headnorm(pf[:], tpf, 64, self.headp[:, 8 * l + 1:8 * l + 2], qT[:, t, :], tq, st, self.bd64_bf)

                    def ev_iq(ci, c0, cn, bk, tb, t=t):
                        f.op(f.act, [tb], [tiq], lambda: nc.scalar.copy(out=iqT[:, t, c0:c0 + cn], in_=bk[:, 0:cn]))
                    self.proj("w_in_t", (l, TI_IQ + t), uT, tu, ev_iq)
                payf = halo[:]
                hx = f.sbuf("hx", [128, 3, 16], F32, st)
                thx = f.tok("hx")

                def small_proj(ti, dst):
                    w_, tw = self.wget("w_in_t", (l, ti), 2048)
                    bk, tb = self.bank()
                    for kt in range(16):
                        self.mm(bk[:, 0:16], w_[:, kt * 128:(kt + 1) * 128], uT[:, kt, PT - 16:PT], kt == 0, kt == 15, [tw, tu[kt]], [tb])
                    f.op(f.act, [tb], [thx], lambda: nc.scalar.copy(out=dst, in_=bk[:, 0:16]))
                for t in range(4):
                    small_proj(TI_XP + t, hx[:, 0, :])
                    f.op(f.dve, [thx], [thalo], lambda: nc.vector.tensor_copy(out=payf[:, t * 15:(t + 1) * 15], in_=hx[:, 0, 1:16]))
                    small_proj(TI_XC + t, hx[:, 1, :])
                    small_proj(TI_CG + t, hx[:, 2, :])
                    f.op(f.dve, [thx], [thalo], lambda: nc.vector.tensor_tensor(out=payf[:, 60 + 2 * t:62 + 2 * t], in0=hx[:, 1, 14:16], in1=hx[:, 2, 14:16], op=ALU.mult))
                f.op(f.dve, [], [tpay], lambda: nc.vector.memset(pay[:, 2568:PAYW], 0.0))
            prev = f.sbuf("prev", [128, PAYW], BF16, st_a)
            pay_d = self.dint("x1_pay_%d" % l, [128, PAYW], BF16)
            gat_d = self.dint("x1_gat_%d" % l, [256, PAYW], BF16)
            tpd, tgd = f.tok("x1pay"), f.tok("x1gat")
            f.dma(f.sp, [tpay], [tpd], lambda: nc.sync.dma_start(out=pay_d, in_=pay[:]), main=tpd)
            f.dma(f.pool, [tpd], [tgd], lambda: nc.gpsimd.collective_compute(
                "AllGather", ALU.bypass, replica_groups=self.cfg["groups"], ins=[pay_d], outs=[gat_d]),
                main=tgd, inc=1)
            f.dma(f.sp, [tgd], [tprev], lambda: nc.sync.dma_start(out=prev[:], in_=gat_d[0:128, :]), main=tprev)
            if "P" not in cfg.get("skip", ""):
                with f.scope() as st:
                    self.attention_prompt(l, st, qT, tq, iqT, tiq, iw_tm, tiw, pay, tpay, prev, tprev, ycT, tyc)
            with f.scope() as st:
                Dg = f.sbuf("sDg", [128, 2, 4, 128], F32, st)
                tDg = f.tok("sDg")
                for (src, tsrc, dst, tdst) in ((iqT, tiq, iqa, tiqa), (qT, tq, qa, tqa)):
                    for j in range(4):
                        f.op(f.dve, [tsrc], [tdst], lambda: nc.vector.tensor_copy(out=dst[0:64, 2 * j, :], in_=src[0:64, j, PT:NT]))
                        f.op(f.dve, [tsrc], [tdst], lambda: nc.vector.tensor_copy(out=dst[64:128, 2 * j + 1, :], in_=src[64:128, j, PT:NT]))
                        bk, tb = self.bank(4, 8)
                        self.mm(bk[:, 0:128], self.swap_bf, src[:, j, PT:NT], True, True, [tsrc, self.tc], [tb])
                        f.op(f.act, [tb], [tdst], lambda: nc.scalar.copy(out=dst[0:64, 2 * j + 1, :], in_=bk[0:64, 0:128]))
                        f.op(f.act, [tb], [tdst], lambda: nc.scalar.copy(out=dst[64:128, 2 * j, :], in_=bk[64:128, 0:128]))
                self.iw_broadcast(iw_tm, tiw, 8, Dg, tDg, siwbc, tsiwbc, natural=True)
        if "S" not in cfg.get("skip", ""):
            with f.scope() as st:
                self.attention_sample(l, st, iqa, tiqa, qa, tqa, siwbc, tsiwbc, KTn, IKTn, Vn, tnew, ycT, tyc)

    def iw_broadcast(self, iw_tm, tiw, tt, Dg, tDg, iwbc, tiwbc, natural=False):
        nc, f = self.nc, self.f
        a = iw_tm[:, tt, :]
        in1 = self.cust(a, [(1, 2), (2, 4), (0, 128)])
        if natural:
            in1 = self.cust(a, [(4, 2), (1, 4), (0, 128)])
        idf = self.ident_f
        in0 = self.cust(idf, [(0, 2), (0, 4), (1, 128)])
        f.op(f.dve, [tiw, self.tc], [tDg],
             lambda: nc.vector.tensor_tensor(out=Dg[:], in0=in0, in1=in1, op=ALU.mult))
        dflat = Dg[:].rearrange("p a b q -> p (a b q)")
        for hb in range(2):
            bk, tb = self.bank(4, 8)
            self.mm(bk[:, 0:512], self.c_f32[:, 0:128], dflat[:, hb * 512:(hb + 1) * 512], True, True,
                    [tDg, self.tc], [tb])
            f.op(f.act, [tb], [tiwbc], lambda: nc.scalar.copy(out=iwbc[:, hb * 512:(hb + 1) * 512], in_=bk[:, 0:512]))

    def topk_mask(self, S, tS, WK, tWK, m8, tm8, thr, MK, tMK, nk):
        nc, f = self.nc, self.f
        cur, tcur = S, tS
        for r in range(32):
            f.op(f.dve, [tcur], [tm8], lambda: nc.vector.max(out=m8[:, r * 8:(r + 1) * 8], in_=cur[:, 0:nk]))
            if r < 31:
                f.op(f.dve, [tcur, tm8], [tWK],
                     lambda: nc.vector.match_replace(out=WK[:, 0:nk], in_to_replace=m8[:, r * 8:(r + 1) * 8],
                                                     in_values=cur[:, 0:nk], imm_value=-3.0e38))
                cur, tcur = WK, tWK
        f.op(f.dve, [tm8], [tm8], lambda: nc.vector.tensor_scalar(out=thr[:], in0=m8[:, 255:256], scalar1=-1.0e29,
                                                                   scalar2=None, op0=ALU.max))
        f.op(f.dve, [tS, tm8], [tMK], lambda: nc.vector.tensor_scalar(out=MK[:, 0:nk], in0=S[:, 0:nk], scalar1=thr[:, 0:1],
                                                                       scalar2=None, op0=ALU.is_ge))

    def attention_prompt(self, l, st, qT, tq, iqT, tiq, iw_tm, tiw, pay, tpay, prev, tprev, ycT, tyc):
        nc, f = self.nc, self.f
        ST = f.sbuf("ST", [128, 16, 128], F32, st)
        S = f.sbuf("S", [128, 2048], F32, st)
        m8 = f.sbuf("m8", [128, 256], F32, st)
        thr = f.sbuf("thr", [128, 1], F32, st)
        MK = f.sbuf("MK", [128, 2048], BF16, st)
        MT = f.sbuf("MT", [128, 16, 128], BF16, st)
        RL = f.sbuf("RL", [128, 1024], F32, st)
        E = [f.sbuf("E", [128, 512], BF16, st) for _ in range(2)]
        iwbc = f.sbuf("iwbc", [128, 1024], F32, st)
        Dg = f.sbuf("Dg", [128, 2, 4, 128], F32, st)
        otm = f.sbuf("otm", [128, 512], BF16, st)
        rec = f.sbuf("rec", [128, 8], F32, st)
        tST, tS, tm8, tMK, tMT, tRL, tiwbc, tDg, totm, trec = [f.tok(n) for n in
            ("ST", "S", "m8", "MK", "MT", "RL", "iwbc", "Dg", "otm", "rec")]
        tE = [f.tok("E0"), f.tok("E1")]
        WK = ST[:].rearrange("p a b -> p (a b)")
        pb = self.cbias
        for i in range(self.cfg.get("nq", 8)):
            q0 = i * 128
            kts = list(range(8)) + [8 + j for j in range(i + 1)]
            nk = len(kts) * 128
            self.iw_broadcast(iw_tm, tiw, i, Dg, tDg, iwbc, tiwbc)
            for n, kt in enumerate(kts):
                Ks, tK = (prev, tprev) if kt < 8 else (pay, tpay)
                kc = 1024 + (kt % 8) * 128
                for hb in range(2):
                    bk, tb = self.bank(4, 8)
                    rhs = self.cust(iqT[hb * 64:(hb + 1) * 64, 0, q0:q0 + 128], [(NT, 4), (1, 128)])
                    self.mm(bk[:, 0:512], Ks[hb * 64:(hb + 1) * 64, kc:kc + 128], rhs, True, True, [tK, tiq], [tb])
                    f.op(f.act, [tb], [tRL], lambda: nc.scalar.activation(out=RL[:, hb * 512:(hb + 1) * 512],
                                                                           in_=bk[:, 0:512], func=AF.Relu))
                f.op(f.dve, [tRL, tiwbc], [tRL], lambda: nc.vector.tensor_tensor(out=RL[:], in0=RL[:], in1=iwbc[:], op=ALU.mult))
                f.op(f.dve, [tRL], [tST], lambda: nc.vector.tensor_reduce(
                    out=ST[:, n, :], in_=self.cust(RL[:], [(1, 128), (128, 8)]), axis=AX.X, op=ALU.add))
            for n, kt in enumerate(kts):
                bk, tb = self.bank(4, 8)
                f.op(f.pe, [tST, self.tc], [tb], lambda: nc.tensor.transpose(out=bk[:, 0:128], in_=ST[:, n, :], identity=self.ident_f))
                dst = S[:, n * 128:(n + 1) * 128]
                if kt < 8:
                    f.op(f.act, [tb, self.tcb], [tS], lambda: nc.scalar.activation(out=dst, in_=bk[:, 0:128], func=AF.Identity,
                                                                                   bias=pb[:, 0:1], scale=1.0))
                elif kt == 8 + i:
                    f.op(f.dve, [tb, self.tcb], [tS], lambda: nc.vector.tensor_tensor(out=dst, in0=bk[:, 0:128], in1=pb[:, 1:129], op=ALU.add))
                else:
                    f.op(f.act, [tb], [tS], lambda: nc.scalar.copy(out=dst, in_=bk[:, 0:128]))
            self.topk_mask(S, tS, WK, tST, m8, tm8, thr, MK, tMK, nk)
            for n in range(len(kts)):
                bk, tb = self.bank(4, 8)
                bkb = bk[:].bitcast(BF16)
                f.op(f.pe, [tMK, self.tc], [tb], lambda: nc.tensor.transpose(out=bkb[:, 0:128], in_=MK[:, n * 128:(n + 1) * 128], identity=self.ident_bf))
                f.op(f.act, [tb], [tMT], lambda: nc.scalar.copy(out=MT[:, n, :], in_=bkb[:, 0:128]))
            for hb in range(2):
                accs = [(self.banks[j], self.tbank[j]) for j in range(4)]
                for n, kt in enumerate(kts):
                    Ks, tK = (prev, tprev) if kt < 8 else (pay, tpay)
                    kc = (kt % 8) * 128
                    vb = 2048 + (kt % 8) * 65
                    bk, tb = self.bank(4, 8)
                    rhs = self.cust(qT[hb * 64:(hb + 1) * 64, 0, q0:q0 + 128], [(NT, 4), (1, 128)])
                    self.mm(bk[:, 0:512], Ks[hb * 64:(hb + 1) * 64, kc:kc + 128], rhs, True, True, [tK, tq], [tb])
                    e, te = E[n % 2], tE[n % 2]
                    f.op(f.act, [tb], [te], lambda: nc.scalar.activation(out=e[:], in_=bk[:, 0:512], func=AF.Exp, scale=0.125))
                    ev = e[:].rearrange("p (j q) -> p j q", j=4)
                    f.op(f.dve, [te, tMT], [te], lambda: nc.vector.tensor_tensor(
                        out=ev, in0=ev, in1=self.cust(MT[:, n, :], [(0, 4), (1, 128)]), op=ALU.mult))
                    for j in range(4):
                        self.mm(accs[j][0][:, 0:65], e[:, j * 128:(j + 1) * 128], Ks[:, vb:vb + 65],
                                n == 0, n == len(kts) - 1, [te, tK], [accs[j][1]])
                for j in range(4):
                    c = hb * 4 + j
                    f.op(f.dve, [accs[j][1]], [trec], lambda: nc.vector.reciprocal(out=rec[:, c:c + 1], in_=accs[j][0][:, 64:65]))
                    f.op(f.dve, [accs[j][1], trec], [totm], lambda: nc.vector.tensor_scalar(
                        out=otm[:, (j * 2 + hb) * 64:(j * 2 + hb + 1) * 64], in0=accs[j][0][:, 0:64],
                        scalar1=rec[:, c:c + 1], scalar2=None, op0=ALU.mult))
            for t in range(4):
                bk, tb = self.bank(4, 8)
                bkb = bk[:].bitcast(BF16)
                f.op(f.pe, [totm, self.tc], [tb], lambda: nc.tensor.transpose(out=bkb[:, 0:128], in_=otm[:, t * 128:(t + 1) * 128], identity=self.ident_bf))
                f.op(f.act, [tb], [tyc], lambda: nc.scalar.copy(out=ycT[:, t, q0:q0 + 128], in_=bkb[:, 0:128]))

    def attention_sample(self, l, st, iqa, tiqa, qa, tqa, iwbc, tiwbc, KTn, IKTn, Vn, tnew, ycT, tyc):
        nc, f = self.nc, self.f
        NKT = 17
        ST = f.sbuf("sST", [128, NKT, 128], F32, st)
        S = f.sbuf("sS", [128, NKT * 128], F32, st)
        m8 = f.sbuf("sm8", [128, 256], F32, st)
        thr = f.sbuf("sthr", [128, 1], F32, st)
        MK = f.sbuf("sMK", [128, NKT * 128], BF16, st)
        MT = f.sbuf("sMT", [128, NKT, 128], BF16, st)
        G = [f.sbuf("G", [128, 1024], F32, st) for _ in range(4)]
        tG = [f.tok("G%d" % i) for i in range(4)]
        XT = [f.sbuf("XT", [128, 1024], BF16, st) for _ in range(2)]
        tXT = [f.tok("XT%d" % i) for i in range(2)]
        Va = [f.sbuf("Va", [128, 16, 65], BF16, st) for _ in range(2)]
        tVa = [f.tok("Va%d" % i) for i in range(2)]
        RL = f.sbuf("sRL", [128, NKT, 64], F32, st)
        E = [f.sbuf("sE", [128, NKT, 64], BF16, st) for _ in range(2)]
        tE = [f.tok("sE%d" % i) for i in range(2)]
        oall = f.sbuf("oall", [64, 16, 128], BF16, st)
        rec = f.sbuf("srec", [64, 1], F32, st)
        tST, tS, tm8, tMK, tMT, tRL, toall, trec = [f.tok(n) for n in
            ("sST", "sS", "sm8", "sMK", "sMT", "sRL", "oall", "srec")]
        WK = ST[:].rearrange("p a b -> p (a b)")
        cik = self.dram["cache_ik_%d" % l]
        ck = self.dram["cache_k_%d" % l]
        cv = self.dram["cache_v_%d" % l]
        def gather(cache, s, gi):
            g, tg = G[gi], tG[gi]
            f.dma(f.pool, [self.tidx], [tg], lambda: nc.gpsimd.indirect_dma_start(
                out=g[:], out_offset=None, in_=cache,
                in_offset=bass.IndirectOffsetOnAxis(ap=self.idx[:, s:s + 1], axis=0)))
            return g, tg

        def transpose_keys(g, tg, xt, txt):
            var = self.cfg.get("tkv", "C")
            if var == "B":
                for pr in range(8):
                    bk, tb = self.bank(4, 8)
                    f.op(f.pe, [tg, self.tc], [tb], lambda: nc.tensor.transpose(
                        out=bk[:, 0:128], in_=g[:, pr * 128:(pr + 1) * 128], identity=self.ident_f))
                    f.op(f.act, [tb], [txt], lambda: nc.scalar.copy(out=xt[:, pr * 128:(pr + 1) * 128], in_=bk[:, 0:128]))
                return
            for q4 in range(2):
                bk, tb = self.bank(4, 8)
                for i4 in range(4):
                    pr = q4 * 4 + i4
                    f.op(f.pe, [tg, self.tc], [tb], lambda: nc.tensor.transpose(
                        out=bk[:, i4 * 128:(i4 + 1) * 128], in_=g[:, pr * 128:(pr + 1) * 128],
                        identity=self.ident_f))
                if var == "A":
                    continue
                f.op(f.act, [tb], [txt], lambda: nc.scalar.copy(out=xt[:, q4 * 512:(q4 + 1) * 512], in_=bk[:, 0:512]))

        def key_ops(xt, kt, newT, qsrc, s):
            hp = (kt % 2) * 64 if kt < 16 else 0
            lhsT = xt[hp:hp + 64, (kt // 2) * 128:(kt // 2 + 1) * 128] if kt < 16 else newT[0:64, :]
            rhs = self.cust(qsrc[hp:hp + 64, 0, s * 8:(s + 1) * 8], [(128, 8), (1, 8)])
            return lhsT, rhs

        sstop = self.cfg.get("sstop", 99)
        for s in range(NSEQ if sstop >= 1 else 0):
            g, tg = gather(cik, s, s % 2)
            xt, txt = XT[s % 2], tXT[s % 2]
            if sstop < 2:
                continue
            transpose_keys(g, tg, xt, txt)
            if sstop < 3:
                continue
            for kts in (list(range(0, 16, 2)), list(range(1, 16, 2)), [16]):
                bk, tb = self.bank(4, 8)
                for n, kt in enumerate(kts):
                    lhsT, rhs = key_ops(xt, kt, IKTn, iqa, s)
                    self.mm(bk[:, n * 64:(n + 1) * 64], lhsT, rhs, True, True, [txt, tnew, tiqa], [tb])
                w = len(kts)
                f.op(f.act, [tb], [tRL], lambda: nc.scalar.activation(
                    out=self.cust(RL[:, kts[0], :], [(128 if w > 1 else 64, w), (1, 64)]),
                    in_=bk[:, 0:w * 64].rearrange("p (a b) -> p a b", b=64), func=AF.Relu))
            if sstop < 4:
                continue
            rl4 = RL[:].rearrange("p k (h q) -> p k h q", h=8)
            f.op(f.dve, [tRL, tiwbc], [tRL], lambda: nc.vector.tensor_tensor(
                out=rl4, in0=rl4, in1=self.cust(iwbc[:, s * 8:s * 8 + 8], [(0, NKT), (128, 8), (1, 8)]), op=ALU.mult))
            f.op(f.dve, [tRL], [tST], lambda: nc.vector.tensor_reduce(
                out=ST[:, :, s * 8:(s + 1) * 8], in_=self.cust(RL[:], [(64, NKT), (1, 8), (8, 8)]), axis=AX.X, op=ALU.add))
        if sstop < 5:
            return
        for n in range(NKT):
            bk, tb = self.bank(4, 8)
            f.op(f.pe, [tST, self.tc], [tb], lambda: nc.tensor.transpose(out=bk[:, 0:128], in_=ST[:, n, :], identity=self.ident_f))
            dst = S[:, n * 128:(n + 1) * 128]
            if n == 16:
                f.op(f.dve, [tb, self.tcb], [tS], lambda: nc.vector.tensor_tensor(out=dst, in0=bk[:, 0:128], in1=self.cbias[:, 129:257], op=ALU.add))
            else:
                f.op(f.act, [tb], [tS], lambda: nc.scalar.copy(out=dst, in_=bk[:, 0:128]))
        self.topk_mask(S, tS, WK, tST, m8, tm8, thr, MK, tMK, NKT * 128)
        for n in range(NKT):
            bk, tb = self.bank(4, 8)
            bkb = bk[:].bitcast(BF16)
            f.op(f.pe, [tMK, self.tc], [tb], lambda: nc.tensor.transpose(out=bkb[:, 0:128], in_=MK[:, n * 128:(n + 1) * 128], identity=self.ident_bf))
            f.op(f.act, [tb], [tMT], lambda: nc.scalar.copy(out=MT[:, n, :], in_=bkb[:, 0:128]))
        if sstop < 6:
            return
        for s in range(NSEQ):
            g, tg = gather(ck, s, s % 2)
            gv, tgv = gather(cv, s, 2 + s % 2)
            xt, txt = XT[s % 2], tXT[s % 2]
            transpose_keys(g, tg, xt, txt)
            va, tva = Va[s % 2], tVa[s % 2]
            f.op(f.dve, [tgv], [tva], lambda: nc.vector.tensor_copy(out=va[:, :, 0:64], in_=gv[:].rearrange("p (a b) -> p a b", b=64)))
            f.op(f.dve, [], [tva], lambda: nc.vector.memset(va[:, :, 64:65], 1.0))
            e, te = E[s % 2], tE[s % 2]
            for kts in (list(range(0, 16, 2)), list(range(1, 16, 2)), [16]):
                bk, tb = self.bank(4, 8)
                for n, kt in enumerate(kts):
                    lhsT, rhs = key_ops(xt, kt, KTn, qa, s)
                    self.mm(bk[:, n * 64:(n + 1) * 64], lhsT, rhs, True, True, [txt, tnew, tqa], [tb])
                w = len(kts)
                f.op(f.act, [tb], [te], lambda: nc.scalar.activation(
                    out=self.cust(e[:, kts[0], :], [(128 if w > 1 else 64, w), (1, 64)]),
                    in_=bk[:, 0:w * 64].rearrange("p (a b) -> p a b", b=64), func=AF.Exp, scale=0.125))
            e4 = e[:].rearrange("p k (h q) -> p k h q", h=8)
            f.op(f.dve, [te, tMT], [te], lambda: nc.vector.tensor_tensor(
                out=e4, in0=e4, in1=self.cust(MT[:, 0, s * 8:(s + 1) * 8], [(128, NKT), (0, 8), (1, 8)]), op=ALU.mult))
            acc, tacc = self.banks[0], self.tbank[0]
            for kt in range(NKT):
                rv = va[:, kt, :] if kt < 16 else Vn[:]
                self.mm(acc[0:64, 0:65], e[:, kt, :], rv, kt == 0, kt == NKT - 1, [te, tva, tnew], [tacc])
            f.op(f.dve, [tacc], [trec], lambda: nc.vector.reciprocal(out=rec[:], in_=acc[0:64, 64:65]))
            for d2 in range(2):
                f.op(f.dve, [tacc, trec], [toall], lambda: nc.vector.tensor_scalar(
                    out=oall[:, s, d2 * 64:(d2 + 1) * 64], in0=acc[0:64, 0:64], scalar1=rec[:, 0:1], scalar2=None, op0=ALU.mult))
        if sstop < 7:
            return
        bk, tb = self.bank(4, 8)
        bkb = bk[:].bitcast(BF16)
        for s in range(NSEQ):
            f.op(f.pe, [toall, self.tc], [tb], lambda: nc.tensor.transpose(
                out=bkb[:, s * 64:(s + 1) * 64], in_=oall[:, s, :], identity=self.ident_bf[0:64, 0:64]))
        for h in range(8):
            p0 = (h % 2) * 64
            src = self.cust(bkb[p0:p0 + 64, h * 8:h * 8 + 8], [(64, 16), (1, 8)])
            dst = ycT[p0:p0 + 64, h // 2, PT:NT].rearrange("p (s q) -> p s q", q=8)
            f.op(f.act, [tb], [tyc], lambda: nc.scalar.copy(out=dst, in_=src))

    def apply_norm(self, uT, tu, gcol0):
        nc, f = self.nc, self.f
        for kt in range(16):
            f.op(f.dve, [self.th[kt], self.trstd, self.tg], [tu[kt]],
                 lambda: nc.vector.scalar_tensor_tensor(
                     out=uT[:, kt, :], in0=self.hT[:, kt, :], scalar=self.gains[:, gcol0 + kt:gcol0 + kt + 1],
                     in1=self.rstd[:], op0=ALU.mult, op1=ALU.mult))

    def proj_split(self, wname, index, uT, tu, evac):
        def ev(ci, c0, cn, bk, tb):
            if c0 + cn <= PT:
                evac("p", c0, cn, bk[:, 0:cn], tb)
            else:
                evac("p", c0, PT - c0, bk[:, 0:PT - c0], tb)
                evac("s", 0, 128, bk[:, PT - c0:cn], tb)
        self.proj(wname, index, uT, tu, ev)

    def mixer_poolconv(self, l, br, tbr, halo, thalo):
        nc, f, cfg = self.nc, self.f, self.cfg
        PW = 1040
        with f.scope() as st:
            uT = f.sbuf("uT", [128, 16, NT], BF16, st)
            tu = [f.tok("u%d" % i) for i in range(16)]
            self.apply_norm(uT, tu, (l * 4 + 1) * 16)
            X = f.sbuf("pX", [128, PW + 384], F32, st)
            A = f.sbuf("pA", [128, PW + 384], F32, st)
            B = f.sbuf("pB", [128, PW + 384], F32, st)
            dT = f.sbuf("pdT", [128, NT], BF16, st)
            tmp16 = f.sbuf("ptmp", [128, 16], F32, st)
            tX, tA, tB, tdT, ttmp = [f.tok(n) for n in ("pX", "pA", "pB", "pdT", "ptmp")]
            sp_in = self.din("st_pool", [L, 4, 128, 16, 15])
            sc_in = self.din("st_conv", [L, 4, 128, 16, 2])
            opp, topp = self.dout("o_pool_p", [L, 4, 128, 15])
            ops_, tops = self.dout("o_pool_s", [L, 4, 128, 16, 15])
            ocp, tocp = self.dout("o_conv_p", [L, 4, 128, 2])
            ocs, tocs = self.dout("o_conv_s", [L, 4, 128, 16, 2])

            def SV(buf, a, b):
                return buf[:, PW:PW + 384].rearrange("p (s c) -> p s c", c=24)[:, :, a:b]
            for t in range(4):
                w = 2 ** (t + 1)
                f.op(f.dve, [thalo], [tX], lambda: nc.vector.tensor_copy(out=X[:, 1:16], in_=halo[:, t * 15:(t + 1) * 15]))
                f.dma(f.sp, [], [tX], lambda: nc.sync.dma_start(out=SV(X, 1, 16), in_=sp_in[l, t]))

                def ev_x(kind, t0, n, src, tb):
                    if kind == "p":
                        f.op(f.act, [tb], [tX], lambda: nc.scalar.copy(out=X[:, 16 + t0:16 + t0 + n], in_=src))
                    else:
                        f.op(f.act, [tb], [tX], lambda: nc.scalar.copy(out=SV(X, 16, 24), in_=src.rearrange("p (s q) -> p s q", q=8)))
                self.proj_split("w_in_t", (l, TI_XP + t), uT, tu, ev_x)
                f.dma(f.sp, [tX], [topp], lambda: nc.sync.dma_start(out=opp[l, t], in_=X[:, PW - 15:PW]), main=tX)
                f.dma(f.sp, [tX], [tops], lambda: nc.sync.dma_start(out=ops_[l, t], in_=SV(X, 9, 24)), main=tX)
                src_, tsrc = X, tX
                bufs = [(A, tA), (B, tB)]
                sh = 1
                for lev in range(t + 1):
                    dst, tdst = bufs[lev % 2]
                    lo = 2 * sh
                    f.op(f.dve, [tsrc], [tdst], lambda: nc.vector.tensor_tensor(
                        out=dst[:, lo:PW], in0=src_[:, lo:PW], in1=src_[:, lo - sh:PW - sh], op=ALU.add))
                    f.op(f.dve, [tsrc], [tdst], lambda: nc.vector.tensor_tensor(
                        out=SV(dst, lo, 24), in0=SV(src_, lo, 24), in1=SV(src_, lo - sh, 24 - sh), op=ALU.add))
                    src_, tsrc = dst, tdst
                    sh *= 2
                f.op(f.dve, [tsrc, tX], [tdT], lambda: nc.vector.scalar_tensor_tensor(
                    out=dT[:, 0:PT], in0=src_[:, 16:PW], scalar=1.0 / w, in1=X[:, 16:PW], op0=ALU.mult, op1=ALU.subtract))
                f.op(f.dve, [tsrc, tX], [tdT], lambda: nc.vector.scalar_tensor_tensor(
                    out=dT[:, PT:NT].rearrange("p (s q) -> p s q", q=8), in0=SV(src_, 16, 24), scalar=1.0 / w,
                    in1=SV(X, 16, 24), op0=ALU.mult, op1=ALU.subtract))
                f.op(f.dve, [tsrc, self.tcb2], [ttmp], lambda: nc.vector.tensor_tensor(
                    out=tmp16[:], in0=src_[:, 16:32], in1=self.rcnt[:, t, :], op=ALU.mult))
                f.op(f.dve, [ttmp, tX], [tdT], lambda: nc.vector.tensor_tensor(
                    out=dT[:, 0:16], in0=tmp16[:], in1=X[:, 16:32], op=ALU.subtract))
                wpool, twpool = self.wget("w_pool", (l, t), 128)
                for (c0, cn) in CH:
                    bk, tb = self.bank()
                    self.mm(bk[:, 0:cn], wpool[:, 0:128], dT[:, c0:c0 + cn], True, True, [twpool, tdT], [tb])
                    f.op(f.act, [tb, self.tcb2], [tbr[t]], lambda: nc.scalar.activation(
                        out=br[:, t, c0:c0 + cn], in_=bk[:, 0:cn], func=AF.Identity, scale=self.pscale[:, l * 4 + t:l * 4 + t + 1]))
            VW = 1026
            V = A
            tV = tA
            Y = B
            tY = tB
            CG = X
            tCG = tX

            def VS(a, b):
                return V[:, VW:VW + 160].rearrange("p (s c) -> p s c", c=10)[:, :, a:b]

            def YS():
                return Y[:, PT:NT].rearrange("p (s q) -> p s q", q=8)
            for c in range(4):
                f.op(f.dve, [thalo], [tV], lambda: nc.vector.tensor_copy(out=V[:, 0:2], in_=halo[:, 60 + 2 * c:62 + 2 * c]))
                f.dma(f.sp, [], [tV], lambda: nc.sync.dma_start(out=VS(0, 2), in_=sc_in[l, c]))

                def ev_cg(kind, t0, n, src, tb):
                    o = CG[:, t0:t0 + n] if kind == "p" else CG[:, PT:NT]
                    f.op(f.act, [tb], [tCG], lambda: nc.scalar.copy(out=o, in_=src))
                self.proj_split("w_in_t", (l, TI_CG + c), uT, tu, ev_cg)

                def ev_xc(kind, t0, n, src, tb):
                    if kind == "p":
                        f.op(f.dve, [tb, tCG], [tV], lambda: nc.vector.tensor_tensor(out=V[:, 2 + t0:2 + t0 + n], in0=src, in1=CG[:, t0:t0 + n], op=ALU.mult))
                    else:
                        f.op(f.dve, [tb, tCG], [tV], lambda: nc.vector.tensor_tensor(
                            out=VS(2, 10), in0=src.rearrange("p (s q) -> p s q", q=8),
                            in1=CG[:, PT:NT].rearrange("p (s q) -> p s q", q=8), op=ALU.mult))
                self.proj_split("w_in_t", (l, TI_XC + c), uT, tu, ev_xc)
                f.dma(f.sp, [tV], [tocp], lambda: nc.sync.dma_start(out=ocp[l, c], in_=V[:, VW - 2:VW]), main=tV)
                f.dma(f.sp, [tV], [tocs], lambda: nc.sync.dma_start(out=ocs[l, c], in_=VS(8, 10)), main=tV)
                cw = lambda j: self.convw[:, (l * 4 + c) * 3 + j:(l * 4 + c) * 3 + j + 1]
                f.op(f.dve, [tV, self.tcb2], [tY], lambda: nc.vector.tensor_scalar(out=Y[:, 0:PT], in0=V[:, 2:VW], scalar1=cw(2), scalar2=None, op0=ALU.mult))
                f.op(f.dve, [tV, self.tcb2], [tY], lambda: nc.vector.tensor_scalar(out=YS(), in0=VS(2, 10), scalar1=cw(2), scalar2=None, op0=ALU.mult))
                for j, sh in ((1, 1), (0, 2)):
                    f.op(f.dve, [tV, tY, self.tcb2], [tY], lambda: nc.vector.scalar_tensor_tensor(
                        out=Y[:, 0:PT], in0=V[:, 2 - sh:VW - sh], scalar=cw(j), in1=Y[:, 0:PT], op0=ALU.mult, op1=ALU.add))
                    f.op(f.dve, [tV, tY, self.tcb2], [tY], lambda: nc.vector.scalar_tensor_tensor(
                        out=YS(), in0=VS(2 - sh, 10 - sh), scalar=cw(j), in1=YS(), op0=ALU.mult, op1=ALU.add))

                def ev_bg(kind, t0, n, src, tb):
                    if kind == "p":
                        f.op(f.dve, [tb, tY], [tbr[4 + c]], lambda: nc.vector.tensor_tensor(out=br[:, 4 + c, t0:t0 + n], in0=src, in1=Y[:, t0:t0 + n], op=ALU.mult))
                    else:
                        f.op(f.dve, [tb, tY], [tbr[4 + c]], lambda: nc.vector.tensor_tensor(out=br[:, 4 + c, PT:NT], in0=src, in1=Y[:, PT:NT], op=ALU.mult))
                self.proj_split("w_in_t", (l, TI_BG + c), uT, tu, ev_bg)

    def ssm_coefs(self, src, F, st, tsrc):
        nc, f = self.nc, self.f
        import math
        names = ["step", "decay", "th", "x", "m", "sin", "cos", "abre", "abim", "rden", "nr", "core", "coim", "t1"]
        T = {n: f.sbuf("sc_" + n, [128, F], F32, st) for n in names}
        tk = f.tok("ssmcoef")
        are, aim, lst = src[:, :, 0], src[:, :, 1], src[:, :, 2]
        V, A = nc.vector, nc.scalar

        def dv(fn):
            f.op(f.dve, [tk, tsrc], [tk], fn)

        def ac(fn):
            f.op(f.act, [tk, tsrc], [tk], fn)
        ac(lambda: A.activation(out=T["step"][:], in_=lst, func=AF.Exp))
        dv(lambda: V.tensor_tensor(out=T["t1"][:], in0=T["step"][:], in1=are, op=ALU.mult))
        ac(lambda: A.activation(out=T["decay"][:], in_=T["t1"][:], func=AF.Exp))
        dv(lambda: V.tensor_tensor(out=T["th"][:], in0=T["step"][:], in1=aim, op=ALU.mult))
        for name, shift in (("sin", 0.0), ("cos", math.pi / 2)):
            dv(lambda: V.tensor_scalar(out=T["x"][:], in0=T["th"][:], scalar1=shift, scalar2=None, op0=ALU.add))
            for k in (1, 3, 5, 7, 9):
                dv(lambda: V.tensor_scalar(out=T["m"][:], in0=T["th"][:], scalar1=k * math.pi - shift, scalar2=-2 * math.pi,
                                           op0=ALU.is_gt, op1=ALU.mult))
                dv(lambda: V.tensor_tensor(out=T["x"][:], in0=T["x"][:], in1=T["m"][:], op=ALU.add))
            ac(lambda: A.activation(out=T[name][:], in_=T["x"][:], func=AF.Sin))
        dv(lambda: V.tensor_tensor(out=T["abre"][:], in0=T["decay"][:], in1=T["cos"][:], op=ALU.mult))
        dv(lambda: V.tensor_tensor(out=T["abim"][:], in0=T["decay"][:], in1=T["sin"][:], op=ALU.mult))
        dv(lambda: V.tensor_tensor(out=T["t1"][:], in0=are, in1=are, op=ALU.mult))
        dv(lambda: V.tensor_tensor(out=T["m"][:], in0=aim, in1=aim, op=ALU.mult))
        dv(lambda: V.tensor_tensor(out=T["t1"][:], in0=T["t1"][:], in1=T["m"][:], op=ALU.add))
        dv(lambda: V.reciprocal(out=T["rden"][:], in_=T["t1"][:]))
        dv(lambda: V.tensor_scalar(out=T["nr"][:], in0=T["abre"][:], scalar1=-1.0, scalar2=None, op0=ALU.add))
        dv(lambda: V.tensor_tensor(out=T["t1"][:], in0=T["nr"][:], in1=are, op=ALU.mult))
        dv(lambda: V.tensor_tensor(out=T["m"][:], in0=T["abim"][:], in1=aim, op=ALU.mult))
        dv(lambda: V.tensor_tensor(out=T["t1"][:], in0=T["t1"][:], in1=T["m"][:], op=ALU.add))
        dv(lambda: V.tensor_tensor(out=T["core"][:], in0=T["t1"][:], in1=T["rden"][:], op=ALU.mult))
        dv(lambda: V.tensor_tensor(out=T["t1"][:], in0=T["abim"][:], in1=are, op=ALU.mult))
        dv(lambda: V.tensor_tensor(out=T["m"][:], in0=T["nr"][:], in1=aim, op=ALU.mult))
        dv(lambda: V.tensor_tensor(out=T["t1"][:], in0=T["t1"][:], in1=T["m"][:], op=ALU.subtract))
        dv(lambda: V.tensor_tensor(out=T["coim"][:], in0=T["t1"][:], in1=T["rden"][:], op=ALU.mult))
        return T, tk

    def mixer_ssm(self, l, br, tbr):
        nc, f, cfg = self.nc, self.f, self.cfg
        V, A = nc.vector, nc.scalar
        with f.scope() as st:
            xsf = f.sbuf("xsf", [128, 4, NT], BF16, st)
            txs = f.tok("xsf")
            with f.scope() as st2:
                uT = f.sbuf("uT", [128, 16, NT], BF16, st2)
                tu = [f.tok("u%d" % i) for i in range(16)]
                self.apply_norm(uT, tu, (l * 4 + 1) * 16)
                for c in range(4):
                    def ev(ci, c0, cn, bk, tb, c=c):
                        f.op(f.act, [tb], [txs], lambda: A.copy(out=xsf[:, c, c0:c0 + cn], in_=bk[:, 0:cn]))
                    self.proj("w_in_t", (l, TI_XS + c), uT, tu, ev)
            Bl = [f.sbuf("Bl", [128, 16, 128], BF16, st) for _ in range(2)]
            Cd = [f.sbuf("Cd", [128, 16, 128], BF16, st) for _ in range(2)]
            tBl, tCd = f.tok("Bl"), f.tok("Cd")
            ab = f.sbuf("ssab", [128, 16, 2], F32, st)
            pw = f.sbuf("sspw", [128, 16, 10, 3], F32, st)
            s0 = f.sbuf("sss0", [128, 16, 16, 2], F32, st)
            mid = f.sbuf("ssmid", [128, 16, 2], F32, st)
            endp = f.sbuf("ssendp", [128, 16, 2], F32, st)
            ends = f.sbuf("ssends", [128, 16, 16, 2], F32, st)
            tab, tpw, ts0, tmid, tendp, tends = [f.tok(n) for n in ("ssab", "sspw", "sss0", "ssmid", "ssendp", "ssends")]
            with f.scope() as st2:
                sm_d = self.din("ssm_sm", [L, 128, 16, 3])
                cm_d = self.din("ssm_cm", [L, 128, 256, 3])
                bcm_d = self.din("ssm_bcm", [L, 128, 256, 2])
                csm_d = self.din("ssm_csm", [L, 128, 16, 16, 2])
                s0_d = self.din("st_ssm", [L, 128, 16, 16, 2])
                sm = f.sbuf("sm", [128, 16, 3], F32, st2)
                cm = f.sbuf("cm", [128, 256, 3], F32, st2)
                bcm = f.sbuf("bcm", [128, 256, 2], F32, st2)
                csm = f.sbuf("csm", [128, 16, 16, 2], F32, st2)
                bb = f.sbuf("bb", [128, 256, 2], F32, st2)
                tsm, tcm, tbcm, tcsm, tbb = [f.tok(n) for n in ("sm", "cm", "bcm", "csm", "bb")]
                f.dma(f.sp, [], [tsm], lambda: nc.sync.dma_start(out=sm[:], in_=sm_d[l]))
                f.dma(f.sp, [], [tcm], lambda: nc.sync.dma_start(out=cm[:], in_=cm_d[l]))
                f.dma(f.sp, [], [tbcm], lambda: nc.sync.dma_start(out=bcm[:], in_=bcm_d[l]))
                f.dma(f.sp, [], [tcsm], lambda: nc.sync.dma_start(out=csm[:], in_=csm_d[l]))
                f.dma(f.sp, [], [ts0], lambda: nc.sync.dma_start(out=s0[:], in_=s0_d[l]))
                Ts, tks = self.ssm_coefs(sm[:], 16, st2, tsm)
                f.op(f.dve, [tks], [tab], lambda: V.tensor_copy(out=ab[:, :, 0], in_=Ts["abre"][:]))
                f.op(f.dve, [tks], [tab], lambda: V.tensor_copy(out=ab[:, :, 1], in_=Ts["abim"][:]))
                f.op(f.dve, [tab], [tpw], lambda: V.tensor_copy(out=pw[:, :, 0, 0:2], in_=ab[:]))
                t1 = f.sbuf("pwt", [128, 16, 2], F32, st2)
                tt1 = f.tok("pwt")
                for k in range(1, 10):
                    pr, pi = pw[:, :, k - 1, 0], pw[:, :, k - 1, 1]
                    f.op(f.dve, [tpw], [tt1], lambda: V.tensor_tensor(out=t1[:, :, 0], in0=pr, in1=pr, op=ALU.mult))
                    f.op(f.dve, [tpw], [tt1], lambda: V.tensor_tensor(out=t1[:, :, 1], in0=pi, in1=pi, op=ALU.mult))
                    f.op(f.dve, [tt1], [tpw], lambda: V.tensor_tensor(out=pw[:, :, k, 0], in0=t1[:, :, 0], in1=t1[:, :, 1], op=ALU.subtract))
                    f.op(f.dve, [tpw], [tt1], lambda: V.tensor_tensor(out=t1[:, :, 0], in0=pr, in1=pi, op=ALU.mult))
                    f.op(f.dve, [tt1], [tpw], lambda: V.tensor_scalar(out=pw[:, :, k, 1], in0=t1[:, :, 0], scalar1=2.0, scalar2=None, op0=ALU.mult))
                f.op(f.dve, [tpw], [tpw], lambda: V.tensor_scalar(out=pw[:, :, :, 2], in0=pw[:, :, :, 1], scalar1=-1.0, scalar2=None, op0=ALU.mult))
                Tc, tkc = self.ssm_coefs(cm[:], 256, st2, tcm)
                tA_ = f.sbuf("bbt", [128, 256], F32, st2)
                ttA = f.tok("bbt")
                bre, bim = bcm[:, :, 0], bcm[:, :, 1]
                f.op(f.dve, [tkc, tbcm], [tbb], lambda: V.tensor_tensor(out=bb[:, :, 0], in0=Tc["core"][:], in1=bre, op=ALU.mult))
                f.op(f.dve, [tkc, tbcm], [ttA], lambda: V.tensor_tensor(out=tA_[:], in0=Tc["coim"][:], in1=bim, op=ALU.mult))
                f.op(f.dve, [ttA, tbb], [tbb], lambda: V.tensor_tensor(out=bb[:, :, 0], in0=bb[:, :, 0], in1=tA_[:], op=ALU.subtract))
                f.op(f.dve, [tkc, tbcm], [tbb], lambda: V.tensor_tensor(out=bb[:, :, 1], in0=Tc["core"][:], in1=bim, op=ALU.mult))
                f.op(f.dve, [tkc, tbcm], [ttA], lambda: V.tensor_tensor(out=tA_[:], in0=Tc["coim"][:], in1=bre, op=ALU.mult))
                f.op(f.dve, [ttA, tbb], [tbb], lambda: V.tensor_tensor(out=bb[:, :, 1], in0=bb[:, :, 1], in1=tA_[:], op=ALU.add))
                for ri in range(2):
                    f.op(f.dve, [], [tBl], lambda: V.memset(Bl[ri][:], 0.0))
                    f.op(f.dve, [], [tCd], lambda: V.memset(Cd[ri][:], 0.0))
                for j in range(16):
                    base, c = (j % 4) * 32, j // 4
                    for ri in range(2):
                        in0 = self.cust(bb[base:base + 32, c * 64, ri], [(0, 2), (2, 64)])
                        in1 = self.bmask[base:base + 32, :].rearrange("p (a b) -> p a b", b=64)
                        f.op(f.dve, [tbb, self.tcb2], [tBl], lambda: V.tensor_tensor(
                            out=Bl[ri][base:base + 32, j, :].rearrange("p (a b) -> p a b", b=64), in0=in0, in1=in1, op=ALU.mult))
                        a0 = ((2 * j) % 8) * 16
                        for hf in range(2):
                            f.op(f.dve, [tcsm], [tCd], lambda: V.tensor_scalar(
                                out=Cd[ri][hf * 64:(hf + 1) * 64, j, a0 + 16 * hf:a0 + 16 * hf + 16],
                                in0=csm[hf * 64:(hf + 1) * 64, j, :, ri], scalar1=(1.0 if ri == 0 else -1.0), scalar2=None, op0=ALU.mult))
            with f.scope() as st2:
                H = [f.sbuf("ssH", [128, NT], F32, st2) for _ in range(4)]
                tH = [f.tok("ssH%d" % i) for i in range(4)]
                hb = [f.sbuf("sshb", [128, NT], BF16, st2) for _ in range(2)]
                thb = [f.tok("sshb%d" % i) for i in range(2)]
                z = f.sbuf("ssz", [128, 4, NT], BF16, st2)
                tz = f.tok("ssz")
                xsb, txsb = None, txs
                yv, y2, tyv, ty2 = H[2], H[3], tH[2], tH[3]

                def S3(buf, a, b):
                    return buf[:, PT:NT].rearrange("p (s q) -> p s q", q=8)[:, :, a:b]

                self._ssm_ctx = dict(H=H, tH=tH, hb=hb, thb=thb, xsb=xsb, txsb=txsb, yv=yv, y2=y2, tyv=tyv, ty2=ty2, S3=S3,
                                     ab=ab, tab=tab, pw=pw, tpw=tpw, s0=s0, ts0=ts0, mid=mid, tmid=tmid, endp=endp, tendp=tendp,
                                     ends=ends, tends=tends, Bl=Bl, tBl=tBl, Cd=Cd, tCd=tCd, xsf=xsf, txs=txs, z=z, tz=tz)
                self.ssm_scans(l, br, tbr)

    def ssm_scans(self, l, br, tbr):
        nc, f = self.nc, self.f
        V, A = nc.vector, nc.scalar
        c_ = self._ssm_ctx
        H, tH, hb, thb, xsb, txsb, yv, y2, tyv, ty2, S3 = [c_[k] for k in ("H", "tH", "hb", "thb", "xsb", "txsb", "yv", "y2", "tyv", "ty2", "S3")]
        ab, tab, pw, tpw, s0, ts0, mid, tmid, endp, tendp, ends, tends = [c_[k] for k in
            ("ab", "tab", "pw", "tpw", "s0", "ts0", "mid", "tmid", "endp", "tendp", "ends", "tends")]
        Bl, tBl, Cd, tCd, xsf, txs, z, tz = [c_[k] for k in ("Bl", "tBl", "Cd", "tCd", "xsf", "txs", "z", "tz")]
        ybk = [(self.banks[i], self.tbank[i]) for i in range(3)]

        def cadd(dre, dim, tdre, tdim, sre, sim, tsrc, pr, pi, npi, are_in1, aim_in1):
            f.op(f.dve, tsrc + [tpw, tab], [tdre], lambda: V.scalar_tensor_tensor(out=dre, in0=sre, scalar=pr, in1=are_in1, op0=ALU.mult, op1=ALU.add))
            f.op(f.dve, tsrc + [tpw, tab], [tdre], lambda: V.scalar_tensor_tensor(out=dre, in0=sim, scalar=npi, in1=dre, op0=ALU.mult, op1=ALU.add))
            f.op(f.dve, tsrc + [tpw, tab], [tdim], lambda: V.scalar_tensor_tensor(out=dim, in0=sim, scalar=pr, in1=aim_in1, op0=ALU.mult, op1=ALU.add))
            f.op(f.dve, tsrc + [tpw, tab], [tdim], lambda: V.scalar_tensor_tensor(out=dim, in0=sre, scalar=pi, in1=dim, op0=ALU.mult, op1=ALU.add))

        def scan(j, pass2):
            for ri in range(2):
                for (c0, cn) in CH:
                    bk, tb = self.bank(3, 8)
                    self.mm(bk[:, 0:cn], Bl[ri][:, j, :], xsf[:, j // 4, c0:c0 + cn], True, True, [tBl, txs], [tb])
                    f.op(f.act, [tb], [tH[ri]], lambda: A.copy(out=H[ri][:, c0:c0 + cn], in_=bk[:, 0:cn]))
            pr0, pi0, npi0 = pw[:, j, 0, 0:1], pw[:, j, 0, 1:2], pw[:, j, 0, 2:3]
            if not pass2:
                cur, oth = (0, 1), (2, 3)
                n = PT
                for k in range(10):
                    n2 = n // 2
                    pr, pi, npi = pw[:, j, k, 0:1], pw[:, j, k, 1:2], pw[:, j, k, 2:3]
                    sre, sim, dre, dim = H[cur[0]], H[cur[1]], H[oth[0]], H[oth[1]]
                    ev = lambda b: self.cust(b[:, 0:1], [(2, n2)])
                    od = lambda b: self.cust(b[:, 1:2], [(2, n2)])
                    cadd(dre[:, 0:n2], dim[:, 0:n2], tH[oth[0]], tH[oth[1]], ev(sre), ev(sim), [tH[cur[0]], tH[cur[1]]], pr, pi, npi,
                         od(sre), od(sim))
                    cur, oth = oth, cur
                    n = n2
                for ri in range(2):
                    f.op(f.act, [tH[cur[ri]]], [tendp], lambda: A.copy(out=endp[:, j, ri:ri + 1], in_=H[cur[ri]][:, 0:1]))
                return
            if pass2:
                cadd(H[0][:, 0:1], H[1][:, 0:1], tH[0], tH[1], mid[:, j, 0:1], mid[:, j, 1:2], [tmid], pr0, pi0, npi0,
                     H[0][:, 0:1], H[1][:, 0:1])
                cadd(S3(H[0], 0, 1), S3(H[1], 0, 1), tH[0], tH[1], s0[:, j, :, 0:1], s0[:, j, :, 1:2], [ts0], pr0, pi0, npi0,
                     S3(H[0], 0, 1), S3(H[1], 0, 1))
            cur, oth = (0, 1), (2, 3)
            for k in range(3):
                sh = 1 << k
                pr, pi, npi = pw[:, j, k, 0:1], pw[:, j, k, 1:2], pw[:, j, k, 2:3]
                sre, sim, dre, dim = H[cur[0]], H[cur[1]], H[oth[0]], H[oth[1]]
                tsrc = [tH[cur[0]], tH[cur[1]]]
                cadd(S3(dre, sh, 8), S3(dim, sh, 8), tH[oth[0]], tH[oth[1]], S3(sre, 0, 8 - sh), S3(sim, 0, 8 - sh), tsrc, pr, pi, npi,
                     S3(sre, sh, 8), S3(sim, sh, 8))
                f.op(f.act, [tH[cur[0]]], [tH[oth[0]]], lambda: A.copy(out=S3(dre, 0, sh), in_=S3(sre, 0, sh)))
                f.op(f.act, [tH[cur[1]]], [tH[oth[1]]], lambda: A.copy(out=S3(dim, 0, sh), in_=S3(sim, 0, sh)))
                cur, oth = oth, cur
            xr, xi = H[0], H[1]

            def strided(b, start, stride, n):
                return self.cust(b[:, start:start + 1], [(stride, n)])

            def upd(hi_start, lo_start, stride, n, k):
                pr, pi, npi = pw[:, j, k, 0:1], pw[:, j, k, 1:2], pw[:, j, k, 2:3]
                rh, ih = strided(xr, hi_start, stride, n), strided(xi, hi_start, stride, n)
                rl, il = strided(xr, lo_start, stride, n), strided(xi, lo_start, stride, n)
                cadd(rh, ih, tH[0], tH[1], rl, il, [tH[0], tH[1]], pr, pi, npi, rh, ih)
            for k in range(10):
                st_ = 1 << (k + 1)
                upd(st_ - 1, (1 << k) - 1, st_, PT // st_, k)
            for k in range(8, -1, -1):
                st_ = 1 << (k + 1)
                n = PT // st_ - 1
                upd(st_ + (1 << k) - 1, st_ - 1, st_, n, k)
            for ri in range(2):
                f.op(f.act, [tH[ri]], [tendp], lambda: A.copy(out=endp[:, j, ri:ri + 1], in_=H[ri][:, PT - 1:PT]))
            if not pass2:
                return
            for ri in range(2):
                f.op(f.act, [tH[2 + ri]], [tends], lambda: A.copy(out=ends[:, j, :, ri:ri + 1], in_=S3(H[2 + ri], 7, 8)))
                f.op(f.act, [tH[ri]], [thb[ri]], lambda: A.copy(out=hb[ri][:, 0:PT], in_=H[ri][:, 0:PT]))
                f.op(f.dve, [tH[2 + ri]], [thb[ri]], lambda: V.tensor_copy(out=hb[ri][:, PT:NT], in_=H[2 + ri][:, PT:NT]))
            for ri in range(2):
                for ci, (c0, cn) in enumerate(CH):
                    self.mm(ybk[ci][0][:, 0:cn], Cd[ri][:, j, :], hb[ri][:, c0:c0 + cn],
                            (j % 4 == 0 and ri == 0), (j % 4 == 3 and ri == 1), [tCd, thb[ri]], [ybk[ci][1]])

        def load_xsb(c):
            pass
        for j in range(16):
            if j % 4 == 0:
                load_xsb(j // 4)
            scan(j, False)
        halo, thalo, hown, thown = self._halo
        p_d = self.dint("x2_pay_%d" % l, [128, 100], F32)
        g_d = self.dint("x2_gat_%d" % l, [256, 100], F32)
        tpd, tgd = f.tok("x2pay"), f.tok("x2gat")
        f.dma(f.sp, [tendp], [tpd], lambda: nc.sync.dma_start(out=p_d[:, 0:32], in_=endp[:].rearrange("p a b -> p (a b)")), main=tpd)
        f.dma(f.sp, [thown], [tpd], lambda: nc.sync.dma_start(out=p_d[:, 32:100], in_=hown[:]), main=tpd)
        f.dma(f.pool, [tpd], [tgd], lambda: nc.gpsimd.collective_compute(
            "AllGather", ALU.bypass, replica_groups=self.cfg["groups"], ins=[p_d], outs=[g_d]), main=tgd, inc=1)
        f.dma(f.sp, [tgd], [tmid], lambda: nc.sync.dma_start(out=mid[:].rearrange("p a b -> p (a b)"), in_=g_d[0:128, 0:32]), main=tmid)
        f.dma(f.sp, [tgd], [thalo], lambda: nc.sync.dma_start(out=halo[:], in_=g_d[0:128, 32:100]), main=thalo)
        f.op(f.dve, [tmid, self.tcb], [tmid], lambda: V.tensor_scalar(out=mid[:], in0=mid[:], scalar1=self.cbias[:, 257:258], scalar2=None, op0=ALU.mult))
        f.op(f.dve, [thalo, self.tcb], [thalo], lambda: V.tensor_scalar(out=halo[:], in0=halo[:], scalar1=self.cbias[:, 257:258], scalar2=None, op0=ALU.mult))
        for j in range(16):
            c = j // 4
            if j % 4 == 0:
                load_xsb(c)
            scan(j, True)
            if j % 4 == 3:
                for ci, (c0, cn) in enumerate(CH):
                    f.op(f.dve, [ybk[ci][1], txs, self.tcb2], [tyv], lambda: V.scalar_tensor_tensor(
                        out=yv[:, c0:c0 + cn], in0=xsf[:, c, c0:c0 + cn], scalar=self.ssd[:, l * 4 + c:l * 4 + c + 1],
                        in1=ybk[ci][0][:, 0:cn], op0=ALU.mult, op1=ALU.add))
                f.op(f.dve, [tyv], [ty2], lambda: V.tensor_tensor(out=y2[:], in0=yv[:], in1=yv[:], op=ALU.mult))
                f.op(f.dve, [ty2], [ty2], lambda: V.tensor_scalar(out=y2[:], in0=y2[:], scalar1=0.044715, scalar2=1.0, op0=ALU.mult, op1=ALU.add))
                f.op(f.dve, [ty2, tyv], [ty2], lambda: V.tensor_tensor(out=y2[:], in0=y2[:], in1=yv[:], op=ALU.mult))
                f.op(f.act, [ty2], [ty2], lambda: A.activation(out=y2[:], in_=y2[:], func=AF.Sigmoid, scale=1.5957691216057308))
                f.op(f.dve, [ty2, tyv], [tz], lambda: V.tensor_tensor(out=z[:, c, :], in0=y2[:], in1=yv[:], op=ALU.mult))
        osp, tosp = self.dout("o_ssm_p", [L, 128, 32])
        oss, toss = self.dout("o_ssm_s", [L, 128, 512])
        f.dma(f.sp, [tendp], [tosp], lambda: nc.sync.dma_start(out=osp[l], in_=endp[:].rearrange("p a b -> p (a b)")), main=tendp)
        f.dma(f.sp, [tends], [toss], lambda: nc.sync.dma_start(out=oss[l], in_=ends[:].rearrange("p a b c -> p (a b c)")), main=tends)
        for e in range(4):
            wgl, twgl = self.wget("w_glu", (l, e), 512)
            for (c0, cn) in CH:
                bk, tb = self.bank(3, 8)
                for kt in range(4):
                    self.mm(bk[:, 0:cn], wgl[:, kt * 128:(kt + 1) * 128], z[:, kt, c0:c0 + cn], kt == 0, kt == 3, [twgl, tz], [tb])
                f.op(f.act, [tb, self.tcb2], [tyv], lambda: A.activation(out=yv[:, c0:c0 + cn], in_=bk[:, 0:cn], func=AF.Sigmoid,
                                                                         bias=self.glub[:, l * 4 + e:l * 4 + e + 1], scale=1.0))
                f.op(f.dve, [tyv, tz], [tbr[8 + e]], lambda: V.tensor_tensor(out=br[:, 8 + e, c0:c0 + cn], in0=yv[:, c0:c0 + cn], in1=z[:, e, c0:c0 + cn], op=ALU.mult))

    def gate_phase(self, l, br, tbr, ycT, tyc):
        nc, f = self.nc, self.f
        V, A = nc.vector, nc.scalar
        with f.scope() as st:
            uT = f.sbuf("uT", [128, 16, NT], BF16, st)
            tu = [f.tok("u%d" % i) for i in range(16)]
            self.apply_norm(uT, tu, (l * 4 + 1) * 16)
            mg = f.sbuf("mg", [128, 8, NT], BF16, st)
            tmg = [f.tok("mg%d" % i) for i in range(8)]
            m32 = f.sbuf("m32", [128, NT], F32, st)
            sg = f.sbuf("sg", [128, NT], F32, st)
            tm32, tsg = f.tok("m32"), f.tok("sg")

            def branch_rhs(i, kt, c0, cn):
                if i == 0:
                    return br[:, kt, c0:c0 + cn], tbr[kt]
                if i == 1:
                    return br[:, 4 + kt, c0:c0 + cn], tbr[4 + kt]
                if i == 2:
                    return ycT[:, kt, c0:c0 + cn], tyc
                return br[:, 8 + kt, c0:c0 + cn], tbr[8 + kt]
            for half in range(2):
                for dd in range(8):
                    d = half * 8 + dd
                    for i in range(4):
                        wg, twg = self.wget("w_gate", (l, i, d), 2048)
                        gb = [self.bank() for _ in CH]
                        for kt in range(16):
                            for ci, (c0, cn) in enumerate(CH):
                                self.mm(gb[ci][0][:, 0:cn], wg[:, kt * 128:(kt + 1) * 128], uT[:, kt, c0:c0 + cn],
                                        kt == 0, kt == 15, [twg, tu[kt]], [gb[ci][1]])
                        wb, twb = self.wget("w_branch", (l, i, d), 512)
                        ob = [self.bank() for _ in CH]
                        for kt in range(4):
                            for ci, (c0, cn) in enumerate(CH):
                                rhs, trhs = branch_rhs(i, kt, c0, cn)
                                self.mm(ob[ci][0][:, 0:cn], wb[:, kt * 128:(kt + 1) * 128], rhs, kt == 0, kt == 3, [twb, trhs], [ob[ci][1]])
                        for ci, (c0, cn) in enumerate(CH):
                            f.op(f.act, [gb[ci][1], self.tcb2], [tsg], lambda: A.activation(
                                out=sg[:, c0:c0 + cn], in_=gb[ci][0][:, 0:cn], func=AF.Sigmoid,
                                bias=self.bgate[:, (l * 4 + i) * 16 + d:(l * 4 + i) * 16 + d + 1], scale=1.0))
                            if i == 0:
                                f.op(f.dve, [tsg, ob[ci][1]], [tm32], lambda: V.tensor_tensor(out=m32[:, c0:c0 + cn], in0=ob[ci][0][:, 0:cn], in1=sg[:, c0:c0 + cn], op=ALU.mult))
                            else:
                                f.op(f.dve, [tsg, ob[ci][1]], [tsg], lambda: V.tensor_tensor(out=sg[:, c0:c0 + cn], in0=ob[ci][0][:, 0:cn], in1=sg[:, c0:c0 + cn], op=ALU.mult))
                                if i < 3:
                                    f.op(f.dve, [tsg, tm32], [tm32], lambda: V.tensor_tensor(out=m32[:, c0:c0 + cn], in0=m32[:, c0:c0 + cn], in1=sg[:, c0:c0 + cn], op=ALU.add))
                                else:
                                    f.op(f.dve, [tsg, tm32], [tmg[dd]], lambda: V.tensor_tensor(out=mg[:, dd, c0:c0 + cn], in0=m32[:, c0:c0 + cn], in1=sg[:, c0:c0 + cn], op=ALU.add))
                if l == 0 and half == 0:
                    self.dbg_dump("mg0", mg[:], tmg[7], [128, 8, NT], BF16)
                for e in range(16):
                    wo, two = self.wget("w_o", (l, half, e), 1024)
                    for (c0, cn) in CH:
                        bk, tb = self.bank()
                        for dd in range(8):
                            self.mm(bk[:, 0:cn], wo[:, dd * 128:(dd + 1) * 128], mg[:, dd, c0:c0 + cn], dd == 0, dd == 7, [two, tmg[dd]], [tb])
                        f.op(f.dve, [tb, self.th[e]], [self.th[e]], lambda: V.tensor_tensor(
                            out=self.hT[:, e, c0:c0 + cn], in0=bk[:, 0:cn], in1=self.hT[:, e, c0:c0 + cn], op=ALU.add))

    def cross_attention(self, l):
        nc, f = self.nc, self.f
        V, A = nc.vector, nc.scalar
        XS = 128 ** -0.5
        with f.scope() as st_x:
            qx = f.sbuf("qx", [128, 4, NT], BF16, st_x)
            ox = f.sbuf("ox", [128, 4, NT], BF16, st_x)
            tqx, tox = f.tok("qx"), f.tok("ox")
            kxb = f.sbuf("kxb", [128, 4, 256], BF16, st_x)
            vxb = f.sbuf("vxb", [128, 2, 512], BF16, st_x)
            tkx, tvx = f.tok("kxb"), f.tok("vxb")
            with f.scope() as st:
                uT = f.sbuf("uT", [128, 16, NT], BF16, st)
                tu = [f.tok("u%d" % i) for i in range(16)]
                self.rmsnorm(uT, tu, (l * 4 + 2) * 16, st)
                pf = f.sbuf("pfx", [128, NT], F32, st)
                tpf = f.tok("pfx")
                for hh in range(4):
                    def ev(ci, c0, cn, bk, tb):
                        f.op(f.act, [tb], [tpf], lambda: A.copy(out=pf[:, c0:c0 + cn], in_=bk[:, 0:cn]))
                    self.proj("w_xq", (l, hh), uT, tu, ev)
                    self.headnorm(pf[:], tpf, 128, self.xqg[:, l:l + 1], qx[:, hh, :], tqx, st, self.ones_bf)
                memT = self.din("memT", [16, 128, 256])
                mm_ = f.sbuf("memf", [128, 16, 256], F32, st)
                mu = f.sbuf("memu", [128, 16, 256], BF16, st)
                tmm, tmu = f.tok("memf"), f.tok("memu")
                for kt in range(16):
                    f.dma(f.sp, [], [tmm], lambda: nc.sync.dma_start(out=mm_[:, kt, :], in_=memT[kt]))
                msq = f.sbuf("msq", [128, 256], BF16, st)
                mrs = f.sbuf("mrs", [128, 256], F32, st)
                tmsq, tmrs = f.tok("msq"), f.tok("mrs")
                bk, tb = self.bank()
                for kt in range(16):
                    f.op(f.act, [tmm], [tmsq], lambda: A.activation(out=msq[:], in_=mm_[:, kt, :], func=AF.Square))
                    self.mm(bk[:, 0:256], self.ones_bf, msq[:], kt == 0, kt == 15, [tmsq, self.tc], [tb])
                f.op(f.act, [tb, self.teps], [tmrs], lambda: A.activation(out=mrs[:], in_=bk[:, 0:256], func=AF.Sqrt, bias=self.eps_t[:, 0:1], scale=1.0 / D))
                f.op(f.dve, [tmrs], [tmrs], lambda: V.reciprocal(out=mrs[:], in_=mrs[:]))
                for kt in range(16):
                    f.op(f.dve, [tmm, tmrs, self.tcb2], [tmu], lambda: V.scalar_tensor_tensor(
                        out=mu[:, kt, :], in0=mm_[:, kt, :], scalar=self.memg[:, l * 16 + kt:l * 16 + kt + 1], in1=mrs[:], op0=ALU.mult, op1=ALU.mult))
                kxf = f.sbuf("kxf", [128, 4, 256], F32, st)
                vxf = f.sbuf("vxf", [128, 2, 512], F32, st)
                tkf, tvf = f.tok("kxf"), f.tok("vxf")
                ksq = f.sbuf("kxsq", [128, 256], BF16, st)
                tksq = f.tok("kxsq")
                for hh in range(4):
                    wk, twk = self.wget("w_xk", (l, hh), 2048)
                    bk, tb = self.bank()
                    for kt in range(16):
                        self.mm(bk[:, 0:256], wk[:, kt * 128:(kt + 1) * 128], mu[:, kt, :], kt == 0, kt == 15, [twk, tmu], [tb])
                    f.op(f.act, [tb], [tkf], lambda: A.copy(out=kxf[:, hh, :], in_=bk[:, 0:256]))
                    f.op(f.act, [tkf], [tksq], lambda: A.activation(out=ksq[:], in_=kxf[:, hh, :], func=AF.Square))
                    bk2, tb2 = self.bank()
                    self.mm(bk2[:, 0:256], self.ones_bf, ksq[:], True, True, [tksq, self.tc], [tb2])
                    f.op(f.act, [tb2, self.teps], [tmrs], lambda: A.activation(out=mrs[:], in_=bk2[:, 0:256], func=AF.Sqrt, bias=self.eps_t[:, 0:1], scale=1.0 / 128))
                    f.op(f.dve, [tmrs], [tmrs], lambda: V.reciprocal(out=mrs[:], in_=mrs[:]))
                    f.op(f.dve, [tmrs, tkf, self.tcb2], [tkf], lambda: V.scalar_tensor_tensor(
                        out=kxf[:, hh, :], in0=kxf[:, hh, :], scalar=self.xkg[:, l:l + 1], in1=mrs[:], op0=ALU.mult, op1=ALU.mult))
                f.op(f.dve, [tkf], [tkx], lambda: V.tensor_copy(out=kxb[:], in_=kxf[:]))
                for mt in range(2):
                    bk, tb = self.bank()
                    for q4 in range(4):
                        wv, twv = self.wget("w_xv_rhs", (l, q4), 2048)
                        for k4 in range(4):
                            kt = q4 * 4 + k4
                            self.mm(bk[:, 0:512], mu[:, kt, mt * 128:(mt + 1) * 128], wv[:, k4 * 512:(k4 + 1) * 512], kt == 0, kt == 15, [twv, tmu], [tb])
                    f.op(f.act, [tb], [tvf], lambda: A.copy(out=vxf[:, mt, :], in_=bk[:, 0:512]))
                f.op(f.dve, [tvf], [tvx], lambda: V.tensor_copy(out=vxb[:], in_=vxf[:]))
                omk, tomk = self.dout("o_mk", [L, 128, 4, 256])
                omv, tomv = self.dout("o_mv", [L, 128, 2, 512])
                f.dma(f.sp, [tkf], [tomk], lambda: nc.sync.dma_start(out=omk[l], in_=kxf[:]), main=tkf)
                f.dma(f.sp, [tvf], [tomv], lambda: nc.sync.dma_start(out=omv[l], in_=vxf[:]), main=tvf)
            with f.scope() as st:
                E = [f.sbuf("xE", [128, 2, 512], BF16, st) for _ in range(2)]
                tE = [f.tok("xE0"), f.tok("xE1")]
                rd = f.sbuf("xrd", [128, 512], F32, st)
                trd = f.tok("xrd")
                ei = 0
                for hh in range(4):
                    for qc in range(2):
                        q0 = qc * 512
                        e, te = E[ei % 2], tE[ei % 2]
                        ei += 1
                        for mt in range(2):
                            bk, tb = self.bank()
                            self.mm(bk[:, 0:512], kxb[:, hh, mt * 128:(mt + 1) * 128], qx[:, hh, q0:q0 + 512], True, True, [tkx, tqx], [tb])
                            f.op(f.act, [tb], [te], lambda: A.activation(out=e[:, mt, :], in_=bk[:, 0:512], func=AF.Exp, scale=XS))
                        bo, tbo = self.bank()
                        bd_, tbd = self.bank()
                        for mt in range(2):
                            self.mm(bo[:, 0:512], vxb[:, mt, hh * 128:(hh + 1) * 128], e[:, mt, :], mt == 0, mt == 1, [tvx, te], [tbo])
                            self.mm(bd_[:, 0:512], self.ones_bf, e[:, mt, :], mt == 0, mt == 1, [self.tc, te], [tbd])
                        f.op(f.dve, [tbd], [trd], lambda: V.reciprocal(out=rd[:], in_=bd_[:, 0:512]))
                        f.op(f.dve, [tbo, trd], [tox], lambda: V.tensor_tensor(out=ox[:, hh, q0:q0 + 512], in0=bo[:, 0:512], in1=rd[:], op=ALU.mult))
                ck_d = self.din("cmem_k", [L, NSEQ, 2, 128, 512])
                cv_d = self.din("cmem_v", [L, NSEQ, 2, 128, 512])
                MKs = [f.sbuf("xMK", [128, 2, 512], F32, st) for _ in range(2)]
                MVs = [f.sbuf("xMV", [128, 2, 512], F32, st) for _ in range(2)]
                tMKs = [f.tok("xMK0"), f.tok("xMK1")]
                tMVs = [f.tok("xMV0"), f.tok("xMV1")]
                kts_ = [f.sbuf("xkT", [128, 4, 256], BF16, st) for _ in range(2)]
                vbs = [f.sbuf("xvb", [128, 2, 512], BF16, st) for _ in range(2)]
                tkts = [f.tok("xkT0"), f.tok("xkT1")]
                tvbs = [f.tok("xvb0"), f.tok("xvb1")]
                es = [f.sbuf("xes", [128, 64], BF16, st) for _ in range(2)]
                tes = [f.tok("xes0"), f.tok("xes1")]
                rds = f.sbuf("xrds", [128, 32], F32, st)
                trds = f.tok("xrds")
                for s_ in range(NSEQ):
                    b2 = s_ % 2
                    mk, mv, tmk, tmv = MKs[b2], MVs[b2], tMKs[b2], tMVs[b2]
                    f.dma(f.sp, [], [tmk], lambda: nc.sync.dma_start(out=mk[:], in_=ck_d[l, s_].rearrange("t p d -> p t d")))
                    f.dma(f.sp, [], [tmv], lambda: nc.sync.dma_start(out=mv[:], in_=cv_d[l, s_].rearrange("t p d -> p t d")))
                    kt_, vb_, tkt_, tvb_ = kts_[b2], vbs[b2], tkts[b2], tvbs[b2]
                    for mt in range(2):
                        bk, tb = self.bank()
                        for hh in range(4):
                            f.op(f.pe, [tmk, self.tc], [tb], lambda: nc.tensor.transpose(out=bk[:, hh * 128:(hh + 1) * 128], in_=mk[:, mt, hh * 128:(hh + 1) * 128], identity=self.ident_f))
                        f.op(f.act, [tb], [tkt_], lambda: A.copy(out=kt_[:, :, mt * 128:(mt + 1) * 128], in_=bk[:, 0:512].rearrange("p (h m) -> p h m", h=4)))
                    f.op(f.dve, [tmv], [tvb_], lambda: V.tensor_copy(out=vb_[:], in_=mv[:]))
                    c0 = PT + s_ * 8
                    e, te = es[b2], tes[b2]
                    bk, tb = self.bank()
                    for hh in range(4):
                        for mt in range(2):
                            self.mm(bk[:, (hh * 2 + mt) * 8:(hh * 2 + mt + 1) * 8], kt_[:, hh, mt * 128:(mt + 1) * 128], qx[:, hh, c0:c0 + 8], True, True, [tkt_, tqx], [tb])
                    f.op(f.act, [tb], [te], lambda: A.activation(out=e[:], in_=bk[:, 0:64], func=AF.Exp, scale=XS))
                    bo, tbo = self.bank()
                    for hh in range(4):
                        for mt in range(2):
                            self.mm(bo[:, hh * 8:(hh + 1) * 8], vb_[:, mt, hh * 128:(hh + 1) * 128], e[:, (hh * 2 + mt) * 8:(hh * 2 + mt + 1) * 8], mt == 0, mt == 1, [tvb_, te], [tbo], skip_group_check=True)
                        for mt in range(2):
                            self.mm(bo[:, 32 + hh * 8:32 + (hh + 1) * 8], self.ones_bf, e[:, (hh * 2 + mt) * 8:(hh * 2 + mt + 1) * 8], mt == 0, mt == 1, [self.tc, te], [tbo], skip_group_check=True)
                    f.op(f.dve, [tbo], [trds], lambda: V.reciprocal(out=rds[:], in_=bo[:, 32:64]))
                    f.op(f.dve, [tbo, trds], [tox], lambda: V.tensor_tensor(
                        out=self.cust(ox[:, 0, c0:c0 + 8], [(NT, 4), (1, 8)]), in0=bo[:, 0:32].rearrange("p (h q) -> p h q", q=8),
                        in1=rds[:].rearrange("p (h q) -> p h q", q=8), op=ALU.mult))
            for e_ in range(16):
                wxo, twxo = self.wget("w_xo", (l, e_), 512)
                for (c0, cn) in CH:
                    bk, tb = self.bank()
                    for hh in range(4):
                        self.mm(bk[:, 0:cn], wxo[:, hh * 128:(hh + 1) * 128], ox[:, hh, c0:c0 + cn], hh == 0, hh == 3, [twxo, tox], [tb])
                    f.op(f.dve, [tb, self.th[e_]], [self.th[e_]], lambda: V.tensor_tensor(
                        out=self.hT[:, e_, c0:c0 + cn], in0=bk[:, 0:cn], in1=self.hT[:, e_, c0:c0 + cn], op=ALU.add))

    def dint(self, name, shape, dtype=F32):
        if name not in self.dram:
            self.dram[name] = self.nc.dram_tensor(name, list(shape), dtype, kind="Internal").ap()
        return self.dram[name]

    def run(self):
        if "custom_run" in self.cfg:
            return self.cfg["custom_run"](self)
        self.setup()
        for l in range(L):
            if not self.layer(l):
                break
            if self.cfg.get("stop") == "full1":
                break
        if self.cfg.get("debug"):
            for kt in range(16):
                pass
        oy, toy = self.dout("o_y", [16, 128, NT])
        for kt in range(16):
            self.f.dma(self.f.sp, [self.th[kt]], [toy],
                       lambda: self.nc.sync.dma_start(out=oy[kt], in_=self.hT[:, kt, :]), main=self.th[kt])


def build_program(cfg):
    nc0 = bass.Bass("TRN2", target_bir_lowering=False)
    f0 = FW(nc0, dry=True)
    k0 = Kern(nc0, f0, None, cfg)
    k0.run()
    plan = k0.wkeys
    f0.close()
    nc = bass.Bass("TRN2", target_bir_lowering=False)
    f = FW(nc, dry=False)
    k = Kern(nc, f, plan, cfg)
    k.run()
    f.finish(k.outs)
    f.close()
    return nc, k


def tile_k(W):
    K, N = W.shape
    KT, NC = K // 128, N // 128
    return np.ascontiguousarray(W.reshape(KT, 128, NC, 128).transpose(2, 1, 0, 3).reshape(NC, 128, KT * 128))


def rhs_k(W):
    K, N = W.shape
    KT = K // 128
    return np.ascontiguousarray(W.reshape(KT, 128, N).transpose(1, 0, 2).reshape(128, KT * N))


def pad_cols(W, dup=False):
    K, n = W.shape
    out = np.zeros((K, 128), W.dtype)
    out[:, :n] = W
    if dup:
        out[:, 64:64 + n] = W
    return out


def prep_shared(inp):
    sh = {}
    f_in = inp["ffn_in"]
    sh["w_ffn_in"] = np.stack([np.stack([tile_k(f_in[l, w]) for w in range(2)]) for l in range(L)])
    f_out = inp["ffn_out"]
    sh["w_ffn_out"] = np.stack([np.stack([np.stack([tile_k(f_out[l, w, hh * 2048:(hh + 1) * 2048]) for hh in range(2)])
                                          for w in range(2)]) for l in range(L)])
    w_in = inp["w_in"]
    tiles = []
    for l in range(L):
        W = w_in[l]
        cols = []
        for base in (0, 512, 1024, 1536, 2048):
            cols += [W[:, base + i * 128: base + (i + 1) * 128] for i in range(4)]
        cols.append(pad_cols(W[:, 2560:2624], dup=True))
        cols += [W[:, 2688 + i * 128: 2688 + (i + 1) * 128] for i in range(4)]
        cols.append(pad_cols(W[:, 3200:3264], dup=True))
        cols += [W[:, 3272 + i * 128: 3272 + (i + 1) * 128] for i in range(4)]
        Wt = np.concatenate(cols, axis=1)
        tiles.append(tile_k(Wt))
    sh["w_in_t"] = np.stack(tiles)
    sh["w_v_rhs"] = np.stack([rhs_k(w_in[l][:, 2624:2688]) for l in range(L)])
    sh["w_iw_rhs"] = np.stack([rhs_k(w_in[l][:, 3264:3272]) for l in range(L)])
    for l in range(L):
        sh["cache_ik_%d" % l] = inp["cache_idx_k"][l].reshape(20480, 1024)
        sh["cache_k_%d" % l] = inp["cache_attn_k"][l].reshape(20480, 1024)
        sh["cache_v_%d" % l] = inp["cache_attn_v"][l].reshape(20480, 1024)
    sh["w_pool"] = np.ascontiguousarray(inp["pool_mix"])
    sh["w_glu"] = np.stack([tile_k(inp["ssm_glu_w"][l]) for l in range(L)])
    sh["w_gate"] = np.stack([np.stack([tile_k(inp["w_gate"][l][:, i * 2048:(i + 1) * 2048]) for i in range(4)]) for l in range(L)])
    sh["w_branch"] = np.stack([np.stack([tile_k(inp["w_branch"][l, i]) for i in range(4)]) for l in range(L)])
    sh["w_o"] = np.stack([np.stack([tile_k(inp["w_o"][l][hf * 1024:(hf + 1) * 1024]) for hf in range(2)]) for l in range(L)])
    sh["w_xq"] = np.stack([tile_k(inp["w_xq"][l]) for l in range(L)])
    sh["w_xk"] = np.stack([tile_k(inp["w_xk"][l]) for l in range(L)])
    sh["w_xv_rhs"] = np.stack([np.ascontiguousarray(inp["w_xv"][l].reshape(4, 4, 128, 512).transpose(0, 2, 1, 3).reshape(4, 128, 2048)) for l in range(L)])
    sh["w_xo"] = np.stack([tile_k(inp["w_xo"][l]) for l in range(L)])
    are, aim = inp["ssm_a_re"], inp["ssm_a_im"]
    lst = np.broadcast_to(inp["ssm_log_step"][:, :, None], are.shape)
    def sm(a):
        return a.reshape(L, 16, 2, 64).transpose(0, 2, 3, 1).reshape(L, 128, 16)
    def cmaj(a):
        x = a.reshape(L, 4, 8, 64).transpose(0, 2, 1, 3)
        x = np.broadcast_to(x[:, :, None], (L, 8, 16, 4, 64))
        return x.reshape(L, 128, 256)
    sh["ssm_sm"] = np.ascontiguousarray(np.stack([sm(are), sm(aim), sm(lst)], axis=-1))
    sh["ssm_cm"] = np.ascontiguousarray(np.stack([cmaj(are), cmaj(aim), cmaj(lst)], axis=-1))
    def bcm(b):
        return b.reshape(L, 4, 8, 64, 16).transpose(0, 2, 4, 1, 3).reshape(L, 128, 256)
    sh["ssm_bcm"] = np.ascontiguousarray(np.stack([bcm(inp["ssm_b_re"]), bcm(inp["ssm_b_im"])], axis=-1))
    def csm(c_):
        return c_.reshape(L, 16, 2, 16, 64).transpose(0, 2, 4, 1, 3).reshape(L, 128, 16, 16)
    sh["ssm_csm"] = np.ascontiguousarray(np.stack([csm(inp["ssm_c_re"]), csm(inp["ssm_c_im"])], axis=-1))
    c2 = np.zeros((128, 404), np.float32)
    def fm(v, nt):
        return v.reshape(nt, 128).T
    for l in range(L):
        c2[:, 64 + 4 * l:68 + 4 * l] = fm(inp["pool_scale"][l], 4)
        cw = inp["conv_w"][l]
        for c_ in range(4):
            for j in range(3):
                c2[:, 72 + (l * 4 + c_) * 3 + j] = cw[j, c_ * 128:(c_ + 1) * 128]
        c2[:, 96 + 4 * l:100 + 4 * l] = fm(inp["ssm_d"][l], 4)
        c2[:, 104 + 4 * l:108 + 4 * l] = fm(inp["ssm_glu_b"][l], 4)
        c2[:, 112 + l] = inp["xq_norm_g"][l]
        c2[:, 114 + l] = inp["xk_norm_g"][l]
        c2[:, 116 + 16 * l:132 + 16 * l] = fm(inp["mem_norm_g"][l], 16)
        for i in range(4):
            c2[:, 148 + (l * 4 + i) * 16:148 + (l * 4 + i + 1) * 16] = fm(inp["b_gate"][l][i * 2048:(i + 1) * 2048], 16)
    r_ = np.arange(128)
    c2[:, 276:404] = ((r_[:, None] % 32) // 16 == (r_[None, :] // 64)).astype(np.float32)
    sh["cst2_base"] = c2
    c = np.zeros((128, 512), np.float32)
    c[:, 0:128] = 1.0
    c[0:64, 128:192] = 1.0
    c[64:128, 192:256] = 1.0
    c[:, 256:384] = np.eye(128, dtype=np.float32)
    c[0:64, 384 + 64:384 + 128] = np.eye(64, dtype=np.float32)
    c[64:128, 384:384 + 64] = np.eye(64, dtype=np.float32)
    sh["consts_bf"] = c
    ng = inp["norm_g"]
    sh["gains"] = np.ascontiguousarray(ng.reshape(L, 4, 16, 128).transpose(3, 0, 1, 2).reshape(128, L * 4 * 16))
    hp = np.zeros((128, 8 * L), np.float32)
    for l in range(L):
        hp[:, 8 * l + 0] = np.tile(inp["k_norm_g"][l], 2)
        hp[:, 8 * l + 1] = np.tile(inp["q_norm_g"][l], 2)
    sh["headp"] = hp
    return sh


def prep_core(inp, c, sh=None):
    b, hf = c // 2, c % 2
    xp = inp["x_prompt"][b, hf * PT:(hf + 1) * PT]
    xs = inp["x_sample"][c * NSEQ:(c + 1) * NSEQ].reshape(NSEQ * 8, D)
    x = np.concatenate([xp, xs], axis=0)
    d = {"xT": np.ascontiguousarray(x.T.reshape(16, 128, NT))}
    cb = np.zeros((128, 259), np.float32)
    cb[:, 0] = 0.0 if hf == 1 else NEG
    qi = np.arange(128)
    cb[:, 1:129] = np.where(qi[None, :] <= qi[:, None], 0.0, NEG)
    sq, qq = qi // 8, qi % 8
    cb[:, 129:257] = np.where((sq[None, :] == sq[:, None]) & (qq[None, :] <= qq[:, None]), 0.0, NEG)
    cb[:, 257] = 1.0 if hf == 1 else 0.0
    cb[:, 258] = qi % 8
    d["cbias"] = cb
    c2 = sh["cst2_base"].copy()
    for gi in range(4):
        w = 2 ** (gi + 1)
        pos = np.arange(16)
        cnt = np.minimum(pos + 1, w) if hf == 0 else np.full(16, w)
        c2[:, gi * 16:(gi + 1) * 16] = (1.0 / cnt)[None, :]
    d["cst2"] = c2
    d["memT"] = np.ascontiguousarray(inp["mem_prompt"][b].T.reshape(16, 128, 256))
    sl = slice(c * NSEQ, (c + 1) * NSEQ)
    d["cmem_k"] = np.ascontiguousarray(inp["cache_mem_k"][:, sl].reshape(L, NSEQ, 2, 128, 512))
    d["cmem_v"] = np.ascontiguousarray(inp["cache_mem_v"][:, sl].reshape(L, NSEQ, 2, 128, 512))
    d["st_pool"] = np.ascontiguousarray(inp["state_pool"][:, sl].reshape(L, NSEQ, 15, 4, 128).transpose(0, 3, 4, 1, 2))
    d["st_conv"] = np.ascontiguousarray(inp["state_conv"][:, sl].reshape(L, NSEQ, 2, 4, 128).transpose(0, 3, 4, 1, 2))
    def ssm_st(a):
        return a.reshape(L, NSEQ, 16, 2, 64).transpose(0, 3, 4, 2, 1).reshape(L, 128, 16, NSEQ)
    d["st_ssm"] = np.ascontiguousarray(np.stack([ssm_st(inp["state_ssm_re"][:, sl]), ssm_st(inp["state_ssm_im"][:, sl])], axis=-1))
    pt = inp["page_table"][c * NSEQ:(c + 1) * NSEQ]
    d["pt_rep"] = np.ascontiguousarray(pt.T[qi // 8, :]).astype(np.int32)
    return d


_PROGRAM_CACHE = {}


def kernel(**inputs):
    inp = {k: np.asarray(v) for k, v in inputs.items()}
    cfg = {"groups": [[0, 1], [2, 3], [4, 5], [6, 7]]}
    if "prog" not in _PROGRAM_CACHE:
        _PROGRAM_CACHE["prog"] = build_program(cfg)
    nc, k = _PROGRAM_CACHE["prog"]
    sh = prep_shared(inp)
    need = [n for n in k.dram if not n.endswith("__tok") and not n.startswith("o_") and not n.startswith("dbg_")
            and not n.startswith("x1_") and not n.startswith("x2_")]
    in_maps = []
    for c in range(8):
        d = prep_core(inp, c, sh)
        in_maps.append({n: (d[n] if n in d else sh[n]) for n in need})
    res = run_bass_kernel_spmd(nc, in_maps, core_ids=list(range(8)))
    R = res.results
    f32 = np.float32
    B, S = 4, 2048
    y_p = np.zeros((B, S, D), f32); y_s = np.zeros((128, 8, D), f32)
    pk = np.zeros((L, B, S, 64), f32); pv = np.zeros((L, B, S, 64), f32); pik = np.zeros((L, B, S, 64), f32)
    sk = np.zeros((L, 128, 8, 64), f32); sv = np.zeros((L, 128, 8, 64), f32); sik = np.zeros((L, 128, 8, 64), f32)
    pmk = np.zeros((L, B, 256, 4, 128), f32); pmv = np.zeros((L, B, 256, 4, 128), f32)
    ppool = np.zeros((L, B, 15, 512), f32); pconv = np.zeros((L, B, 2, 512), f32)
    pre = np.zeros((L, B, 32, 64), f32); pim = np.zeros((L, B, 32, 64), f32)
    spool = np.zeros((L, 128, 15, 512), f32); sconv = np.zeros((L, 128, 2, 512), f32)
    sre = np.zeros((L, 128, 32, 64), f32); sim_ = np.zeros((L, 128, 32, 64), f32)
    for c in range(8):
        r = R[c]
        b, hf = c // 2, c % 2
        ps = slice(hf * PT, (hf + 1) * PT)
        ss = slice(c * NSEQ, (c + 1) * NSEQ)
        yt = np.asarray(r["o_y"], f32).reshape(D, NT).T
        y_p[b, ps] = yt[:PT]
        y_s[ss] = yt[PT:].reshape(NSEQ, 8, D)
        ok = np.asarray(r["o_k"], f32); oik = np.asarray(r["o_ik"], f32); ov = np.asarray(r["o_v"], f32).reshape(L, NT, 64)
        for l in range(L):
            kt = ok[l].T; ikt = oik[l].T
            pk[l, b, ps] = kt[:PT]; sk[l, ss] = kt[PT:].reshape(NSEQ, 8, 64)
            pik[l, b, ps] = ikt[:PT]; sik[l, ss] = ikt[PT:].reshape(NSEQ, 8, 64)
            pv[l, b, ps] = ov[l, :PT]; sv[l, ss] = ov[l, PT:].reshape(NSEQ, 8, 64)
            sp_ = np.asarray(r["o_pool_s"], f32)[l]
            spool[l, ss] = sp_.transpose(2, 3, 0, 1).reshape(NSEQ, 15, 512)
            sc_ = np.asarray(r["o_conv_s"], f32)[l]
            sconv[l, ss] = sc_.transpose(2, 3, 0, 1).reshape(NSEQ, 2, 512)
            st_ = np.asarray(r["o_ssm_s"], f32)[l].reshape(2, 64, 16, NSEQ, 2)
            st_ = st_.transpose(3, 2, 0, 1, 4).reshape(NSEQ, 32, 64, 2)
            sre[l, ss] = st_[..., 0]; sim_[l, ss] = st_[..., 1]
            if hf == 0:
                mk = np.asarray(r["o_mk"], f32)[l]
                pmk[l, b] = mk.transpose(2, 1, 0)
                mv = np.asarray(r["o_mv"], f32)[l]
                pmv[l, b] = mv.transpose(1, 0, 2).reshape(256, 4, 128)
            else:
                pp = np.asarray(r["o_pool_p"], f32)[l]
                ppool[l, b] = pp.transpose(2, 0, 1).reshape(15, 512)
                pc = np.asarray(r["o_conv_p"], f32)[l]
                pconv[l, b] = pc.transpose(2, 0, 1).reshape(2, 512)
                e_ = np.asarray(r["o_ssm_p"], f32)[l].reshape(2, 64, 16, 2).transpose(2, 0, 1, 3).reshape(32, 64, 2)
                pre[l, b] = e_[..., 0]; pim[l, b] = e_[..., 1]
    return (y_p, y_s, pk, pv, pik, pmk, pmv, ppool, pconv, pre, pim, sk, sv, sik, spool, sconv, sre, sim_)
```
